# Optimizing a Trainium2 kernel written in Bass

```python
import jax
import jax.numpy as jnp
from jax import lax
import numpy as np

D_MODEL = 2048
BATCH = 16
SEQ = 2048
DEPTH = 2

GRID_W = 64
CTX_LEN = 256
HEAD_DIM = 128
BRANCH_W = D_MODEL // 2
NA_HEADS = BRANCH_W // HEAD_DIM
GQA_HEADS = BRANCH_W // HEAD_DIM
GQA_KV_HEADS = GQA_HEADS // 4
KV_W = GQA_KV_HEADS * HEAD_DIM
CONV_K = 3
MLP_HIDDEN = 4 * D_MODEL
N_BRANCH = 3
WIN_R = 8
WIN_C = 16
Q_BLOCK = 128
ROPE_THETA = 10000.0
NORM_EPS = 1e-6
NEG_INF = -1e30
ATTN_SCALE = HEAD_DIM ** -0.5
DN_ALPHA = (2 * DEPTH) ** 0.25
DN_BETA = (8 * DEPTH) ** -0.25

QA0 = 0
QB0 = QA0 + BRANCH_W
KA0 = QB0 + BRANCH_W
VA0 = KA0 + BRANCH_W
KB0 = VA0 + BRANCH_W
VB0 = KB0 + KV_W
CB0 = VB0 + KV_W
CC0 = CB0 + BRANCH_W
CX0 = CC0 + BRANCH_W
G0 = CX0 + BRANCH_W
IN_W = G0 + N_BRANCH * D_MODEL

kernel_name = 'hybrid_na_gqa_shortconv_dit_block'


def layer_norm(x):
    xf = x.astype(jnp.float32)
    mu = jnp.mean(xf, axis=-1, keepdims=True)
    var = jnp.mean(jnp.square(xf - mu), axis=-1, keepdims=True)
    return ((xf - mu) * lax.rsqrt(var + NORM_EPS)).astype(x.dtype)


def modulate(x, shift, scale):
    return layer_norm(x) * (1.0 + scale) + shift


def rms_norm(x, gain):
    xf = x.astype(jnp.float32)
    y = xf * lax.rsqrt(jnp.mean(jnp.square(xf), axis=-1, keepdims=True) + NORM_EPS)
    return y.astype(x.dtype) * gain


def split_heads(z, n_heads):
    b, n, _ = z.shape
    return z.reshape(b, n, n_heads, HEAD_DIM).transpose(0, 2, 1, 3)


def merge_heads(o):
    b, h, n, d = o.shape
    return o.transpose(0, 2, 1, 3).reshape(b, n, h * d)


def axial_rope_angles(n_tokens):
    t = jnp.arange(n_tokens)
    row = (t // GRID_W).astype(jnp.float32)
    col = (t % GRID_W).astype(jnp.float32)
    axis_dim = HEAD_DIM // 2
    freqs = ROPE_THETA ** (-jnp.arange(0, axis_dim, 2, dtype=jnp.float32) / axis_dim)
    return row[:, None] * freqs, col[:, None] * freqs


def rotate(x, ang):
    half = x.shape[-1] // 2
    x1, x2 = x[..., :half], x[..., half:]
    cos = jnp.cos(ang).astype(x.dtype)
    sin = jnp.sin(ang).astype(x.dtype)
    return jnp.concatenate([x1 * cos - x2 * sin, x1 * sin + x2 * cos], axis=-1)


def apply_axial_rope(x, ang_row, ang_col):
    half = HEAD_DIM // 2
    return jnp.concatenate([rotate(x[..., :half], ang_row), rotate(x[..., half:], ang_col)], axis=-1)


def grouped_attention(q, k, v):
    b, hq, n, d = q.shape
    hkv = k.shape[1]
    qg = q.reshape(b, hkv, hq // hkv, n, d)
    s = jnp.einsum('bkgqd,bksd->bkgqs', qg, k).astype(jnp.float32) * ATTN_SCALE
    p = jax.nn.softmax(s, axis=-1).astype(v.dtype)
    return jnp.einsum('bkgqs,bksd->bkgqd', p, v).reshape(b, hq, n, d)


def blocked_latent_attention(q, k, v, k_ctx, v_ctx):
    b, hq, s_len, d = q.shape
    hkv = k.shape[1]
    g = hq // hkv
    nb = s_len // Q_BLOCK
    k_all = jnp.concatenate([k, k_ctx], axis=2)
    v_all = jnp.concatenate([v, v_ctx], axis=2)
    qb = q.reshape(b, hkv, g, nb, Q_BLOCK, d).transpose(3, 0, 1, 2, 4, 5)

    def one_block(q_blk):
        sc = jnp.einsum('bkgqd,bksd->bkgqs', q_blk, k_all).astype(jnp.float32) * ATTN_SCALE
        p = jax.nn.softmax(sc, axis=-1).astype(v_all.dtype)
        return jnp.einsum('bkgqs,bksd->bkgqd', p, v_all)

    o = lax.map(one_block, qb)
    return o.transpose(1, 2, 3, 0, 4, 5).reshape(b, hq, s_len, d)


def neighbourhood_attention(q, k, v, k_ctx, v_ctx, rpb):
    b, h, s_len, d = q.shape
    rows = s_len // GRID_W
    wr = min(WIN_R, rows)
    r = jnp.arange(rows)
    row_start = jnp.clip(r - wr // 2, 0, rows - wr)
    key_rows = row_start[:, None] + jnp.arange(wr)[None, :]
    col = jnp.arange(GRID_W)
    col_start = jnp.clip(col - WIN_C // 2, 0, GRID_W - WIN_C)
    in_win = (col[None, :] >= col_start[:, None]) & (col[None, :] < col_start[:, None] + WIN_C)
    dr_idx = key_rows - r[:, None] + WIN_R - 1
    dc_idx = jnp.clip(col[None, :] - col[:, None], -(WIN_C - 1), WIN_C - 1) + WIN_C - 1
    bias = rpb[:, dr_idx[:, None, :, None], dc_idx[None, :, None, :]].astype(jnp.float32)
    bias = jnp.where(in_win[None, None, :, None, :], bias, NEG_INF)
    qg = q.reshape(b, h, rows, GRID_W, d)
    kg = k.reshape(b, h, rows, GRID_W, d)[:, :, key_rows]
    vg = v.reshape(b, h, rows, GRID_W, d)[:, :, key_rows]
    s_loc = jnp.einsum('bhrqd,bhrwkd->bhrqwk', qg, kg).astype(jnp.float32) * ATTN_SCALE + bias[None]
    n_loc = wr * GRID_W
    s_loc = s_loc.reshape(b, h, rows, GRID_W, n_loc)
    s_ctx = jnp.einsum('bhrqd,bhld->bhrql', qg, k_ctx).astype(jnp.float32) * ATTN_SCALE
    p = jax.nn.softmax(jnp.concatenate([s_loc, s_ctx], axis=-1), axis=-1).astype(v.dtype)
    p_loc = p[..., :n_loc].reshape(b, h, rows, GRID_W, wr, GRID_W)
    o = (jnp.einsum('bhrqwk,bhrwkd->bhrqd', p_loc, vg)
         + jnp.einsum('bhrql,bhld->bhrqd', p[..., n_loc:], v_ctx))
    return o.reshape(b, h, s_len, d)


def short_conv(u, w):
    return lax.conv_general_dilated(
        u, w[:, None, :], window_strides=(1,), padding=[(CONV_K // 2, CONV_K // 2)],
        dimension_numbers=('NWC', 'WIO', 'NWC'), feature_group_count=u.shape[-1])


def conv_branch(z, conv_w):
    b_gate, c_gate, u = z[..., CB0:CC0], z[..., CC0:CX0], z[..., CX0:G0]
    return b_gate * short_conv(c_gate * u, conv_w)


def merge_branches(z, o_na, o_gqa, y_conv, w_branch, w_o):
    g = jax.nn.sigmoid(z[..., G0:].astype(jnp.float32)).astype(z.dtype)
    g_na, g_gqa, g_conv = jnp.split(g, N_BRANCH, axis=-1)
    m = (g_na * (merge_heads(o_na) @ w_branch[0])
         + g_gqa * (merge_heads(o_gqa) @ w_branch[1])
         + g_conv * (y_conv @ w_branch[2]))
    return m @ w_o


def squared_relu_mlp(h, w_up, w_down):
    return jnp.square(jax.nn.relu(h @ w_up)) @ w_down


def token_mixer(h, hc, w_in, w_branch, w_o, rpb, q_gain, k_gain, conv_w, ang_row, ang_col, with_ctx_out):
    z = h @ w_in
    if with_ctx_out:
        zc = hc @ w_in
        zc_kv = zc[..., KA0:CB0]
    else:
        zc_kv = hc @ w_in[:, KA0:CB0]
    ka_c = split_heads(zc_kv[..., :BRANCH_W], NA_HEADS)
    va_c = split_heads(zc_kv[..., BRANCH_W:2 * BRANCH_W], NA_HEADS)
    kb_c = rms_norm(split_heads(zc_kv[..., 2 * BRANCH_W:2 * BRANCH_W + KV_W], GQA_KV_HEADS), k_gain)
    vb_c = split_heads(zc_kv[..., 2 * BRANCH_W + KV_W:], GQA_KV_HEADS)
    o_na = neighbourhood_attention(
        split_heads(z[..., QA0:QB0], NA_HEADS), split_heads(z[..., KA0:VA0], NA_HEADS),
        split_heads(z[..., VA0:KB0], NA_HEADS), ka_c, va_c, rpb)
    qb = apply_axial_rope(rms_norm(split_heads(z[..., QB0:KA0], GQA_HEADS), q_gain), ang_row, ang_col)
    kb = apply_axial_rope(rms_norm(split_heads(z[..., KB0:VB0], GQA_KV_HEADS), k_gain), ang_row, ang_col)
    o_gqa = blocked_latent_attention(qb, kb, split_heads(z[..., VB0:CB0], GQA_KV_HEADS), kb_c, vb_c)
    out = merge_branches(z, o_na, o_gqa, conv_branch(z, conv_w), w_branch, w_o)
    if not with_ctx_out:
        return out, None
    o_na_c = grouped_attention(split_heads(zc[..., QA0:QB0], NA_HEADS), ka_c, va_c)
    qb_c = rms_norm(split_heads(zc[..., QB0:KA0], GQA_HEADS), q_gain)
    o_gqa_c = grouped_attention(qb_c, kb_c, vb_c)
    out_c = merge_branches(zc, o_na_c, o_gqa_c, conv_branch(zc, conv_w), w_branch, w_o)
    return out, out_c


def setup_inputs(seed: int = 0) -> dict:
    key = jax.random.key(seed)
    ks = jax.random.split(key, 17)

    def nrm(k, shape, s):
        return jax.random.normal(k, shape, jnp.float32) * s

    return {
        'x': nrm(ks[0], (BATCH, SEQ, D_MODEL), 1.0),
        'c': nrm(ks[1], (BATCH, D_MODEL), 1.0),
        'ctx': nrm(ks[2], (BATCH, CTX_LEN, D_MODEL), 1.0),
        'c_ctx': nrm(ks[3], (D_MODEL,), 1.0),
        'w_mod': nrm(ks[4], (DEPTH, D_MODEL, 6 * D_MODEL), 0.5 * D_MODEL ** -0.5),
        'b_mod': nrm(ks[5], (DEPTH, 6 * D_MODEL), 0.02),
        'w_in': nrm(ks[6], (DEPTH, D_MODEL, IN_W), D_MODEL ** -0.5),
        'rpb': nrm(ks[7], (DEPTH, NA_HEADS, 2 * WIN_R - 1, 2 * WIN_C - 1), 0.1),
        'q_gain': 1.0 + nrm(ks[8], (DEPTH, HEAD_DIM), 0.02),
        'k_gain': 1.0 + nrm(ks[9], (DEPTH, HEAD_DIM), 0.02),
        'conv_w': nrm(ks[10], (DEPTH, CONV_K, BRANCH_W), CONV_K ** -0.5),
        'w_branch': nrm(ks[11], (DEPTH, N_BRANCH, BRANCH_W, D_MODEL), DN_BETA * BRANCH_W ** -0.5),
        'w_o': nrm(ks[12], (DEPTH, D_MODEL, D_MODEL), DN_BETA * D_MODEL ** -0.5),
        'w_up': nrm(ks[13], (DEPTH, D_MODEL, MLP_HIDDEN), D_MODEL ** -0.5),
        'w_down': nrm(ks[14], (DEPTH, MLP_HIDDEN, D_MODEL), DN_BETA * MLP_HIDDEN ** -0.5),
        'ln_g': 1.0 + nrm(ks[15], (DEPTH, 2, D_MODEL), 0.02),
        'ln_b': nrm(ks[16], (DEPTH, 2, D_MODEL), 0.02),
    }


def reference(x, c, ctx, c_ctx, w_mod, b_mod, w_in, rpb, q_gain, k_gain, conv_w,
              w_branch, w_o, w_up, w_down, ln_g, ln_b):
    ang_row, ang_col = axial_rope_angles(x.shape[1])
    c_act = jax.nn.silu(c)
    cc_act = jax.nn.silu(c_ctx)
    for l in range(DEPTH):
        with_ctx_out = l < DEPTH - 1
        mod = jnp.split((c_act @ w_mod[l] + b_mod[l])[:, None, :], 6, axis=-1)
        mod_c = jnp.split(cc_act @ w_mod[l] + b_mod[l], 6, axis=-1)
        h = modulate(x, mod[0], mod[1])
        hc = modulate(ctx, mod_c[0], mod_c[1])
        mix, mix_c = token_mixer(h, hc, w_in[l], w_branch[l], w_o[l], rpb[l], q_gain[l], k_gain[l],
                                 conv_w[l], ang_row, ang_col, with_ctx_out)
        x = layer_norm(DN_ALPHA * x + mod[2] * mix) * ln_g[l, 0] + ln_b[l, 0]
        h = modulate(x, mod[3], mod[4])
        x = layer_norm(DN_ALPHA * x + mod[5] * squared_relu_mlp(h, w_up[l], w_down[l])) * ln_g[l, 1] + ln_b[l, 1]
        if with_ctx_out:
            ctx = layer_norm(DN_ALPHA * ctx + mod_c[2] * mix_c) * ln_g[l, 0] + ln_b[l, 0]
            hc = modulate(ctx, mod_c[3], mod_c[4])
            ctx = layer_norm(DN_ALPHA * ctx + mod_c[5] * squared_relu_mlp(hc, w_up[l], w_down[l])) * ln_g[l, 1] + ln_b[l, 1]
    return x
```

```python
import contextlib
import numpy as np
import concourse.bass as bass
import concourse.mybir as mybir
from concourse.bass_utils import run_bass_kernel_spmd

F32 = mybir.dt.float32
BF16 = mybir.dt.bfloat16
AF = mybir.ActivationFunctionType
ALU = mybir.AluOpType
AX = mybir.AxisListType

D = 2048
KC = 16
S = 2048
L = 256
NT = S + L
NTILE = NT // 128
DEPTH = 2
IN_W = 13824
QA0, QB0, KA0, VA0, KB0, VB0, CB0, CC0, CX0, G0 = 0, 1024, 2048, 3072, 4096, 4352, 4608, 5632, 6656, 7680
HID = 8192
EPS = 1e-6
SCALE = 128 ** -0.5
ALPHA = (2 * DEPTH) ** 0.25
GROUPS = [(0, 512), (512, 512), (1024, 512), (1536, 512), (2048, 256)]
NEG = -30000.0
NFILL = 1


class Sem:
    def __init__(self, handle):
        self.handle = handle
        self.val = 0


class Eng:
    def __init__(self, obj, kind, sem=None):
        self.obj = obj
        self.kind = kind
        self.sem = sem
        self.count = 0
        self.waited = {}


class T:
    __slots__ = ("ap", "last_w", "reads", "lsem", "ssem")

    def __init__(self, ap=None):
        self.ap = ap
        self.last_w = None
        self.reads = {}
        self.lsem = None
        self.ssem = None


class KB:
    def __init__(self, nc, nsem=100):
        self.nc = nc
        self.stack = contextlib.ExitStack()
        self.free_sems = []
        self.all_sems = []
        for i in range(nsem):
            s = Sem(self.stack.enter_context(nc.semaphore(f"s{i}")))
            self.free_sems.append(s)
            self.all_sems.append(s)
        mk = lambda obj, kind: Eng(obj, kind, self.free_sems.pop())
        self.pe = mk(nc.tensor, "pe")
        self.act = mk(nc.scalar, "act")
        self.dve = mk(nc.vector, "dve")
        self.pool = mk(nc.gpsimd, "pool")
        self.sp = Eng(nc.sync, "sp", None)
        self.engs = [self.pe, self.act, self.dve, self.pool, self.sp]
        self.phase_sems = []
        self.uid = 0

    def name(self, p):
        self.uid += 1
        return f"{p}{self.uid}"

    def sb(self, st, shape, dt, name="t"):
        return st.enter_context(self.nc.sbuf_tensor(self.name(name), list(shape), dt))

    def ps(self, st, shape, dt, name="p"):
        return st.enter_context(self.nc.psum_tensor(self.name(name), list(shape), dt))

    def getsem(self):
        s = self.free_sems.pop()
        self.phase_sems.append(s)
        return s

    def _wait(self, eng, deps):
        for sem, val in deps:
            if eng.waited.get(sem, 0) >= val:
                continue
            eng.obj.wait_ge(sem.handle, val)
            eng.waited[sem] = val

    def _deps(self, eng, reads, writes):
        deps = {}

        def add(tok, skip_same):
            if tok is None:
                return
            sem, val = tok
            if eng.sem is not None and sem is eng.sem and (skip_same or eng.kind == "pe"):
                return
            if deps.get(sem, 0) < val:
                deps[sem] = val

        for t in reads:
            add(t.last_w, False)
        for t in writes:
            add(t.last_w, True)
            for s, v in t.reads.items():
                add((s, v), True)
        return deps

    def op(self, eng, fn, reads=(), writes=()):
        deps = self._deps(eng, reads, writes)
        self._wait(eng, deps.items())
        ins = fn()
        eng.count += 1
        ins.then_inc(eng.sem.handle, 1)
        tok = (eng.sem, eng.count)
        for t in reads:
            t.reads[eng.sem] = eng.count
        for t in writes:
            t.last_w = tok
            t.reads = {}
        return ins

    def dma(self, q, out_ap, in_ap, reads, writes, load_into=None, store_from=None, **kw):
        if load_into is not None:
            if load_into.lsem is None:
                load_into.lsem = self.getsem()
            sem = load_into.lsem
        else:
            if store_from.ssem is None:
                store_from.ssem = self.getsem()
            sem = store_from.ssem
        deps = self._deps(q, reads, writes)
        if sem.val > 0 and deps.get(sem, 0) < sem.val:
            deps[sem] = sem.val
        self._wait(q, deps.items())
        ins = q.obj.dma_start(out=out_ap, in_=in_ap, **kw)
        sem.val += 16
        ins.then_inc(sem.handle, 16)
        tok = (sem, sem.val)
        for t in reads:
            t.reads[sem] = sem.val
        for t in writes:
            t.last_w = tok
            t.reads = {}

    def barrier(self, release=True):
        toks = [(e.sem, e.count) for e in self.engs if e.sem is not None and e.count > 0]
        toks += [(s, s.val) for s in self.all_sems if s.val > 0 and all(s is not e.sem for e in self.engs)]
        for e in self.engs:
            self._wait(e, [t for t in toks if t[0] is not e.sem])
        if release:
            self.free_sems.extend(self.phase_sems)
            self.phase_sems = []

    def mm(self, out_t, out_ap, lhsT_ap, rhs_ap, reads, start=True, stop=True):
        return self.op(self.pe, lambda: self.nc.tensor.matmul(out_ap, lhsT_ap, rhs_ap, start=start, stop=stop),
                       reads=reads, writes=[out_t])

    def tr(self, out_t, out_ap, in_ap, ident_ap, reads):
        return self.op(self.pe, lambda: self.nc.tensor.transpose(out_ap, in_ap, ident_ap), reads=reads, writes=[out_t])

    def actf(self, out_ap, in_ap, func, reads, writes, **kw):
        return self.op(self.act, lambda: self.nc.scalar.activation(out_ap, in_ap, func, **kw), reads=reads, writes=writes)


def build_program(debug=False, stop_after=None, nlayers=DEPTH, units=(0, 1)):
    nc = bass.Bass("TRN2", target_bir_lowering=False)
    K = KB(nc)
    pe, act, dve, pool, sp = K.pe, K.act, K.dve, K.pool, K.sp
    V = nc.vector
    dbg_kind = "ExternalOutput" if debug else "Internal"

    def din(name, shape, dt=F32):
        return nc.dram_tensor(name, list(shape), dt, kind="ExternalInput").ap()

    def dscr(name, shape, dt):
        return nc.dram_tensor(name, list(shape), dt, kind=dbg_kind).ap()

    x_in = din("x", [2, S, D])
    ctx_in = din("ctx", [2, L, D])
    cvec = din("cvec", [4, D])
    w_mod = din("w_mod", [DEPTH, D, 6 * D])
    b_mod = din("b_mod", [DEPTH, 6 * D])
    w_in = din("w_in", [DEPTH, D, IN_W])
    btab = din("btab", [DEPTH, 8, 128, 21 * 128])
    q_gain = din("q_gain", [DEPTH, 128])
    k_gain = din("k_gain", [DEPTH, 128])
    conv_w = din("conv_w", [DEPTH, 3, 1024])
    w_branch = din("w_branch", [DEPTH, 3, 1024, D])
    w_o = din("w_o", [DEPTH, D, D])
    w_up = din("w_up", [DEPTH, D, HID])
    w_down = din("w_down", [DEPTH, HID, D])
    ln_g = din("ln_g", [DEPTH, 2, D])
    ln_b = din("ln_b", [DEPTH, 2, D])
    rope_cos = din("rope_cos", [NT, 128])
    rope_sin = din("rope_sin", [NT, 128])
    ident_in = din("ident", [128, 128])
    y_out = nc.dram_tensor("y", [2, S, D], F32, kind="ExternalOutput").ap()

    xs = dscr("xs", [2, NT, D], F32)
    qaT = dscr("qaT", [8, 128, NT], BF16)
    kaT = dscr("kaT", [8, 128, NT], BF16)
    va = dscr("va", [NT, 1024], BF16)
    qbT = dscr("qbT", [8, 128, NT], BF16)
    kbT = dscr("kbT", [2, 128, NT], BF16)
    vb = dscr("vb", [NT, 256], BF16)
    ycT = dscr("ycT", [8, 128, NT], BF16)
    gT = dscr("gT", [48, 128, NT], BF16)
    onT = dscr("onT", [8, 128, NT], BF16)
    ogT = dscr("ogT", [8, 128, NT], BF16)
    mT = dscr("mT", [16, 128, NT], BF16)

    xs_T = [[T() for _ in GROUPS] for _ in range(2)]
    qaT_T = [T() for _ in range(8)]
    kaT_T = [T() for _ in range(8)]
    va_T = [T() for _ in range(NTILE)]
    qbT_T = [T() for _ in range(2)]
    kbT_T = T()
    vb_T = [T() for _ in range(NTILE)]
    ycT_T = [T() for _ in range(8)]
    gT_T = [T() for _ in range(48)]
    onT_T = [T() for _ in range(8)]
    ogT_T = [T() for _ in range(8)]
    mT_T = [T() for _ in range(16)]
    DIN = T()

    gst = K.stack
    ident_f = K.sb(gst, [128, 128], F32, "identf"); ident_f_T = T()
    ident_b = K.sb(gst, [128, 128], BF16, "identb"); ident_b_T = T()
    ones_b = K.sb(gst, [128, 128], BF16, "onesb"); ones_T = T()
    modT = [K.sb(gst, [128, 96, 4], F32, f"modT{l}") for l in range(DEPTH)]
    modT_T = [T() for _ in range(DEPTH)]

    K.dma(sp, ident_f[:], ident_in[:, :], [DIN], [ident_f_T], load_into=ident_f_T)
    K.dma(pool, ident_b[:], ident_in[:, :], [DIN], [ident_b_T], load_into=ident_b_T)
    K.op(dve, lambda: V.memset(ones_b[:], 1.0), writes=[ones_T])
    ones_f = K.sb(gst, [128, 128], F32, "onesf")
    K.op(dve, lambda: V.memset(ones_f[:], 1.0), writes=[ones_T])
    eps_t = K.sb(gst, [128, 1], F32, "eps"); eps_T = T()
    K.op(dve, lambda: V.memset(eps_t[:], EPS), writes=[eps_T])

    cact = K.sb(gst, [128, 64], BF16, "cact"); cact_T = T()

    def phase_cact():
        with contextlib.ExitStack() as st:
            cv = K.sb(st, [64, 128], F32); cv_T = T()
            psc_full = K.ps(st, [128, 512], F32); psc = psc_full[:, 0:64]; psc_T = T()
            K.dma(sp, cv[:], cvec.rearrange("r (kc p) -> (r kc) p", p=128), [DIN], [cv_T], load_into=cv_T)
            K.tr(psc_T, psc, cv[:], ident_f[0:64, 0:64], [cv_T, ident_f_T])
            K.actf(cact[:], psc, AF.Silu, [psc_T], [cact_T])
            K.barrier()

    def mod_gen(l, st, NB):
        cact_v = cact[:].rearrange("p (r kc) -> p kc r", kc=16)
        wm = [K.sb(st, [128, 16, 256], BF16) for _ in range(NB)]
        wm_T = [T() for _ in range(NB)]
        psm = K.ps(st, [128, 128, 4], F32); psm_T = T()
        psb_full = K.ps(st, [128, 512], F32); psb = psb_full[:, 0:96]; psb_T = T()
        brow = K.sb(st, [96, 128], F32); brow_T = T()
        bsb = K.sb(st, [128, 96], F32); bsb_T = T()
        wmv = w_mod[l].rearrange("(kc p) n -> p kc n", p=128)

        def load(i):
            K.dma(pool, wm[i % NB][:], wmv[:, :, i * 256:(i + 1) * 256], [DIN], [wm_T[i % NB]], load_into=wm_T[i % NB])

        for i in range(NB - 1):
            load(i)
        for b in range(48):
            if b + NB - 1 < 48:
                load(b + NB - 1)
            for c in range(2):
                n = b * 2 + c
                for kc in range(16):
                    K.mm(psm_T, psm[:, n, :], wm[b % NB][:, kc, c * 128:(c + 1) * 128], cact_v[:, kc, :],
                         [wm_T[b % NB], cact_T], start=(kc == 0), stop=(kc == 15))
            if b == 47:
                K.dma(sp, brow[:], b_mod[l].rearrange("(n p) -> n p", p=128), [DIN], [brow_T], load_into=brow_T)
                K.tr(psb_T, psb, brow[:], ident_f[0:96, 0:96], [brow_T, ident_f_T])
                K.op(dve, lambda: V.tensor_copy(bsb[:], psb), reads=[psb_T], writes=[bsb_T])
                for r in range(4):
                    K.op(dve, lambda r=r: V.tensor_tensor(modT[l][:, :, r], psm[:, 0:96, r], bsb[:], ALU.add),
                         reads=[psm_T, bsb_T], writes=[modT_T[l]])
                for p0 in (16, 64):
                    K.op(dve, lambda p0=p0: V.tensor_scalar_add(modT[l][:, p0:p0 + 16, :], modT[l][:, p0:p0 + 16, :], 1.0),
                         reads=[modT_T[l]], writes=[modT_T[l]])
            yield b

    def phase_mod(l):
        with contextlib.ExitStack() as st:
            for _ in mod_gen(l, st, 6):
                pass
            K.barrier()

    phase_cact()
    phase_mod(0)
    mod_pending = list(range(1, nlayers))


    def row_of(u, t0):
        return u if t0 < S else 2

    def QG(l):
        return GROUPS if l < nlayers - 1 else GROUPS[:4]

    def ln_stats(st_tiles, x_ap, x_T):
        bst, mv, rs, nm, sT = st_tiles
        xv = x_ap.rearrange("p (c f) -> p c f", f=512)
        for c in range(4):
            K.op(dve, lambda c=c: V.bn_stats(bst[:, c, :], xv[:, c, :]), reads=[x_T], writes=[sT])
        K.op(dve, lambda: V.bn_aggr(mv[:], bst[:].rearrange("p c f -> p (c f)")), reads=[sT], writes=[sT])
        K.actf(rs[:], mv[:, 1:2], AF.Sqrt, [sT, eps_T], [sT], bias=eps_t[:], scale=1.0)
        K.op(dve, lambda: V.reciprocal(rs[:], rs[:]), reads=[sT], writes=[sT])
        K.op(dve, lambda: V.scalar_tensor_tensor(nm[:], mv[:, 0:1], -1.0, rs[:], ALU.mult, ALU.mult), reads=[sT], writes=[sT])
        return rs, nm

    def mk_stats(st):
        return (K.sb(st, [128, 4, 6], F32), K.sb(st, [128, 2], F32), K.sb(st, [128, 1], F32), K.sb(st, [128, 1], F32), T())

    class LNMod:
        def __init__(self, st, l, part):
            self.l = l
            self.part = part
            self.xn = [K.sb(st, [128, D], BF16) for _ in range(4)]
            self.xn_T = [T() for _ in range(4)]
            self.stt = [mk_stats(st) for _ in range(2)]
            self.pT = [K.ps(st, [128, 2, 512], BF16) for _ in range(2)]
            self.pT_T = [T() for _ in range(2)]
            self.cnt = 0
            self.ev = 0

        def run(self, tiles, row, hT, hT_T, col0):
            self.prep(tiles)
            self.emitT(len(tiles), row, hT, hT_T, col0)

        def prep(self, tiles):
            for i, (x_ap, x_T) in enumerate(tiles):
                rs, nm = ln_stats(self.stt[self.cnt % 2], x_ap, x_T)
                sT = self.stt[self.cnt % 2][4]
                self.cnt += 1
                K.actf(self.xn[i][:], x_ap, AF.Identity, [x_T, sT], [self.xn_T[i]], scale=rs[:], bias=nm[:])

        def emitT(self, n, row, hT, hT_T, col0):
            l = self.l
            for kp in range(8):
                pt = self.pT[kp % 2]; pt_T = self.pT_T[kp % 2]
                for k2 in range(2):
                    kc = kp * 2 + k2
                    for i in range(n):
                        K.tr(pt_T, pt[:, k2, i * 128:(i + 1) * 128], self.xn[i][:, kc * 128:(kc + 1) * 128], ident_b[:],
                             [self.xn_T[i], ident_b_T])
                for k2 in range(2):
                    kc = kp * 2 + k2
                    sc = modT[l][:, (self.part + 1) * 16 + kc, row:row + 1]
                    sh = modT[l][:, self.part * 16 + kc, row:row + 1]
                    o = hT[:, kc, col0:col0 + n * 128]
                    if kp % 2 == 0:
                        K.op(dve, lambda o=o, pt=pt, k2=k2, sc=sc, sh=sh: V.tensor_scalar(o, pt[:, k2, 0:n * 128], sc, sh, ALU.mult, ALU.add),
                             reads=[pt_T, modT_T[l]], writes=[hT_T])
                    else:
                        K.actf(o, pt[:, k2, 0:n * 128], AF.Identity, [pt_T, modT_T[l]], [hT_T], scale=sc, bias=sh)
                    self.ev += 1

    def src_rows(l, u, t0, n):
        if l == 0:
            if t0 < S:
                return x_in[u, t0:t0 + n, :], DIN
            return ctx_in[u, t0 - S:t0 - S + n, :], DIN
        g = [i for i, (a, b) in enumerate(GROUPS) if a <= t0 < a + b][0]
        return xs[u, t0:t0 + n, :], xs_T[u][g]

    def phase_in_proj(l, u):
        with contextlib.ExitStack() as st:
            hT = K.sb(st, [128, KC, NT], BF16, "hT"); hT_T = [T() for _ in GROUPS]
            with contextlib.ExitStack() as s2:
                lnm = LNMod(s2, l, 0)
                xt = [K.sb(s2, [128, D], F32) for _ in range(8)]
                xt_T = [T() for _ in range(8)]

                def loadx(gi):
                    t0, ng = GROUPS[gi]
                    for i in range(ng // 128):
                        ap, dT = src_rows(l, u, t0 + i * 128, 128)
                        k = (gi % 2) * 4 + i
                        K.dma(sp, xt[k][:], ap, [dT], [xt_T[k]], load_into=xt_T[k])

                loadx(0)
                for gi, (t0, ng) in enumerate(GROUPS):
                    if gi + 1 < len(GROUPS):
                        loadx(gi + 1)
                    tiles = [(xt[(gi % 2) * 4 + i][:], xt_T[(gi % 2) * 4 + i]) for i in range(ng // 128)]
                    lnm.run(tiles, row_of(u, t0), hT, hT_T[gi], t0)
                K.barrier()
            if stop_after == "A":
                if debug:
                    dbg = nc.dram_tensor("dbg_hT", [128, KC, NT], BF16, kind="ExternalOutput").ap()
                    dT = T()
                    K.dma(sp, dbg[:, :, :], hT[:], hT_T, [dT], store_from=hT_T[0])
                    K.barrier()
                return
            wv = w_in[l].rearrange("(kc p) n -> p kc n", p=128)

            with contextlib.ExitStack() as s2:
                NW = 2
                wt = [K.sb(s2, [128, KC, 512], BF16) for _ in range(NW)]
                wt_T = [T() for _ in range(NW)]
                cosT = K.sb(s2, [128, NTILE, 128], F32); sinT = K.sb(s2, [128, NTILE, 128], F32); tab_T = T()
                gq = K.sb(s2, [128, 128], F32); gk = K.sb(s2, [128, 128], F32); gain_T = T()
                K.dma(sp, cosT[:], rope_cos.rearrange("(t p) d -> p t d", p=128), [DIN], [tab_T], load_into=tab_T)
                K.dma(sp, sinT[:], rope_sin.rearrange("(t p) d -> p t d", p=128), [DIN], [tab_T], load_into=tab_T)
                K.dma(sp, gq[:], q_gain[l].partition_broadcast(128), [DIN], [gain_T], load_into=gain_T)
                K.dma(sp, gk[:], k_gain[l].partition_broadcast(128), [DIN], [gain_T], load_into=gain_T)
                pz = [K.ps(s2, [128, 512], F32) for _ in range(3)]; pz_T = [T() for _ in range(3)]
                ptr = [K.ps(s2, [128, 8, 128], BF16) for _ in range(2)]; ptr_T = [T() for _ in range(2)]
                NZ = 3
                zs = [K.sb(s2, [128, 512], F32) for _ in range(NZ)]; zs_T = [T() for _ in range(NZ)]
                sq = [K.sb(s2, [128, 512], F32) for _ in range(2)]; sq_T = [T() for _ in range(2)]
                ss = [K.sb(s2, [128, 4], F32) for _ in range(NZ)]
                t1 = [K.sb(s2, [128, 512], F32) for _ in range(2)]; t1_T = [T() for _ in range(2)]
                t2 = [K.sb(s2, [128, 512], F32) for _ in range(2)]; t2_T = [T() for _ in range(2)]
                qr = [K.sb(s2, [128, 512], BF16) for _ in range(NZ)]; qr_T = [T() for _ in range(NZ)]
                qst = K.sb(s2, [128, 4, NT], BF16); qst_T = T()
                vst = [K.sb(s2, [128, 512], BF16) for _ in range(3)]; vst_T = [T() for _ in range(3)]
                tmblocks = [("q", QB0, 0), ("q", QB0 + 512, 1), ("kv", KB0, 0), ("v", VA0, 0), ("v", VA0 + 512, 1)]

                def loadw(i):
                    c0 = tmblocks[i][1]
                    K.dma(pool, wt[i % NW][:], wv[:, :, c0:c0 + 512], [DIN], [wt_T[i % NW]], load_into=wt_T[i % NW])

                loadw(0)
                cnt = 0
                pend = []
                trc = [0]

                def do_tr(item):
                    q_, q_T, nh, t = item
                    pt = ptr[trc[0] % 2]; pt_T = ptr_T[trc[0] % 2]
                    trc[0] += 1
                    for h in range(nh):
                        K.tr(pt_T, pt[:, h, :], q_[:, h * 128:(h + 1) * 128], ident_b[:], [q_T, ident_b_T])
                    K.actf(qst[:, 0:nh, t * 128:(t + 1) * 128], pt[:, 0:nh, :], AF.Copy, [pt_T], [qst_T])

                for bi, (kind, c0, sub) in enumerate(tmblocks):
                    if bi + 1 < len(tmblocks):
                        loadw(bi + 1)
                    w = wt[bi % NW]; w_T = wt_T[bi % NW]
                    for t in range(NTILE if (kind != "q" or l < nlayers - 1) else 16):
                        gi = min(t // 4, 4)
                        p = pz[cnt % 3]; p_T = pz_T[cnt % 3]
                        for kc in range(KC):
                            K.mm(p_T, p[:], hT[:, kc, t * 128:(t + 1) * 128], w[:, kc, :], [hT_T[gi], w_T],
                                 start=(kc == 0), stop=(kc == KC - 1))
                        if kind == "v":
                            v_ = vst[cnt % 3]; v_T = vst_T[cnt % 3]
                            K.actf(v_[:], p[:], AF.Copy, [p_T], [v_T])
                            K.dma(sp, va[t * 128:(t + 1) * 128, sub * 512:(sub + 1) * 512], v_[:], [v_T], [va_T[t]], store_from=v_T)
                        else:
                            nh = 4 if kind == "q" else 2
                            W_ = nh * 128
                            gain = gq if kind == "q" else gk
                            z = zs[cnt % NZ]; z_T = zs_T[cnt % NZ]
                            s_ = ss[cnt % NZ]
                            q_ = qr[cnt % NZ]; q_T = qr_T[cnt % NZ]
                            sq_ = sq[cnt % 2]; sq__T = sq_T[cnt % 2]
                            t1_ = t1[cnt % 2]; t1__T = t1_T[cnt % 2]
                            t2_ = t2[cnt % 2]; t2__T = t2_T[cnt % 2]
                            K.actf(z[:, 0:W_], p[:, 0:W_], AF.Copy, [p_T], [z_T])
                            if kind == "kv":
                                v_ = vst[cnt % 3]; v_T = vst_T[cnt % 3]
                                K.actf(v_[:, 0:256], p[:, 256:512], AF.Copy, [p_T], [v_T])
                                K.dma(sp, vb[t * 128:(t + 1) * 128, :], v_[:, 0:256], [v_T], [vb_T[t]], store_from=v_T)
                            z3 = z[:, 0:W_].rearrange("p (h d) -> p h d", d=128)
                            K.op(pool, lambda: nc.gpsimd.tensor_tensor(sq_[:, 0:W_], z[:, 0:W_], z[:, 0:W_], ALU.mult), reads=[z_T], writes=[sq__T])
                            K.op(dve, lambda: V.tensor_reduce(s_[:, 0:nh], sq_[:, 0:W_].rearrange("p (h d) -> p h d", d=128), AX.X, ALU.add),
                                 reads=[sq__T], writes=[z_T])
                            K.actf(s_[:, 0:nh], s_[:, 0:nh], AF.Sqrt, [z_T, eps_T], [z_T], bias=eps_t[:], scale=1.0 / 128)
                            K.op(dve, lambda: V.reciprocal(s_[:, 0:nh], s_[:, 0:nh]), reads=[z_T], writes=[z_T])
                            K.op(dve, lambda: V.tensor_tensor(z3, z3, s_[:, 0:nh].unsqueeze(2).to_broadcast([128, nh, 128]), ALU.mult), reads=[z_T], writes=[z_T])
                            K.op(dve, lambda: V.tensor_tensor(z3, z3, gain[:].unsqueeze(1).to_broadcast([128, nh, 128]), ALU.mult), reads=[z_T, gain_T], writes=[z_T])
                            K.op(dve, lambda: V.tensor_tensor(t1_[:, 0:W_].rearrange("p (h d) -> p h d", d=128), z3,
                                                              cosT[:, t, :].unsqueeze(1).to_broadcast([128, nh, 128]), ALU.mult),
                                 reads=[z_T, tab_T], writes=[t1__T])
                            zv = z[:, 0:W_].rearrange("p (h a b c) -> p h a b c", a=2, b=2, c=32)
                            sv = sinT[:, t, :].rearrange("p (a b c) -> p a b c", a=2, b=2)
                            tv = t2_[:, 0:W_].rearrange("p (h a b c) -> p h a b c", a=2, b=2, c=32)
                            K.op(dve, lambda: V.tensor_tensor(tv[:, :, :, 0, :], zv[:, :, :, 1, :], sv[:, :, 0, :].unsqueeze(1).to_broadcast([128, nh, 2, 32]), ALU.mult),
                                 reads=[z_T, tab_T], writes=[t2__T])
                            K.op(dve, lambda: V.tensor_tensor(tv[:, :, :, 1, :], zv[:, :, :, 0, :], sv[:, :, 1, :].unsqueeze(1).to_broadcast([128, nh, 2, 32]), ALU.mult),
                                 reads=[z_T, tab_T], writes=[t2__T])
                            K.op(pool, lambda: nc.gpsimd.tensor_tensor(q_[:, 0:W_], t1_[:, 0:W_], t2_[:, 0:W_], ALU.add), reads=[t1__T, t2__T], writes=[q_T])
                            pend.append((q_, q_T, nh, t))
                            if len(pend) > 2:
                                do_tr(pend.pop(0))
                        cnt += 1
                    while pend:
                        do_tr(pend.pop(0))
                    if kind == "q":
                        K.dma(sp, qbT[sub * 4:(sub + 1) * 4].rearrange("h p t -> p h t"), qst[:], [qst_T], [qbT_T[sub]], store_from=qst_T)
                    elif kind == "kv":
                        K.dma(sp, kbT.rearrange("h p t -> p h t"), qst[:, 0:2, :], [qst_T], [kbT_T], store_from=qst_T)
                K.barrier()
            if stop_after == "B1":
                return

            with contextlib.ExitStack() as s2:
                NW = 4
                wt = [K.sb(s2, [128, KC, 256], BF16) for _ in range(NW)]
                wt_T = [T() for _ in range(NW)]
                pz = [K.ps(s2, [128, 512], F32) for _ in range(6)]; pz_T = [T() for _ in range(6)]
                stg = [K.sb(s2, [128, NT], BF16) for _ in range(3)]; stg_T = [T() for _ in range(3)]
                ccs = K.sb(s2, [128, NT], F32); ccs_T = T()
                vv = K.sb(s2, [128, NT], F32); vv_T = T()
                acc = K.sb(s2, [128, NT], F32); acc_T = T()
                cwr = K.sb(s2, [24, 128], F32); cwr_T = T()
                cw = K.sb(s2, [128, 24], F32); cw_T = T()
                K.dma(sp, cwr[:], conv_w[l].rearrange("k (c p) -> (k c) p", p=128), [DIN], [cwr_T], load_into=cwr_T)
                K.tr(pz_T[0], pz[0][:, 0:24], cwr[:], ident_f[0:24, 0:24], [cwr_T, ident_f_T])
                K.op(dve, lambda: V.tensor_copy(cw[:], pz[0][:, 0:24]), reads=[pz_T[0]], writes=[cw_T])
                wl = []
                for h2 in range(4):
                    wl.append(("qa", QA0 + h2 * 256, h2))
                for h2 in range(4):
                    wl.append(("ka", KA0 + h2 * 256, h2))
                for c2 in range(4):
                    wl.append(("cc", CC0 + c2 * 256, c2)); wl.append(("cx", CX0 + c2 * 256, c2)); wl.append(("cb", CB0 + c2 * 256, c2))
                for g2 in range(24):
                    wl.append(("g", G0 + g2 * 256, g2))

                def loadw(i):
                    c0 = wl[i][1]
                    K.dma(pool, wt[i % NW][:], wv[:, :, c0:c0 + 256], [DIN], [wt_T[i % NW]], load_into=wt_T[i % NW])

                for i in range(NW - 1):
                    loadw(i)
                mg = None
                if mod_pending and mod_pending[0] == l + 1:
                    mg = mod_gen(mod_pending.pop(0), s2, 4)
                pc = [0]
                sc_ = [0]

                def proj_chunk(w, w_T, c, gi):
                    t0, ng = GROUPS[gi]
                    p = pz[pc[0] % 6]; p_T = pz_T[pc[0] % 6]
                    pc[0] += 1
                    for kc in range(KC):
                        K.mm(p_T, p[:, 0:ng], w[:, kc, c * 128:(c + 1) * 128], hT[:, kc, t0:t0 + ng], [w_T, hT_T[gi]],
                             start=(kc == 0), stop=(kc == KC - 1))
                    return p, p_T

                ev = [0]
                for wi, (kind, c0, idx) in enumerate(wl):
                    if wi + NW - 1 < len(wl):
                        loadw(wi + NW - 1)
                    if mg is not None and wi >= 2:
                        next(mg, None)
                        if kind == "g":
                            next(mg, None)
                    w = wt[wi % NW]; w_T = wt_T[wi % NW]
                    if kind in ("qa", "ka", "g"):
                        for c in range(2):
                            sg = stg[sc_[0] % 3]; sg_T = stg_T[sc_[0] % 3]
                            sc_[0] += 1
                            for gi, (t0, ng) in enumerate(GROUPS if kind == "ka" else QG(l)):
                                p, p_T = proj_chunk(w, w_T, c, gi)
                                if kind == "g":
                                    K.actf(sg[:, t0:t0 + ng], p[:, 0:ng], AF.Sigmoid, [p_T], [sg_T])
                                elif ev[0] % 2 == 0:
                                    K.actf(sg[:, t0:t0 + ng], p[:, 0:ng], AF.Copy, [p_T], [sg_T])
                                else:
                                    K.op(dve, lambda sg=sg, p=p, t0=t0, ng=ng: V.tensor_copy(sg[:, t0:t0 + ng], p[:, 0:ng]), reads=[p_T], writes=[sg_T])
                                ev[0] += 1
                            n = idx * 2 + c
                            dst, dT = {"qa": (qaT, qaT_T), "ka": (kaT, kaT_T), "g": (gT, gT_T)}[kind]
                            K.dma(sp, dst[n], sg[:], [sg_T], [dT[n]], store_from=sg_T)
                    elif kind == "cc":
                        wcx = wt[(wi + 1) % NW]; wcx_T = wt_T[(wi + 1) % NW]
                        wcb = wt[(wi + 2) % NW]; wcb_T = wt_T[(wi + 2) % NW]
                        for c in range(2):
                            ci = idx * 2 + c
                            for gi, (t0, ng) in enumerate(QG(l)):
                                p, p_T = proj_chunk(w, w_T, c, gi)
                                K.actf(ccs[:, t0:t0 + ng], p[:, 0:ng], AF.Copy, [p_T], [ccs_T])
                            for gi, (t0, ng) in enumerate(QG(l)):
                                p, p_T = proj_chunk(wcx, wcx_T, c, gi)
                                K.op(dve, lambda p=p, t0=t0, ng=ng: V.tensor_tensor(vv[:, t0:t0 + ng], p[:, 0:ng], ccs[:, t0:t0 + ng], ALU.mult),
                                     reads=[p_T, ccs_T], writes=[vv_T])
                            w0 = cw[:, 0 * 8 + ci:0 * 8 + ci + 1]; w1 = cw[:, 8 + ci:8 + ci + 1]; w2 = cw[:, 16 + ci:16 + ci + 1]
                            nv = NT if l < nlayers - 1 else S
                            K.actf(acc[:, 0:nv], vv[:, 0:nv], AF.Identity, [vv_T, cw_T], [acc_T], scale=w1)
                            for (a, b) in (((0, S), (S, NT)) if l < nlayers - 1 else ((0, S),)):
                                K.op(dve, lambda a=a, b=b, w0=w0: V.scalar_tensor_tensor(acc[:, a + 1:b], vv[:, a:b - 1], w0, acc[:, a + 1:b], ALU.mult, ALU.add),
                                     reads=[vv_T, acc_T, cw_T], writes=[acc_T])
                                K.op(dve, lambda a=a, b=b, w2=w2: V.scalar_tensor_tensor(acc[:, a:b - 1], vv[:, a + 1:b], w2, acc[:, a:b - 1], ALU.mult, ALU.add),
                                     reads=[vv_T, acc_T, cw_T], writes=[acc_T])
                            sg = stg[sc_[0] % 3]; sg_T = stg_T[sc_[0] % 3]
                            sc_[0] += 1
                            for gi, (t0, ng) in enumerate(QG(l)):
                                p, p_T = proj_chunk(wcb, wcb_T, c, gi)
                                K.op(dve, lambda sg=sg, p=p, t0=t0, ng=ng: V.tensor_tensor(sg[:, t0:t0 + ng], p[:, 0:ng], acc[:, t0:t0 + ng], ALU.mult),
                                     reads=[p_T, acc_T], writes=[sg_T])
                            K.dma(sp, ycT[ci], sg[:], [sg_T], [ycT_T[ci]], store_from=sg_T)
                if mg is not None:
                    for _ in mg:
                        pass
                K.barrier()

    def phase_gqa(l, u):
        with contextlib.ExitStack() as st:
            kb_s = K.sb(st, [128, 2, NT], BF16); kb_sT = T()
            vb_s = K.sb(st, [128, NTILE, 256], BF16); vb_sT = T()
            K.dma(sp, kb_s[:], kbT.rearrange("h p t -> p h t"), [kbT_T], [kb_sT], load_into=kb_sT)
            K.dma(sp, vb_s[:], vb.rearrange("(t p) c -> p t c", p=128), vb_T, [vb_sT], load_into=vb_sT)
            qT = [K.sb(st, [128, NT], BF16) for _ in range(2)]; qT_T = [T() for _ in range(2)]
            ost = [K.sb(st, [128, NT], BF16) for _ in range(2)]; ost_T = [T() for _ in range(2)]
            pS = [K.ps(st, [128, 512], F32) for _ in range(3)]; pS_T = [T() for _ in range(3)]
            pO = [K.ps(st, [128, 512], F32) for _ in range(2)]; pO_T = [T() for _ in range(2)]
            pD = [K.ps(st, [128, 512], F32) for _ in range(2)]; pD_T = [T() for _ in range(2)]
            P = [K.sb(st, [128, 512], BF16) for _ in range(3)]; P_T = [T() for _ in range(3)]
            rd = [K.sb(st, [128, 512], F32) for _ in range(2)]; rd_T = [T() for _ in range(2)]
            pF = K.ps(st, [128, 512], F32); pF_T = T()

            def loadq(h):
                K.dma(sp, qT[h % 2][:], qbT[h], [qbT_T[h // 4]], [qT_T[h % 2]], load_into=qT_T[h % 2])

            loadq(0)
            sc = 0
            oc = 0
            for h in range(8):
                if h + 1 < 8:
                    loadq(h + 1)
                g = h // 4
                q = qT[h % 2]; q_T = qT_T[h % 2]
                o_ = ost[h % 2]; o_T = ost_T[h % 2]
                for gi, (q0, nq) in enumerate(QG(l)):
                    keys = list(range(NTILE)) if q0 < S else [16, 17]
                    po = pO[oc % 2]; po_T = pO_T[oc % 2]
                    pd = pD[oc % 2]; pd_T = pD_T[oc % 2]
                    r_ = rd[oc % 2]; r_T = rd_T[oc % 2]
                    oc += 1

                    def smm(c, slot):
                        K.mm(pS_T[slot], pS[slot][:, 0:nq], kb_s[:, g, c * 128:(c + 1) * 128], q[:, q0:q0 + nq], [kb_sT, q_T])

                    smm(keys[0], sc % 3)
                    for ki, c in enumerate(keys):
                        slot = sc % 3
                        if ki + 1 < len(keys):
                            smm(keys[ki + 1], (sc + 1) % 3)
                        K.actf(P[slot][:, 0:nq], pS[slot][:, 0:nq], AF.Exp, [pS_T[slot]], [P_T[slot]], scale=SCALE)
                        K.mm(po_T, po[:, 0:nq], vb_s[:, c, g * 128:(g + 1) * 128], P[slot][:, 0:nq], [vb_sT, P_T[slot]],
                             start=(ki == 0), stop=(ki == len(keys) - 1))
                        K.mm(pd_T, pd[:, 0:nq], ones_b[:], P[slot][:, 0:nq], [ones_T, P_T[slot]],
                             start=(ki == 0), stop=(ki == len(keys) - 1))
                        for _ in range(NFILL):
                            K.mm(pF_T, pF[:, 0:128], ones_b[:], q[:, q0:q0 + 128], [ones_T, q_T])
                        sc += 1
                    K.op(dve, lambda: V.reciprocal(r_[:, 0:nq], pd[:, 0:nq]), reads=[pd_T], writes=[r_T])
                    K.op(dve, lambda: V.tensor_tensor(o_[:, q0:q0 + nq], po[:, 0:nq], r_[:, 0:nq], ALU.mult), reads=[po_T, r_T], writes=[o_T])
                K.dma(sp, ogT[h], o_[:], [o_T], [ogT_T[h]], store_from=o_T)
            K.barrier()

    def na_keys(j):
        if j == 0:
            return [0, 1, 2, 3], 5
        if j == 1:
            return [0, 1, 2, 3], 9
        if j == 14:
            return [12, 13, 14, 15], 13
        if j == 15:
            return [12, 13, 14, 15], 17
        return [j - 2, j - 1, j, j + 1, j + 2], 0

    def phase_na(l, u):
        with contextlib.ExitStack() as st:
            va_s = K.sb(st, [128, NTILE, 1024], BF16); va_sT = T()
            K.dma(sp, va_s[:], va.rearrange("(t p) c -> p t c", p=128), va_T, [va_sT], load_into=va_sT)
            kT = [K.sb(st, [128, NT], BF16) for _ in range(2)]; kT_T = [T() for _ in range(2)]
            qT = [K.sb(st, [128, NT], BF16) for _ in range(2)]; qT_T = [T() for _ in range(2)]
            bt = [K.sb(st, [128, 21 * 128], F32) for _ in range(2)]; bt_T = [T() for _ in range(2)]
            ost = [K.sb(st, [128, NT], BF16) for _ in range(2)]; ost_T = [T() for _ in range(2)]
            pS = [K.ps(st, [128, 1024], F32) for _ in range(2)]; pS_T = [T() for _ in range(2)]
            pO = [K.ps(st, [128, 512], F32) for _ in range(2)]; pO_T = [T() for _ in range(2)]
            pD = [K.ps(st, [128, 512], F32) for _ in range(2)]; pD_T = [T() for _ in range(2)]
            tmp = [K.sb(st, [128, 640], F32) for _ in range(2)]; tmp_T = [T() for _ in range(2)]
            P = [K.sb(st, [128, 896], BF16) for _ in range(2)]; P_T = [T() for _ in range(2)]
            rd = [K.sb(st, [128, 512], F32) for _ in range(2)]; rd_T = [T() for _ in range(2)]

            def loadh(h):
                K.dma(sp, kT[h % 2][:], kaT[h], [kaT_T[h]], [kT_T[h % 2]], load_into=kT_T[h % 2])
                K.dma(sp, qT[h % 2][:], qaT[h], [qaT_T[h]], [qT_T[h % 2]], load_into=qT_T[h % 2])
                K.dma(sp, bt[h % 2][:], btab[l, h], [DIN], [bt_T[h % 2]], load_into=bt_T[h % 2])

            loadh(0)
            work = []
            for h in range(8):
                items = [(j * 128, 128) + na_keys(j) for j in range(16)]
                if l < nlayers - 1:
                    items.append((S, 256, [], 0))
                for it, (q0, nq, loc, tb0) in enumerate(items):
                    work.append((h, it, q0, nq, loc, tb0))
            last_it = 16 if l < nlayers - 1 else 15

            def do_S(k):
                h, it, q0, nq, loc, tb0 = work[k]
                ps_ = pS[k % 2]; ps_T = pS_T[k % 2]
                chunks = loc + [16, 17]
                for i, c in enumerate(chunks):
                    K.mm(ps_T, ps_[:, i * nq:(i + 1) * nq], kT[h % 2][:, c * 128:(c + 1) * 128], qT[h % 2][:, q0:q0 + nq], [kT_T[h % 2], qT_T[h % 2]])

            oc = 0
            loadh(1)
            do_S(0)
            for k, (h, it, q0, nq, loc, tb0) in enumerate(work):
                if k + 1 < len(work):
                    do_S(k + 1)
                b_ = bt[h % 2]; b_T = bt_T[h % 2]
                o_ = ost[h % 2]; o_T = ost_T[h % 2]
                slot4 = it % 4 if it < 16 else 0
                if (it < 16 and slot4 == 0) or it == 16:
                    po = pO[oc % 2]; po_T = pO_T[oc % 2]
                    pd = pD[oc % 2]; pd_T = pD_T[oc % 2]
                    r_ = rd[oc % 2]; r_T = rd_T[oc % 2]
                    oc += 1
                ocol = slot4 * 128
                ps_ = pS[k % 2]; ps_T = pS_T[k % 2]
                tm = tmp[k % 2]; tm_T = tmp_T[k % 2]
                p_ = P[k % 2]; p_T = P_T[k % 2]
                chunks = loc + [16, 17]
                nl = len(loc)
                if nl:
                    K.op(dve, lambda: V.scalar_tensor_tensor(tm[:, 0:nl * 128], ps_[:, 0:nl * 128], SCALE, b_[:, tb0 * 128:(tb0 + nl) * 128], ALU.mult, ALU.add),
                         reads=[ps_T, b_T], writes=[tm_T])
                    K.actf(p_[:, 0:nl * 128], tm[:, 0:nl * 128], AF.Exp, [tm_T], [p_T])
                K.actf(p_[:, nl * nq:(nl + 2) * nq], ps_[:, nl * nq:(nl + 2) * nq], AF.Exp, [ps_T], [p_T], scale=SCALE)
                for i, c in enumerate(chunks):
                    K.mm(po_T, po[:, ocol:ocol + nq], va_s[:, c, h * 128:(h + 1) * 128], p_[:, i * nq:(i + 1) * nq], [va_sT, p_T],
                         start=(i == 0), stop=(i == len(chunks) - 1))
                for i, c in enumerate(chunks):
                    K.mm(pd_T, pd[:, ocol:ocol + nq], ones_b[:], p_[:, i * nq:(i + 1) * nq], [ones_T, p_T],
                         start=(i == 0), stop=(i == len(chunks) - 1))
                if it < 16 and slot4 == 3:
                    qq = q0 - 384
                    K.op(dve, lambda: V.reciprocal(r_[:], pd[:]), reads=[pd_T], writes=[r_T])
                    K.op(dve, lambda: V.tensor_tensor(o_[:, qq:qq + 512], po[:], r_[:], ALU.mult), reads=[po_T, r_T], writes=[o_T])
                elif it == 16:
                    K.op(dve, lambda: V.reciprocal(r_[:, 0:256], pd[:, 0:256]), reads=[pd_T], writes=[r_T])
                    K.op(dve, lambda: V.tensor_tensor(o_[:, S:NT], po[:, 0:256], r_[:, 0:256], ALU.mult), reads=[po_T, r_T], writes=[o_T])
                if it == last_it:
                    K.dma(sp, onT[h], o_[:], [o_T], [onT_T[h]], store_from=o_T)
                    if h + 2 < 8:
                        loadh(h + 2)
            K.barrier()

    def phase_merge(l, u):
        with contextlib.ExitStack() as st:
            srcs = []
            for (dr, dT) in ((onT, onT_T), (ogT, ogT_T), (ycT, ycT_T)):
                s_ = K.sb(st, [128, 8, NT], BF16)
                srcs.append((s_, [T() for _ in GROUPS], dr.rearrange("h p t -> p h t"), dT))
            for gi, (t0, ng) in enumerate(QG(l)):
                for (s_, s_Ts, dv, dT) in srcs:
                    K.dma(sp, s_[:, :, t0:t0 + ng], dv[:, :, t0:t0 + ng], dT, [s_Ts[gi]], load_into=s_Ts[gi])
            NW = 2
            wb = [K.sb(st, [128, 3, 8, 256], BF16) for _ in range(NW)]; wb_T = [T() for _ in range(NW)]
            gs = [K.sb(st, [128, 3, NT], BF16) for _ in range(2)]; gs_T = [T() for _ in range(2)]
            pz = [K.ps(st, [128, 512], F32) for _ in range(6)]; pz_T = [T() for _ in range(6)]
            ta = [K.sb(st, [128, 512], F32) for _ in range(2)]; ta_T = [T() for _ in range(2)]
            tb = [K.sb(st, [128, 512], F32) for _ in range(2)]; tb_T = [T() for _ in range(2)]
            stg = [K.sb(st, [128, NT], BF16) for _ in range(2)]; stg_T = [T() for _ in range(2)]
            wbv = w_branch[l].rearrange("i (kc p) n -> p i kc n", p=128)

            def loadw(i):
                K.dma(pool, wb[i % NW][:], wbv[:, :, :, i * 256:(i + 1) * 256], [DIN], [wb_T[i % NW]], load_into=wb_T[i % NW])

            def loadg(n):
                K.dma(sp, gs[n % 2][:], gT.rearrange("(i n) p t -> n p i t", i=3)[n], [gT_T[n], gT_T[16 + n], gT_T[32 + n]],
                      [gs_T[n % 2]], load_into=gs_T[n % 2])

            loadw(0)
            loadg(0)
            pc = 0
            tc = 0
            for n in range(16):
                if n % 2 == 0 and n // 2 + 1 < 8:
                    loadw(n // 2 + 1)
                if n + 1 < 16:
                    loadg(n + 1)
                w = wb[(n // 2) % NW]; w_T = wb_T[(n // 2) % NW]
                c = n % 2
                g_ = gs[n % 2]; g_T = gs_T[n % 2]
                sg = stg[n % 2]; sg_T = stg_T[n % 2]
                for gi, (t0, ng) in enumerate(QG(l)):
                    pp = []
                    for i in range(3):
                        p = pz[pc % 6]; p_T = pz_T[pc % 6]
                        pc += 1
                        for kc in range(8):
                            K.mm(p_T, p[:, 0:ng], w[:, i, kc, c * 128:(c + 1) * 128], srcs[i][0][:, kc, t0:t0 + ng], [w_T, srcs[i][1][gi]],
                                 start=(kc == 0), stop=(kc == 7))
                        pp.append((p, p_T))
                    a_ = ta[tc % 2]; a_T = ta_T[tc % 2]
                    b_ = tb[tc % 2]; b_T = tb_T[tc % 2]
                    tc += 1
                    K.op(dve, lambda: V.tensor_tensor(a_[:, 0:ng], pp[0][0][:, 0:ng], g_[:, 0, t0:t0 + ng], ALU.mult), reads=[pp[0][1], g_T], writes=[a_T])
                    K.op(dve, lambda: V.tensor_tensor(b_[:, 0:ng], pp[1][0][:, 0:ng], g_[:, 1, t0:t0 + ng], ALU.mult), reads=[pp[1][1], g_T], writes=[b_T])
                    K.op(dve, lambda: V.tensor_tensor(a_[:, 0:ng], a_[:, 0:ng], b_[:, 0:ng], ALU.add), reads=[a_T, b_T], writes=[a_T])
                    K.op(dve, lambda: V.tensor_tensor(b_[:, 0:ng], pp[2][0][:, 0:ng], g_[:, 2, t0:t0 + ng], ALU.mult), reads=[pp[2][1], g_T], writes=[b_T])
                    K.op(dve, lambda: V.tensor_tensor(sg[:, t0:t0 + ng], a_[:, 0:ng], b_[:, 0:ng], ALU.add), reads=[a_T, b_T], writes=[sg_T])
                K.dma(sp, mT[n], sg[:], [sg_T], [mT_T[n]], store_from=sg_T)
            K.barrier()

    class PostLN:
        def __init__(self, st, l, which, npz=3, ntacc=2):
            self.npz = npz
            self.l = l
            self.which = which
            self.gpart = 2 if which == 0 else 5
            self.tacc = [K.sb(st, [128, 4, D], F32, "tacc") for _ in range(ntacc)]
            self.tacc_T = [[T() for _ in range(4)] for _ in range(ntacc)]
            self.lng = K.sb(st, [128, D], F32); self.lnb = K.sb(st, [128, D], F32); self.ln_T = T()
            K.dma(sp, self.lng[:], ln_g[l, which].partition_broadcast(128), [DIN], [self.ln_T], load_into=self.ln_T)
            K.dma(sp, self.lnb[:], ln_b[l, which].partition_broadcast(128), [DIN], [self.ln_T], load_into=self.ln_T)
            self.pz = [K.ps(st, [128, 512], F32) for _ in range(npz)]; self.pz_T = [T() for _ in range(npz)]
            self.pt = [K.ps(st, [128, 4, 128], F32) for _ in range(2)]; self.pt_T = [T() for _ in range(2)]
            self.ys = [K.sb(st, [128, 512], F32) for _ in range(2)]; self.ys_T = [T() for _ in range(2)]
            self.stt = [mk_stats(st) for _ in range(2)]
            self.pc = 0
            self.yc = 0
            self.lc = 0
            self.pending = None
            self.fin = []
            self.after_fin = []

        def when_drained(self, fn):
            if not self.fin:
                fn()
            else:
                self.after_fin.append(fn)

        def load_x(self, l, u, t0, ng, b, from_xs=False):
            for i in range(ng // 128):
                ap, dT = src_rows(1 if from_xs else l, u, t0 + i * 128, 128)
                K.dma(sp, self.tacc[b][:, i, :], ap, [dT], [self.tacc_T[b][i]], load_into=self.tacc_T[b][i])

        def chunk(self, n, ng, row, mms, first, b):
            l = self.l
            p = self.pz[self.pc % self.npz]; p_T = self.pz_T[self.pc % self.npz]
            self.pc += 1
            for i, (lh, rh, rd_) in enumerate(mms):
                K.mm(p_T, p[:, 0:ng], lh, rh, rd_, start=(i == 0), stop=(i == len(mms) - 1))
            y_ = self.ys[self.yc % 2]; y_T = self.ys_T[self.yc % 2]
            pt = self.pt[self.yc % 2]; pt_T = self.pt_T[self.yc % 2]
            self.yc += 1
            K.actf(y_[:, 0:ng], p[:, 0:ng], AF.Identity, [p_T, modT_T[l]], [y_T], scale=modT[l][:, self.gpart * 16 + n, row:row + 1])
            self.flush()
            self.pending = (n, ng, first, y_, y_T, pt, pt_T, b)
            self.step_finish()

        def flush(self):
            if self.pending is None:
                return
            n, ng, first, y_, y_T, pt, pt_T, b = self.pending
            self.pending = None
            nt = ng // 128
            for i in range(nt):
                K.tr(pt_T, pt[:, i, :], y_[:, i * 128:(i + 1) * 128], ident_f[:], [y_T, ident_f_T])
            tv = self.tacc[b][:, 0:nt, n * 128:(n + 1) * 128]
            tT = self.tacc_T[b][0:nt]
            if first:
                K.op(dve, lambda: V.scalar_tensor_tensor(tv, tv, ALPHA, pt[:, 0:nt, :], ALU.mult, ALU.add), reads=[pt_T] + tT, writes=tT)
            else:
                K.op(dve, lambda: V.tensor_tensor(tv, tv, pt[:, 0:nt, :], ALU.add), reads=[pt_T] + tT, writes=tT)

        def _finish_tile(self, i, b, store_fn):
            xa = self.tacc[b][:, i, :]; x_T = self.tacc_T[b][i]
            stt = self.stt[self.lc % 2]
            self.lc += 1
            rs, nm = ln_stats(stt, xa, x_T)
            K.actf(xa, xa, AF.Identity, [x_T, stt[4]], [x_T], scale=rs[:], bias=nm[:])
            K.op(dve, lambda: V.tensor_tensor(xa, xa, self.lng[:], ALU.mult), reads=[x_T, self.ln_T], writes=[x_T])
            K.op(dve, lambda: V.tensor_tensor(xa, xa, self.lnb[:], ALU.add), reads=[x_T, self.ln_T], writes=[x_T])
            store_fn(i, xa, x_T)

        def finish(self, ng, store_fn, b):
            self.flush()
            for i in range(ng // 128):
                self.fin.append((i, b, store_fn))

        def step_finish(self, k=1):
            for _ in range(k):
                if self.fin:
                    self._finish_tile(*self.fin.pop(0))
            if not self.fin:
                while self.after_fin:
                    self.after_fin.pop(0)()

        def drain(self):
            self.flush()
            while self.fin:
                self._finish_tile(*self.fin.pop(0))
            while self.after_fin:
                self.after_fin.pop(0)()

    def dst_rows(l, u, t0, i, which):
        r0 = t0 + i * 128
        g = [k for k, (a, b) in enumerate(GROUPS) if a <= t0 < a + b][0]
        if l == nlayers - 1 and which == 1:
            if r0 >= S:
                return None
            return y_out[u, r0:r0 + 128, :], None
        return xs[u, r0:r0 + 128, :], xs_T[u][g]

    OUT_T = T()

    def phase_outproj(l, u):
        with contextlib.ExitStack() as st:
            wo = K.sb(st, [128, KC, D], BF16, "wo"); wo_T = [T() for _ in range(8)]
            wv = w_o[l].rearrange("(kc p) n -> p kc n", p=128)
            for c2 in range(8):
                K.dma(pool, wo[:, :, c2 * 256:(c2 + 1) * 256], wv[:, :, c2 * 256:(c2 + 1) * 256], [DIN], [wo_T[c2]], load_into=wo_T[c2])
            mTs = [K.sb(st, [128, 16, 512], BF16) for _ in range(2)]; mTs_T = [T() for _ in range(2)]
            mv = mT.rearrange("n p t -> p n t")

            def loadm(gi):
                t0, ng = GROUPS[gi]
                K.dma(sp, mTs[gi % 2][:, :, 0:ng], mv[:, :, t0:t0 + ng], mT_T, [mTs_T[gi % 2]], load_into=mTs_T[gi % 2])

            loadm(0)
            pl = PostLN(st, l, 0)
            pl.load_x(l, u, GROUPS[0][0], GROUPS[0][1], 0)
            for gi, (t0, ng) in enumerate(QG(l)):
                b = gi % 2
                if gi + 1 < len(QG(l)):
                    loadm(gi + 1)
                    pl.when_drained(lambda gi=gi: pl.load_x(l, u, GROUPS[gi + 1][0], GROUPS[gi + 1][1], (gi + 1) % 2))
                m_ = mTs[gi % 2]; m_T = mTs_T[gi % 2]
                for n in range(16):
                    mms = [(wo[:, kc, n * 128:(n + 1) * 128], m_[:, kc, 0:ng], [wo_T[n // 2], m_T]) for kc in range(KC)]
                    pl.chunk(n, ng, row_of(u, t0), mms, True, b)

                def store(i, xa, x_T, t0=t0):
                    d = dst_rows(l, u, t0, i, 0)
                    K.dma(sp, d[0], xa, [x_T], [d[1]], store_from=x_T)
                pl.finish(ng, store, b)
            pl.drain()
            K.barrier()

    def phase_mlp(l, u):
        NSPLIT = 2
        HC = 64 // NSPLIT
        with contextlib.ExitStack() as st:
            pl = PostLN(st, l, 1, npz=2)
            lnm = LNMod(st, l, 3)
            h2 = K.sb(st, [128, KC, 512], BF16); h2_T = T()
            uT = K.sb(st, [128, HC, 512], BF16); uT_T = T()
            NU = 3
            wu = [K.sb(st, [128, KC, 256], BF16) for _ in range(NU)]; wu_T = [T() for _ in range(NU)]
            ND = 3
            wd = [K.sb(st, [128, HC, 128], BF16) for _ in range(ND)]; wd_T = [T() for _ in range(ND)]
            pu = [K.ps(st, [128, 512], F32) for _ in range(2)]; pu_T = [T() for _ in range(2)]
            rl = [K.sb(st, [128, 512], F32) for _ in range(1)]; rl_T = [T() for _ in range(1)]
            wuv = w_up[l].rearrange("(kc p) n -> p kc n", p=128)
            wdv = w_down[l].rearrange("(kc p) n -> p kc n", p=128)
            ngroups = len(GROUPS) if l < nlayers - 1 else len(GROUPS) - 1
            ul = [(gi, sp_, j) for gi in range(ngroups) for sp_ in range(NSPLIT) for j in range(HC // 2)]
            dl = [(gi, sp_, j) for gi in range(ngroups) for sp_ in range(NSPLIT) for j in range(16)]

            def loadu(i):
                gi, sp_, j = ul[i]
                c0 = sp_ * HC * 128 + j * 256
                K.dma(pool, wu[i % NU][:], wuv[:, :, c0:c0 + 256], [DIN], [wu_T[i % NU]], load_into=wu_T[i % NU])

            def loadd(i):
                gi, sp_, j = dl[i]
                K.dma(pool, wd[i % ND][:], wdv[:, sp_ * HC:(sp_ + 1) * HC, j * 128:(j + 1) * 128], [DIN], [wd_T[i % ND]], load_into=wd_T[i % ND])

            ui = 0
            di = 0
            for i in range(NU - 1):
                loadu(i)
            loadd(0)
            loadd(1)
            uc = 0
            groups = GROUPS[:ngroups]

            def prep(gi):
                t0, ng = groups[gi]
                b = gi % 2
                pl.load_x(l, u, t0, ng, b, from_xs=True)
                lnm.prep([(pl.tacc[b][:, i, :], pl.tacc_T[b][i]) for i in range(ng // 128)])

            def emitT(gi):
                t0, ng = groups[gi]
                lnm.emitT(ng // 128, row_of(u, t0), h2, h2_T, 0)

            prep(0)
            emitT(0)
            for gi in range(ngroups):
                t0, ng = groups[gi]
                b = gi % 2
                for sp_ in range(NSPLIT):
                    last = sp_ == NSPLIT - 1
                    if last and gi + 1 < ngroups:
                        prep(gi + 1)
                    for j in range(HC // 2):
                        if ui + NU - 1 < len(ul):
                            loadu(ui + NU - 1)
                        w = wu[ui % NU]; w_T = wu_T[ui % NU]
                        ui += 1
                        for c in range(2):
                            hc = j * 2 + c
                            p = pu[uc % 2]; p_T = pu_T[uc % 2]
                            r_ = rl[0]; r_T = rl_T[0]
                            uc += 1
                            for kc in range(KC):
                                K.mm(p_T, p[:, 0:ng], w[:, kc, c * 128:(c + 1) * 128], h2[:, kc, 0:ng], [w_T, h2_T],
                                     start=(kc == 0), stop=(kc == KC - 1))
                            K.actf(r_[:, 0:ng], p[:, 0:ng], AF.Relu, [p_T], [r_T])
                            K.op(dve, lambda: V.tensor_tensor(uT[:, hc, 0:ng], r_[:, 0:ng], r_[:, 0:ng], ALU.mult), reads=[r_T], writes=[uT_T])
                            pl.step_finish()
                    if last and gi + 1 < ngroups:
                        emitT(gi + 1)
                    for n in range(16):
                        if di + 2 < len(dl):
                            loadd(di + 2)
                        w = wd[di % ND]; w_T = wd_T[di % ND]
                        di += 1
                        mms = [(w[:, kc, :], uT[:, kc, 0:ng], [w_T, uT_T]) for kc in range(HC)]
                        pl.chunk(n, ng, row_of(u, t0), mms, sp_ == 0, b)

                def store(i, xa, x_T, t0=t0):
                    d = dst_rows(l, u, t0, i, 1)
                    if d is None:
                        return
                    if d[1] is None:
                        K.dma(sp, d[0], xa, [x_T], [OUT_T], store_from=x_T)
                    else:
                        K.dma(sp, d[0], xa, [x_T], [d[1]], store_from=x_T)
                pl.finish(ng, store, b)
            pl.drain()
            K.barrier()

    done = False
    for l in range(nlayers):
        for u in units:
            phase_in_proj(l, u)
            if stop_after in ("A", "B1", "B"):
                done = True
                break
            phase_gqa(l, u)
            phase_na(l, u)
            if stop_after == "D":
                done = True
                break
            phase_merge(l, u)
            if stop_after == "E":
                done = True
                break
            phase_outproj(l, u)
            if stop_after == "F":
                done = True
                break
            phase_mlp(l, u)
            if stop_after == "H":
                done = True
                break
        if done:
            break
    K.barrier()
    K.stack.close()
    return nc


def _rope_tables():
    t = np.arange(S)
    row = (t // 64).astype(np.float32)
    col = (t % 64).astype(np.float32)
    freqs = (np.float32(10000.0) ** (-np.arange(0, 64, 2, dtype=np.float32) / np.float32(64))).astype(np.float32)
    ar = row[:, None] * freqs
    ac = col[:, None] * freqs
    cr, sr, cc, sc = np.cos(ar), np.sin(ar), np.cos(ac), np.sin(ac)
    cos = np.ones((NT, 128), np.float32)
    sin = np.zeros((NT, 128), np.float32)
    cos[:S] = np.concatenate([cr, cr, cc, cc], axis=1)
    sin[:S] = np.concatenate([-sr, sr, -sc, sc], axis=1)
    return cos.astype(np.float32), sin.astype(np.float32)


def _bias_tables(rpb):
    rows = 32
    r = np.arange(rows)
    row_start = np.clip(r - 4, 0, rows - 8)
    col = np.arange(64)
    col_start = np.clip(col - 8, 0, 48)
    in_win = (col[None, :] >= col_start[:, None]) & (col[None, :] < col_start[:, None] + 16)
    dc_idx = np.clip(col[None, :] - col[:, None], -15, 15) + 15

    def tile(j, c):
        kr = np.repeat(np.array([2 * c, 2 * c + 1]), 64)
        kc_ = np.tile(col, 2)
        qr = np.repeat(np.array([2 * j, 2 * j + 1]), 64)
        qc = np.tile(col, 2)
        valid = (kr[:, None] >= row_start[qr][None, :]) & (kr[:, None] < row_start[qr][None, :] + 8) & in_win[qc[None, :], kc_[:, None]]
        dr = np.clip(kr[:, None] - qr[None, :] + 7, 0, 14)
        dc = dc_idx[qc[None, :], kc_[:, None]]
        return valid, dr, dc

    specs = [(5, 5 + d) for d in (-2, -1, 0, 1, 2)]
    for j, ks in ((0, [0, 1, 2, 3]), (1, [0, 1, 2, 3]), (14, [12, 13, 14, 15]), (15, [12, 13, 14, 15])):
        specs += [(j, c) for c in ks]
    out = np.empty((rpb.shape[0], 8, 128, 21 * 128), np.float32)
    for i, (j, c) in enumerate(specs):
        valid, dr, dc = tile(j, c)
        g = rpb[:, :, dr, dc]
        out[:, :, :, i * 128:(i + 1) * 128] = np.where(valid[None, None], g, np.float32(NEG))
    return out


_NC_CACHE = {}


def kernel(x, c, ctx, c_ctx, w_mod, b_mod, w_in, rpb, q_gain, k_gain, conv_w,
           w_branch, w_o, w_up, w_down, ln_g, ln_b):
    f = lambda a: np.ascontiguousarray(np.asarray(a, dtype=np.float32))
    x, c, ctx, c_ctx = f(x), f(c), f(ctx), f(c_ctx)
    shared = dict(w_mod=f(w_mod), b_mod=f(b_mod), w_in=f(w_in), btab=_bias_tables(f(rpb)), q_gain=f(q_gain), k_gain=f(k_gain),
                  conv_w=f(conv_w), w_branch=f(w_branch), w_o=f(w_o), w_up=f(w_up), w_down=f(w_down), ln_g=f(ln_g), ln_b=f(ln_b),
                  ident=np.eye(128, dtype=np.float32))
    shared["rope_cos"], shared["rope_sin"] = _rope_tables()
    if "nc" not in _NC_CACHE:
        _NC_CACHE["nc"] = build_program()
    nc = _NC_CACHE["nc"]
    in_maps = []
    for i in range(8):
        m = dict(shared)
        m["x"] = np.ascontiguousarray(x[2 * i:2 * i + 2])
        m["ctx"] = np.ascontiguousarray(ctx[2 * i:2 * i + 2])
        m["cvec"] = np.ascontiguousarray(np.stack([c[2 * i], c[2 * i + 1], c_ctx, c_ctx]))
        in_maps.append(m)
    res = run_bass_kernel_spmd(nc, in_maps, core_ids=list(range(8)))
    return np.concatenate([r["y"] for r in res.results], axis=0).astype(np.float32)
```

```python
import contextlib
import numpy as np
import concourse.bass as bass
import concourse.mybir as mybir
from concourse.bass_utils import run_bass_kernel_spmd

F32 = mybir.dt.float32
BF16 = mybir.dt.bfloat16
AF = mybir.ActivationFunctionType
ALU = mybir.AluOpType
AX = mybir.AxisListType

D = 2048
KC = 16
S = 2048
L = 256
NT = S + L
NTILE = NT // 128
DEPTH = 2
IN_W = 13824
QA0, QB0, KA0, VA0, KB0, VB0, CB0, CC0, CX0, G0 = 0, 1024, 2048, 3072, 4096, 4352, 4608, 5632, 6656, 7680
HID = 8192
EPS = 1e-6
SCALE = 128 ** -0.5
ALPHA = (2 * DEPTH) ** 0.25
GROUPS = [(0, 512), (512, 512), (1024, 512), (1536, 512), (2048, 256)]
NEG = -30000.0
NFILL = 1


class Sem:
    def __init__(self, handle):
        self.handle = handle
        self.val = 0


class Eng:
    def __init__(self, obj, kind, sem=None):
        self.obj = obj
        self.kind = kind
        self.sem = sem
        self.count = 0
        self.waited = {}


class T:
    __slots__ = ("ap", "last_w", "reads", "lsem", "ssem")

    def __init__(self, ap=None):
        self.ap = ap
        self.last_w = None
        self.reads = {}
        self.lsem = None
        self.ssem = None


class KB:
    def __init__(self, nc, nsem=100):
        self.nc = nc
        self.stack = contextlib.ExitStack()
        self.free_sems = []
        self.all_sems = []
        for i in range(nsem):
            s = Sem(self.stack.enter_context(nc.semaphore(f"s{i}")))
            self.free_sems.append(s)
            self.all_sems.append(s)
        mk = lambda obj, kind: Eng(obj, kind, self.free_sems.pop())
        self.pe = mk(nc.tensor, "pe")
        self.act = mk(nc.scalar, "act")
        self.dve = mk(nc.vector, "dve")
        self.pool = mk(nc.gpsimd, "pool")
        self.sp = Eng(nc.sync, "sp", None)
        self.engs = [self.pe, self.act, self.dve, self.pool, self.sp]
        self.phase_sems = []
        self.uid = 0

    def name(self, p):
        self.uid += 1
        return f"{p}{self.uid}"

    def sb(self, st, shape, dt, name="t"):
        return st.enter_context(self.nc.sbuf_tensor(self.name(name), list(shape), dt))

    def ps(self, st, shape, dt, name="p"):
        return st.enter_context(self.nc.psum_tensor(self.name(name), list(shape), dt))

    def getsem(self):
        s = self.free_sems.pop()
        self.phase_sems.append(s)
        return s

    def _wait(self, eng, deps):
        for sem, val in deps:
            if eng.waited.get(sem, 0) >= val:
                continue
            eng.obj.wait_ge(sem.handle, val)
            eng.waited[sem] = val

    def _deps(self, eng, reads, writes):
        deps = {}

        def add(tok, skip_same):
            if tok is None:
                return
            sem, val = tok
            if eng.sem is not None and sem is eng.sem and (skip_same or eng.kind == "pe"):
                return
            if deps.get(sem, 0) < val:
                deps[sem] = val

        for t in reads:
            add(t.last_w, False)
        for t in writes:
            add(t.last_w, True)
            for s, v in t.reads.items():
                add((s, v), True)
        return deps

    def op(self, eng, fn, reads=(), writes=()):
        deps = self._deps(eng, reads, writes)
        self._wait(eng, deps.items())
        ins = fn()
        eng.count += 1
        ins.then_inc(eng.sem.handle, 1)
        tok = (eng.sem, eng.count)
        for t in reads:
            t.reads[eng.sem] = eng.count
        for t in writes:
            t.last_w = tok
            t.reads = {}
        return ins

    def dma(self, q, out_ap, in_ap, reads, writes, load_into=None, store_from=None, **kw):
        if load_into is not None:
            if load_into.lsem is None:
                load_into.lsem = self.getsem()
            sem = load_into.lsem
        else:
            if store_from.ssem is None:
                store_from.ssem = self.getsem()
            sem = store_from.ssem
        deps = self._deps(q, reads, writes)
        if sem.val > 0 and deps.get(sem, 0) < sem.val:
            deps[sem] = sem.val
        self._wait(q, deps.items())
        ins = q.obj.dma_start(out=out_ap, in_=in_ap, **kw)
        sem.val += 16
        ins.then_inc(sem.handle, 16)
        tok = (sem, sem.val)
        for t in reads:
            t.reads[sem] = sem.val
        for t in writes:
            t.last_w = tok
            t.reads = {}

    def barrier(self, release=True):
        toks = [(e.sem, e.count) for e in self.engs if e.sem is not None and e.count > 0]
        toks += [(s, s.val) for s in self.all_sems if s.val > 0 and all(s is not e.sem for e in self.engs)]
        for e in self.engs:
            self._wait(e, [t for t in toks if t[0] is not e.sem])
        if release:
            self.free_sems.extend(self.phase_sems)
            self.phase_sems = []

    def mm(self, out_t, out_ap, lhsT_ap, rhs_ap, reads, start=True, stop=True):
        return self.op(self.pe, lambda: self.nc.tensor.matmul(out_ap, lhsT_ap, rhs_ap, start=start, stop=stop),
                       reads=reads, writes=[out_t])

    def tr(self, out_t, out_ap, in_ap, ident_ap, reads):
        return self.op(self.pe, lambda: self.nc.tensor.transpose(out_ap, in_ap, ident_ap), reads=reads, writes=[out_t])

    def actf(self, out_ap, in_ap, func, reads, writes, **kw):
        return self.op(self.act, lambda: self.nc.scalar.activation(out_ap, in_ap, func, **kw), reads=reads, writes=writes)


def build_program(debug=False, stop_after=None, nlayers=DEPTH, units=(0, 1)):
    nc = bass.Bass("TRN2", target_bir_lowering=False)
    K = KB(nc)
    pe, act, dve, pool, sp = K.pe, K.act, K.dve, K.pool, K.sp
    V = nc.vector
    dbg_kind = "ExternalOutput" if debug else "Internal"

    def din(name, shape, dt=F32):
        return nc.dram_tensor(name, list(shape), dt, kind="ExternalInput").ap()

    def dscr(name, shape, dt):
        return nc.dram_tensor(name, list(shape), dt, kind=dbg_kind).ap()

    x_in = din("x", [2, S, D])
    ctx_in = din("ctx", [2, L, D])
    cvec = din("cvec", [4, D])
    w_mod = din("w_mod", [DEPTH, D, 6 * D])
    b_mod = din("b_mod", [DEPTH, 6 * D])
    w_in = din("w_in", [DEPTH, D, IN_W])
    btab = din("btab", [DEPTH, 8, 128, 21 * 128])
    q_gain = din("q_gain", [DEPTH, 128])
    k_gain = din("k_gain", [DEPTH, 128])
    conv_w = din("conv_w", [DEPTH, 3, 1024])
    w_branch = din("w_branch", [DEPTH, 3, 1024, D])
    w_o = din("w_o", [DEPTH, D, D])
    w_up = din("w_up", [DEPTH, D, HID])
    w_down = din("w_down", [DEPTH, HID, D])
    ln_g = din("ln_g", [DEPTH, 2, D])
    ln_b = din("ln_b", [DEPTH, 2, D])
    rope_cos = din("rope_cos", [NT, 128])
    rope_sin = din("rope_sin", [NT, 128])
    ident_in = din("ident", [128, 128])
    y_out = nc.dram_tensor("y", [2, S, D], F32, kind="ExternalOutput").ap()

    xs = dscr("xs", [2, NT, D], F32)
    qaT = dscr("qaT", [8, 128, NT], BF16)
    kaT = dscr("kaT", [8, 128, NT], BF16)
    va = dscr("va", [NT, 1024], BF16)
    qbT = dscr("qbT", [8, 128, NT], BF16)
    kbT = dscr("kbT", [2, 128, NT], BF16)
    vb = dscr("vb", [NT, 256], BF16)
    ycT = dscr("ycT", [8, 128, NT], BF16)
    gT = dscr("gT", [48, 128, NT], BF16)
    onT = dscr("onT", [8, 128, NT], BF16)
    ogT = dscr("ogT", [8, 128, NT], BF16)
    mT = dscr("mT", [16, 128, NT], BF16)

    xs_T = [[T() for _ in GROUPS] for _ in range(2)]
    qaT_T = [T() for _ in range(8)]
    kaT_T = [T() for _ in range(8)]
    va_T = [T() for _ in range(NTILE)]
    qbT_T = [T() for _ in range(2)]
    kbT_T = T()
    vb_T = [T() for _ in range(NTILE)]
    ycT_T = [T() for _ in range(8)]
    gT_T = [T() for _ in range(48)]
    onT_T = [T() for _ in range(8)]
    ogT_T = [T() for _ in range(8)]
    mT_T = [T() for _ in range(16)]
    DIN = T()

    gst = K.stack
    ident_f = K.sb(gst, [128, 128], F32, "identf"); ident_f_T = T()
    ident_b = K.sb(gst, [128, 128], BF16, "identb"); ident_b_T = T()
    ones_b = K.sb(gst, [128, 128], BF16, "onesb"); ones_T = T()
    modT = [K.sb(gst, [128, 96, 4], F32, f"modT{l}") for l in range(DEPTH)]
    modT_T = [T() for _ in range(DEPTH)]

    K.dma(sp, ident_f[:], ident_in[:, :], [DIN], [ident_f_T], load_into=ident_f_T)
    K.dma(pool, ident_b[:], ident_in[:, :], [DIN], [ident_b_T], load_into=ident_b_T)
    K.op(dve, lambda: V.memset(ones_b[:], 1.0), writes=[ones_T])
    ones_f = K.sb(gst, [128, 128], F32, "onesf")
    K.op(dve, lambda: V.memset(ones_f[:], 1.0), writes=[ones_T])
    eps_t = K.sb(gst, [128, 1], F32, "eps"); eps_T = T()
    K.op(dve, lambda: V.memset(eps_t[:], EPS), writes=[eps_T])

    cact = K.sb(gst, [128, 64], BF16, "cact"); cact_T = T()

    def phase_cact():
        with contextlib.ExitStack() as st:
            cv = K.sb(st, [64, 128], F32); cv_T = T()
            psc_full = K.ps(st, [128, 512], F32); psc = psc_full[:, 0:64]; psc_T = T()
            K.dma(sp, cv[:], cvec.rearrange("r (kc p) -> (r kc) p", p=128), [DIN], [cv_T], load_into=cv_T)
            K.tr(psc_T, psc, cv[:], ident_f[0:64, 0:64], [cv_T, ident_f_T])
            K.actf(cact[:], psc, AF.Silu, [psc_T], [cact_T])
            K.barrier()

    def mod_gen(l, st, NB):
        cact_v = cact[:].rearrange("p (r kc) -> p kc r", kc=16)
        wm = [K.sb(st, [128, 16, 256], BF16) for _ in range(NB)]
        wm_T = [T() for _ in range(NB)]
        psm = K.ps(st, [128, 128, 4], F32); psm_T = T()
        psb_full = K.ps(st, [128, 512], F32); psb = psb_full[:, 0:96]; psb_T = T()
        brow = K.sb(st, [96, 128], F32); brow_T = T()
        bsb = K.sb(st, [128, 96], F32); bsb_T = T()
        wmv = w_mod[l].rearrange("(kc p) n -> p kc n", p=128)

        def load(i):
            K.dma(pool, wm[i % NB][:], wmv[:, :, i * 256:(i + 1) * 256], [DIN], [wm_T[i % NB]], load_into=wm_T[i % NB])

        for i in range(NB - 1):
            load(i)
        for b in range(48):
            if b + NB - 1 < 48:
                load(b + NB - 1)
            for c in range(2):
                n = b * 2 + c
                for kc in range(16):
                    K.mm(psm_T, psm[:, n, :], wm[b % NB][:, kc, c * 128:(c + 1) * 128], cact_v[:, kc, :],
                         [wm_T[b % NB], cact_T], start=(kc == 0), stop=(kc == 15))
            if b == 47:
                K.dma(sp, brow[:], b_mod[l].rearrange("(n p) -> n p", p=128), [DIN], [brow_T], load_into=brow_T)
                K.tr(psb_T, psb, brow[:], ident_f[0:96, 0:96], [brow_T, ident_f_T])
                K.op(dve, lambda: V.tensor_copy(bsb[:], psb), reads=[psb_T], writes=[bsb_T])
                for r in range(4):
                    K.op(dve, lambda r=r: V.tensor_tensor(modT[l][:, :, r], psm[:, 0:96, r], bsb[:], ALU.add),
                         reads=[psm_T, bsb_T], writes=[modT_T[l]])
                for p0 in (16, 64):
                    K.op(dve, lambda p0=p0: V.tensor_scalar_add(modT[l][:, p0:p0 + 16, :], modT[l][:, p0:p0 + 16, :], 1.0),
                         reads=[modT_T[l]], writes=[modT_T[l]])
            yield b

    def phase_mod(l):
        with contextlib.ExitStack() as st:
            for _ in mod_gen(l, st, 6):
                pass
            K.barrier()

    phase_cact()
    phase_mod(0)
    mod_pending = list(range(1, nlayers))


    def row_of(u, t0):
        return u if t0 < S else 2

    def QG(l):
        return GROUPS if l < nlayers - 1 else GROUPS[:4]

    def ln_stats(st_tiles, x_ap, x_T):
        bst, mv, rs, nm, sT = st_tiles
        xv = x_ap.rearrange("p (c f) -> p c f", f=512)
        for c in range(4):
            K.op(dve, lambda c=c: V.bn_stats(bst[:, c, :], xv[:, c, :]), reads=[x_T], writes=[sT])
        K.op(dve, lambda: V.bn_aggr(mv[:], bst[:].rearrange("p c f -> p (c f)")), reads=[sT], writes=[sT])
        K.actf(rs[:], mv[:, 1:2], AF.Sqrt, [sT, eps_T], [sT], bias=eps_t[:], scale=1.0)
        K.op(dve, lambda: V.reciprocal(rs[:], rs[:]), reads=[sT], writes=[sT])
        K.op(dve, lambda: V.scalar_tensor_tensor(nm[:], mv[:, 0:1], -1.0, rs[:], ALU.mult, ALU.mult), reads=[sT], writes=[sT])
        return rs, nm

    def mk_stats(st):
        return (K.sb(st, [128, 4, 6], F32), K.sb(st, [128, 2], F32), K.sb(st, [128, 1], F32), K.sb(st, [128, 1], F32), T())

    class LNMod:
        def __init__(self, st, l, part):
            self.l = l
            self.part = part
            self.xn = [K.sb(st, [128, D], BF16) for _ in range(4)]
            self.xn_T = [T() for _ in range(4)]
            self.stt = [mk_stats(st) for _ in range(2)]
            self.pT = [K.ps(st, [128, 2, 512], BF16) for _ in range(2)]
            self.pT_T = [T() for _ in range(2)]
            self.cnt = 0
            self.ev = 0

        def run(self, tiles, row, hT, hT_T, col0):
            self.prep(tiles)
            self.emitT(len(tiles), row, hT, hT_T, col0)

        def prep(self, tiles):
            for i, (x_ap, x_T) in enumerate(tiles):
                rs, nm = ln_stats(self.stt[self.cnt % 2], x_ap, x_T)
                sT = self.stt[self.cnt % 2][4]
                self.cnt += 1
                K.actf(self.xn[i][:], x_ap, AF.Identity, [x_T, sT], [self.xn_T[i]], scale=rs[:], bias=nm[:])

        def emitT(self, n, row, hT, hT_T, col0):
            l = self.l
            for kp in range(8):
                pt = self.pT[kp % 2]; pt_T = self.pT_T[kp % 2]
                for k2 in range(2):
                    kc = kp * 2 + k2
                    for i in range(n):
                        K.tr(pt_T, pt[:, k2, i * 128:(i + 1) * 128], self.xn[i][:, kc * 128:(kc + 1) * 128], ident_b[:],
                             [self.xn_T[i], ident_b_T])
                for k2 in range(2):
                    kc = kp * 2 + k2
                    sc = modT[l][:, (self.part + 1) * 16 + kc, row:row + 1]
                    sh = modT[l][:, self.part * 16 + kc, row:row + 1]
                    o = hT[:, kc, col0:col0 + n * 128]
                    if kp % 2 == 0:
                        K.op(dve, lambda o=o, pt=pt, k2=k2, sc=sc, sh=sh: V.tensor_scalar(o, pt[:, k2, 0:n * 128], sc, sh, ALU.mult, ALU.add),
                             reads=[pt_T, modT_T[l]], writes=[hT_T])
                    else:
                        K.actf(o, pt[:, k2, 0:n * 128], AF.Identity, [pt_T, modT_T[l]], [hT_T], scale=sc, bias=sh)
                    self.ev += 1

    def src_rows(l, u, t0, n):
        if l == 0:
            if t0 < S:
                return x_in[u, t0:t0 + n, :], DIN
            return ctx_in[u, t0 - S:t0 - S + n, :], DIN
        g = [i for i, (a, b) in enumerate(GROUPS) if a <= t0 < a + b][0]
        return xs[u, t0:t0 + n, :], xs_T[u][g]

    def phase_in_proj(l, u):
        with contextlib.ExitStack() as st:
            hT = K.sb(st, [128, KC, NT], BF16, "hT"); hT_T = [T() for _ in GROUPS]
            with contextlib.ExitStack() as s2:
                lnm = LNMod(s2, l, 0)
                xt = [K.sb(s2, [128, D], F32) for _ in range(8)]
                xt_T = [T() for _ in range(8)]

                def loadx(gi):
                    t0, ng = GROUPS[gi]
                    for i in range(ng // 128):
                        ap, dT = src_rows(l, u, t0 + i * 128, 128)
                        k = (gi % 2) * 4 + i
                        K.dma(sp, xt[k][:], ap, [dT], [xt_T[k]], load_into=xt_T[k])

                loadx(0)
                for gi, (t0, ng) in enumerate(GROUPS):
                    if gi + 1 < len(GROUPS):
                        loadx(gi + 1)
                    tiles = [(xt[(gi % 2) * 4 + i][:], xt_T[(gi % 2) * 4 + i]) for i in range(ng // 128)]
                    lnm.run(tiles, row_of(u, t0), hT, hT_T[gi], t0)
                K.barrier()
            if stop_after == "A":
                if debug:
                    dbg = nc.dram_tensor("dbg_hT", [128, KC, NT], BF16, kind="ExternalOutput").ap()
                    dT = T()
                    K.dma(sp, dbg[:, :, :], hT[:], hT_T, [dT], store_from=hT_T[0])
                    K.barrier()
                return
            wv = w_in[l].rearrange("(kc p) n -> p kc n", p=128)

            with contextlib.ExitStack() as s2:
                NW = 2
                wt = [K.sb(s2, [128, KC, 512], BF16) for _ in range(NW)]
                wt_T = [T() for _ in range(NW)]
                cosT = K.sb(s2, [128, NTILE, 128], F32); sinT = K.sb(s2, [128, NTILE, 128], F32); tab_T = T()
                gq = K.sb(s2, [128, 128], F32); gk = K.sb(s2, [128, 128], F32); gain_T = T()
                K.dma(sp, cosT[:], rope_cos.rearrange("(t p) d -> p t d", p=128), [DIN], [tab_T], load_into=tab_T)
                K.dma(sp, sinT[:], rope_sin.rearrange("(t p) d -> p t d", p=128), [DIN], [tab_T], load_into=tab_T)
                K.dma(sp, gq[:], q_gain[l].partition_broadcast(128), [DIN], [gain_T], load_into=gain_T)
                K.dma(sp, gk[:], k_gain[l].partition_broadcast(128), [DIN], [gain_T], load_into=gain_T)
                pz = [K.ps(s2, [128, 512], F32) for _ in range(3)]; pz_T = [T() for _ in range(3)]
                ptr = [K.ps(s2, [128, 8, 128], BF16) for _ in range(2)]; ptr_T = [T() for _ in range(2)]
                NZ = 3
                zs = [K.sb(s2, [128, 512], F32) for _ in range(NZ)]; zs_T = [T() for _ in range(NZ)]
                sq = [K.sb(s2, [128, 512], F32) for _ in range(2)]; sq_T = [T() for _ in range(2)]
                ss = [K.sb(s2, [128, 4], F32) for _ in range(NZ)]
                t1 = [K.sb(s2, [128, 512], F32) for _ in range(2)]; t1_T = [T() for _ in range(2)]
                t2 = [K.sb(s2, [128, 512], F32) for _ in range(2)]; t2_T = [T() for _ in range(2)]
                qr = [K.sb(s2, [128, 512], BF16) for _ in range(NZ)]; qr_T = [T() for _ in range(NZ)]
                qst = K.sb(s2, [128, 4, NT], BF16); qst_T = T()
                vst = [K.sb(s2, [128, 512], BF16) for _ in range(3)]; vst_T = [T() for _ in range(3)]
                tmblocks = [("q", QB0, 0), ("q", QB0 + 512, 1), ("kv", KB0, 0), ("v", VA0, 0), ("v", VA0 + 512, 1)]

                def loadw(i):
                    c0 = tmblocks[i][1]
                    K.dma(pool, wt[i % NW][:], wv[:, :, c0:c0 + 512], [DIN], [wt_T[i % NW]], load_into=wt_T[i % NW])

                loadw(0)
                cnt = 0
                pend = []
                trc = [0]

                def do_tr(item):
                    q_, q_T, nh, t = item
                    pt = ptr[trc[0] % 2]; pt_T = ptr_T[trc[0] % 2]
                    trc[0] += 1
                    for h in range(nh):
                        K.tr(pt_T, pt[:, h, :], q_[:, h * 128:(h + 1) * 128], ident_b[:], [q_T, ident_b_T])
                    K.actf(qst[:, 0:nh, t * 128:(t + 1) * 128], pt[:, 0:nh, :], AF.Copy, [pt_T], [qst_T])

                for bi, (kind, c0, sub) in enumerate(tmblocks):
                    if bi + 1 < len(tmblocks):
                        loadw(bi + 1)
                    w = wt[bi % NW]; w_T = wt_T[bi % NW]
                    for t in range(NTILE if (kind != "q" or l < nlayers - 1) else 16):
                        gi = min(t // 4, 4)
                        p = pz[cnt % 3]; p_T = pz_T[cnt % 3]
                        for kc in range(KC):
                            K.mm(p_T, p[:], hT[:, kc, t * 128:(t + 1) * 128], w[:, kc, :], [hT_T[gi], w_T],
                                 start=(kc == 0), stop=(kc == KC - 1))
                        if kind == "v":
                            v_ = vst[cnt % 3]; v_T = vst_T[cnt % 3]
                            K.actf(v_[:], p[:], AF.Copy, [p_T], [v_T])
                            K.dma(sp, va[t * 128:(t + 1) * 128, sub * 512:(sub + 1) * 512], v_[:], [v_T], [va_T[t]], store_from=v_T)
                        else:
                            nh = 4 if kind == "q" else 2
                            W_ = nh * 128
                            gain = gq if kind == "q" else gk
                            z = zs[cnt % NZ]; z_T = zs_T[cnt % NZ]
                            s_ = ss[cnt % NZ]
                            q_ = qr[cnt % NZ]; q_T = qr_T[cnt % NZ]
                            sq_ = sq[cnt % 2]; sq__T = sq_T[cnt % 2]
                            t1_ = t1[cnt % 2]; t1__T = t1_T[cnt % 2]
                            t2_ = t2[cnt % 2]; t2__T = t2_T[cnt % 2]
                            K.actf(z[:, 0:W_], p[:, 0:W_], AF.Copy, [p_T], [z_T])
                            if kind == "kv":
                                v_ = vst[cnt % 3]; v_T = vst_T[cnt % 3]
                                K.actf(v_[:, 0:256], p[:, 256:512], AF.Copy, [p_T], [v_T])
                                K.dma(sp, vb[t * 128:(t + 1) * 128, :], v_[:, 0:256], [v_T], [vb_T[t]], store_from=v_T)
                            z3 = z[:, 0:W_].rearrange("p (h d) -> p h d", d=128)
                            K.op(pool, lambda: nc.gpsimd.tensor_tensor(sq_[:, 0:W_], z[:, 0:W_], z[:, 0:W_], ALU.mult), reads=[z_T], writes=[sq__T])
                            K.op(dve, lambda: V.tensor_reduce(s_[:, 0:nh], sq_[:, 0:W_].rearrange("p (h d) -> p h d", d=128), AX.X, ALU.add),
                                 reads=[sq__T], writes=[z_T])
                            K.actf(s_[:, 0:nh], s_[:, 0:nh], AF.Sqrt, [z_T, eps_T], [z_T], bias=eps_t[:], scale=1.0 / 128)
                            K.op(dve, lambda: V.reciprocal(s_[:, 0:nh], s_[:, 0:nh]), reads=[z_T], writes=[z_T])
                            K.op(dve, lambda: V.tensor_tensor(z3, z3, s_[:, 0:nh].unsqueeze(2).to_broadcast([128, nh, 128]), ALU.mult), reads=[z_T], writes=[z_T])
                            K.op(dve, lambda: V.tensor_tensor(z3, z3, gain[:].unsqueeze(1).to_broadcast([128, nh, 128]), ALU.mult), reads=[z_T, gain_T], writes=[z_T])
                            K.op(dve, lambda: V.tensor_tensor(t1_[:, 0:W_].rearrange("p (h d) -> p h d", d=128), z3,
                                                              cosT[:, t, :].unsqueeze(1).to_broadcast([128, nh, 128]), ALU.mult),
                                 reads=[z_T, tab_T], writes=[t1__T])
                            zv = z[:, 0:W_].rearrange("p (h a b c) -> p h a b c", a=2, b=2, c=32)
                            sv = sinT[:, t, :].rearrange("p (a b c) -> p a b c", a=2, b=2)
                            tv = t2_[:, 0:W_].rearrange("p (h a b c) -> p h a b c", a=2, b=2, c=32)
                            K.op(dve, lambda: V.tensor_tensor(tv[:, :, :, 0, :], zv[:, :, :, 1, :], sv[:, :, 0, :].unsqueeze(1).to_broadcast([128, nh, 2, 32]), ALU.mult),
                                 reads=[z_T, tab_T], writes=[t2__T])
                            K.op(dve, lambda: V.tensor_tensor(tv[:, :, :, 1, :], zv[:, :, :, 0, :], sv[:, :, 1, :].unsqueeze(1).to_broadcast([128, nh, 2, 32]), ALU.mult),
                                 reads=[z_T, tab_T], writes=[t2__T])
                            K.op(pool, lambda: nc.gpsimd.tensor_tensor(q_[:, 0:W_], t1_[:, 0:W_], t2_[:, 0:W_], ALU.add), reads=[t1__T, t2__T], writes=[q_T])
                            pend.append((q_, q_T, nh, t))
                            if len(pend) > 2:
                                do_tr(pend.pop(0))
                        cnt += 1
                    while pend:
                        do_tr(pend.pop(0))
                    if kind == "q":
                        K.dma(sp, qbT[sub * 4:(sub + 1) * 4].rearrange("h p t -> p h t"), qst[:], [qst_T], [qbT_T[sub]], store_from=qst_T)
                    elif kind == "kv":
                        K.dma(sp, kbT.rearrange("h p t -> p h t"), qst[:, 0:2, :], [qst_T], [kbT_T], store_from=qst_T)
                K.barrier()
            if stop_after == "B1":
                return

            with contextlib.ExitStack() as s2:
                NW = 4
                wt = [K.sb(s2, [128, KC, 256], BF16) for _ in range(NW)]
                wt_T = [T() for _ in range(NW)]
                pz = [K.ps(s2, [128, 512], F32) for _ in range(6)]; pz_T = [T() for _ in range(6)]
                stg = [K.sb(s2, [128, NT], BF16) for _ in range(3)]; stg_T = [T() for _ in range(3)]
                ccs = K.sb(s2, [128, NT], F32); ccs_T = T()
                vv = K.sb(s2, [128, NT], F32); vv_T = T()
                acc = K.sb(s2, [128, NT], F32); acc_T = T()
                cwr = K.sb(s2, [24, 128], F32); cwr_T = T()
                cw = K.sb(s2, [128, 24], F32); cw_T = T()
                K.dma(sp, cwr[:], conv_w[l].rearrange("k (c p) -> (k c) p", p=128), [DIN], [cwr_T], load_into=cwr_T)
                K.tr(pz_T[0], pz[0][:, 0:24], cwr[:], ident_f[0:24, 0:24], [cwr_T, ident_f_T])
                K.op(dve, lambda: V.tensor_copy(cw[:], pz[0][:, 0:24]), reads=[pz_T[0]], writes=[cw_T])
                wl = []
                for h2 in range(4):
                    wl.append(("qa", QA0 + h2 * 256, h2))
                for h2 in range(4):
                    wl.append(("ka", KA0 + h2 * 256, h2))
                for c2 in range(4):
                    wl.append(("cc", CC0 + c2 * 256, c2)); wl.append(("cx", CX0 + c2 * 256, c2)); wl.append(("cb", CB0 + c2 * 256, c2))
                for g2 in range(24):
                    wl.append(("g", G0 + g2 * 256, g2))

                def loadw(i):
                    c0 = wl[i][1]
                    K.dma(pool, wt[i % NW][:], wv[:, :, c0:c0 + 256], [DIN], [wt_T[i % NW]], load_into=wt_T[i % NW])

                for i in range(NW - 1):
                    loadw(i)
                mg = None
                if mod_pending and mod_pending[0] == l + 1:
                    mg = mod_gen(mod_pending.pop(0), s2, 4)
                pc = [0]
                sc_ = [0]

                def proj_chunk(w, w_T, c, gi):
                    t0, ng = GROUPS[gi]
                    p = pz[pc[0] % 6]; p_T = pz_T[pc[0] % 6]
                    pc[0] += 1
                    for kc in range(KC):
                        K.mm(p_T, p[:, 0:ng], w[:, kc, c * 128:(c + 1) * 128], hT[:, kc, t0:t0 + ng], [w_T, hT_T[gi]],
                             start=(kc == 0), stop=(kc == KC - 1))
                    return p, p_T

                ev = [0]
                for wi, (kind, c0, idx) in enumerate(wl):
                    if wi + NW - 1 < len(wl):
                        loadw(wi + NW - 1)
                    if mg is not None and wi >= 2:
                        next(mg, None)
                        if kind == "g":
                            next(mg, None)
                    w = wt[wi % NW]; w_T = wt_T[wi % NW]
                    if kind in ("qa", "ka", "g"):
                        for c in range(2):
                            sg = stg[sc_[0] % 3]; sg_T = stg_T[sc_[0] % 3]
                            sc_[0] += 1
                            for gi, (t0, ng) in enumerate(GROUPS if kind == "ka" else QG(l)):
                                p, p_T = proj_chunk(w, w_T, c, gi)
                                if kind == "g":
                                    K.actf(sg[:, t0:t0 + ng], p[:, 0:ng], AF.Sigmoid, [p_T], [sg_T])
                                elif ev[0] % 2 == 0:
                                    K.actf(sg[:, t0:t0 + ng], p[:, 0:ng], AF.Copy, [p_T], [sg_T])
                                else:
                                    K.op(dve, lambda sg=sg, p=p, t0=t0, ng=ng: V.tensor_copy(sg[:, t0:t0 + ng], p[:, 0:ng]), reads=[p_T], writes=[sg_T])
                                ev[0] += 1
                            n = idx * 2 + c
                            dst, dT = {"qa": (qaT, qaT_T), "ka": (kaT, kaT_T), "g": (gT, gT_T)}[kind]
                            K.dma(sp, dst[n], sg[:], [sg_T], [dT[n]], store_from=sg_T)
                    elif kind == "cc":
                        wcx = wt[(wi + 1) % NW]; wcx_T = wt_T[(wi + 1) % NW]
                        wcb = wt[(wi + 2) % NW]; wcb_T = wt_T[(wi + 2) % NW]
                        for c in range(2):
                            ci = idx * 2 + c
                            for gi, (t0, ng) in enumerate(QG(l)):
                                p, p_T = proj_chunk(w, w_T, c, gi)
                                K.actf(ccs[:, t0:t0 + ng], p[:, 0:ng], AF.Copy, [p_T], [ccs_T])
                            for gi, (t0, ng) in enumerate(QG(l)):
                                p, p_T = proj_chunk(wcx, wcx_T, c, gi)
                                K.op(dve, lambda p=p, t0=t0, ng=ng: V.tensor_tensor(vv[:, t0:t0 + ng], p[:, 0:ng], ccs[:, t0:t0 + ng], ALU.mult),
                                     reads=[p_T, ccs_T], writes=[vv_T])
                            w0 = cw[:, 0 * 8 + ci:0 * 8 + ci + 1]; w1 = cw[:, 8 + ci:8 + ci + 1]; w2 = cw[:, 16 + ci:16 + ci + 1]
                            nv = NT if l < nlayers - 1 else S
                            K.actf(acc[:, 0:nv], vv[:, 0:nv], AF.Identity, [vv_T, cw_T], [acc_T], scale=w1)
                            for (a, b) in (((0, S), (S, NT)) if l < nlayers - 1 else ((0, S),)):
                                K.op(dve, lambda a=a, b=b, w0=w0: V.scalar_tensor_tensor(acc[:, a + 1:b], vv[:, a:b - 1], w0, acc[:, a + 1:b], ALU.mult, ALU.add),
                                     reads=[vv_T, acc_T, cw_T], writes=[acc_T])
                                K.op(dve, lambda a=a, b=b, w2=w2: V.scalar_tensor_tensor(acc[:, a:b - 1], vv[:, a + 1:b], w2, acc[:, a:b - 1], ALU.mult, ALU.add),
                                     reads=[vv_T, acc_T, cw_T], writes=[acc_T])
                            sg = stg[sc_[0] % 3]; sg_T = stg_T[sc_[0] % 3]
                            sc_[0] += 1
                            for gi, (t0, ng) in enumerate(QG(l)):
                                p, p_T = proj_chunk(wcb, wcb_T, c, gi)
                                K.op(dve, lambda sg=sg, p=p, t0=t0, ng=ng: V.tensor_tensor(sg[:, t0:t0 + ng], p[:, 0:ng], acc[:, t0:t0 + ng], ALU.mult),
                                     reads=[p_T, acc_T], writes=[sg_T])
                            K.dma(sp, ycT[ci], sg[:], [sg_T], [ycT_T[ci]], store_from=sg_T)
                if mg is not None:
                    for _ in mg:
                        pass
                K.barrier()

    def phase_gqa(l, u):
        with contextlib.ExitStack() as st:
            kb_s = K.sb(st, [128, 2, NT], BF16); kb_sT = T()
            vb_s = K.sb(st, [128, NTILE, 256], BF16); vb_sT = T()
            K.dma(sp, kb_s[:], kbT.rearrange("h p t -> p h t"), [kbT_T], [kb_sT], load_into=kb_sT)
            K.dma(sp, vb_s[:], vb.rearrange("(t p) c -> p t c", p=128), vb_T, [vb_sT], load_into=vb_sT)
            qT = [K.sb(st, [128, NT], BF16) for _ in range(2)]; qT_T = [T() for _ in range(2)]
            ost = [K.sb(st, [128, NT], BF16) for _ in range(2)]; ost_T = [T() for _ in range(2)]
            pS = [K.ps(st, [128, 512], F32) for _ in range(3)]; pS_T = [T() for _ in range(3)]
            pO = [K.ps(st, [128, 512], F32) for _ in range(2)]; pO_T = [T() for _ in range(2)]
            pD = [K.ps(st, [128, 512], F32) for _ in range(2)]; pD_T = [T() for _ in range(2)]
            P = [K.sb(st, [128, 512], BF16) for _ in range(3)]; P_T = [T() for _ in range(3)]
            rd = [K.sb(st, [128, 512], F32) for _ in range(2)]; rd_T = [T() for _ in range(2)]
            pF = K.ps(st, [128, 512], F32); pF_T = T()

            def loadq(h):
                K.dma(sp, qT[h % 2][:], qbT[h], [qbT_T[h // 4]], [qT_T[h % 2]], load_into=qT_T[h % 2])

            loadq(0)
            sc = 0
            oc = 0
            for h in range(8):
                if h + 1 < 8:
                    loadq(h + 1)
                g = h // 4
                q = qT[h % 2]; q_T = qT_T[h % 2]
                o_ = ost[h % 2]; o_T = ost_T[h % 2]
                for gi, (q0, nq) in enumerate(QG(l)):
                    keys = list(range(NTILE)) if q0 < S else [16, 17]
                    po = pO[oc % 2]; po_T = pO_T[oc % 2]
                    pd = pD[oc % 2]; pd_T = pD_T[oc % 2]
                    r_ = rd[oc % 2]; r_T = rd_T[oc % 2]
                    oc += 1

                    def smm(c, slot):
                        K.mm(pS_T[slot], pS[slot][:, 0:nq], kb_s[:, g, c * 128:(c + 1) * 128], q[:, q0:q0 + nq], [kb_sT, q_T])

                    smm(keys[0], sc % 3)
                    for ki, c in enumerate(keys):
                        slot = sc % 3
                        if ki + 1 < len(keys):
                            smm(keys[ki + 1], (sc + 1) % 3)
                        K.actf(P[slot][:, 0:nq], pS[slot][:, 0:nq], AF.Exp, [pS_T[slot]], [P_T[slot]], scale=SCALE)
                        K.mm(po_T, po[:, 0:nq], vb_s[:, c, g * 128:(g + 1) * 128], P[slot][:, 0:nq], [vb_sT, P_T[slot]],
                             start=(ki == 0), stop=(ki == len(keys) - 1))
                        K.mm(pd_T, pd[:, 0:nq], ones_b[:], P[slot][:, 0:nq], [ones_T, P_T[slot]],
                             start=(ki == 0), stop=(ki == len(keys) - 1))
                        for _ in range(NFILL):
                            K.mm(pF_T, pF[:, 0:128], ones_b[:], q[:, q0:q0 + 128], [ones_T, q_T])
                        sc += 1
                    K.op(dve, lambda: V.reciprocal(r_[:, 0:nq], pd[:, 0:nq]), reads=[pd_T], writes=[r_T])
                    K.op(dve, lambda: V.tensor_tensor(o_[:, q0:q0 + nq], po[:, 0:nq], r_[:, 0:nq], ALU.mult), reads=[po_T, r_T], writes=[o_T])
                K.dma(sp, ogT[h], o_[:], [o_T], [ogT_T[h]], store_from=o_T)
            K.barrier()

    def na_keys(j):
        if j == 0:
            return [0, 1, 2, 3], 5
        if j == 1:
            return [0, 1, 2, 3], 9
        if j == 14:
            return [12, 13, 14, 15], 13
        if j == 15:
            return [12, 13, 14, 15], 17
        return [j - 2, j - 1, j, j + 1, j + 2], 0

    def phase_na(l, u):
        with contextlib.ExitStack() as st:
            va_s = K.sb(st, [128, NTILE, 1024], BF16); va_sT = T()
            K.dma(sp, va_s[:], va.rearrange("(t p) c -> p t c", p=128), va_T, [va_sT], load_into=va_sT)
            kT = [K.sb(st, [128, NT], BF16) for _ in range(2)]; kT_T = [T() for _ in range(2)]
            qT = [K.sb(st, [128, NT], BF16) for _ in range(2)]; qT_T = [T() for _ in range(2)]
            bt = [K.sb(st, [128, 21 * 128], F32) for _ in range(2)]; bt_T = [T() for _ in range(2)]
            ost = [K.sb(st, [128, NT], BF16) for _ in range(2)]; ost_T = [T() for _ in range(2)]
            pS = [K.ps(st, [128, 1024], F32) for _ in range(2)]; pS_T = [T() for _ in range(2)]
            pO = [K.ps(st, [128, 512], F32) for _ in range(2)]; pO_T = [T() for _ in range(2)]
            pD = [K.ps(st, [128, 512], F32) for _ in range(2)]; pD_T = [T() for _ in range(2)]
            tmp = [K.sb(st, [128, 640], F32) for _ in range(2)]; tmp_T = [T() for _ in range(2)]
            P = [K.sb(st, [128, 896], BF16) for _ in range(2)]; P_T = [T() for _ in range(2)]
            rd = [K.sb(st, [128, 512], F32) for _ in range(2)]; rd_T = [T() for _ in range(2)]

            def loadh(h):
                K.dma(sp, kT[h % 2][:], kaT[h], [kaT_T[h]], [kT_T[h % 2]], load_into=kT_T[h % 2])
                K.dma(sp, qT[h % 2][:], qaT[h], [qaT_T[h]], [qT_T[h % 2]], load_into=qT_T[h % 2])
                K.dma(sp, bt[h % 2][:], btab[l, h], [DIN], [bt_T[h % 2]], load_into=bt_T[h % 2])

            loadh(0)
            work = []
            for h in range(8):
                items = [(j * 128, 128) + na_keys(j) for j in range(16)]
                if l < nlayers - 1:
                    items.append((S, 256, [], 0))
                for it, (q0, nq, loc, tb0) in enumerate(items):
                    work.append((h, it, q0, nq, loc, tb0))
            last_it = 16 if l < nlayers - 1 else 15

            def do_S(k):
                h, it, q0, nq, loc, tb0 = work[k]
                ps_ = pS[k % 2]; ps_T = pS_T[k % 2]
                chunks = loc + [16, 17]
                for i, c in enumerate(chunks):
                    K.mm(ps_T, ps_[:, i * nq:(i + 1) * nq], kT[h % 2][:, c * 128:(c + 1) * 128], qT[h % 2][:, q0:q0 + nq], [kT_T[h % 2], qT_T[h % 2]])

            oc = 0
            loadh(1)
            do_S(0)
            for k, (h, it, q0, nq, loc, tb0) in enumerate(work):
                if k + 1 < len(work):
                    do_S(k + 1)
                b_ = bt[h % 2]; b_T = bt_T[h % 2]
                o_ = ost[h % 2]; o_T = ost_T[h % 2]
                slot4 = it % 4 if it < 16 else 0
                if (it < 16 and slot4 == 0) or it == 16:
                    po = pO[oc % 2]; po_T = pO_T[oc % 2]
                    pd = pD[oc % 2]; pd_T = pD_T[oc % 2]
                    r_ = rd[oc % 2]; r_T = rd_T[oc % 2]
                    oc += 1
                ocol = slot4 * 128
                ps_ = pS[k % 2]; ps_T = pS_T[k % 2]
                tm = tmp[k % 2]; tm_T = tmp_T[k % 2]
                p_ = P[k % 2]; p_T = P_T[k % 2]
                chunks = loc + [16, 17]
                nl = len(loc)
                if nl:
                    K.op(dve, lambda: V.scalar_tensor_tensor(tm[:, 0:nl * 128], ps_[:, 0:nl * 128], SCALE, b_[:, tb0 * 128:(tb0 + nl) * 128], ALU.mult, ALU.add),
                         reads=[ps_T, b_T], writes=[tm_T])
                    K.actf(p_[:, 0:nl * 128], tm[:, 0:nl * 128], AF.Exp, [tm_T], [p_T])
                K.actf(p_[:, nl * nq:(nl + 2) * nq], ps_[:, nl * nq:(nl + 2) * nq], AF.Exp, [ps_T], [p_T], scale=SCALE)
                for i, c in enumerate(chunks):
                    K.mm(po_T, po[:, ocol:ocol + nq], va_s[:, c, h * 128:(h + 1) * 128], p_[:, i * nq:(i + 1) * nq], [va_sT, p_T],
                         start=(i == 0), stop=(i == len(chunks) - 1))
                for i, c in enumerate(chunks):
                    K.mm(pd_T, pd[:, ocol:ocol + nq], ones_b[:], p_[:, i * nq:(i + 1) * nq], [ones_T, p_T],
                         start=(i == 0), stop=(i == len(chunks) - 1))
                if it < 16 and slot4 == 3:
                    qq = q0 - 384
                    K.op(dve, lambda: V.reciprocal(r_[:], pd[:]), reads=[pd_T], writes=[r_T])
                    K.op(dve, lambda: V.tensor_tensor(o_[:, qq:qq + 512], po[:], r_[:], ALU.mult), reads=[po_T, r_T], writes=[o_T])
                elif it == 16:
                    K.op(dve, lambda: V.reciprocal(r_[:, 0:256], pd[:, 0:256]), reads=[pd_T], writes=[r_T])
                    K.op(dve, lambda: V.tensor_tensor(o_[:, S:NT], po[:, 0:256], r_[:, 0:256], ALU.mult), reads=[po_T, r_T], writes=[o_T])
                if it == last_it:
                    K.dma(sp, onT[h], o_[:], [o_T], [onT_T[h]], store_from=o_T)
                    if h + 2 < 8:
                        loadh(h + 2)
            K.barrier()

    def phase_merge(l, u):
        with contextlib.ExitStack() as st:
            srcs = []
            for (dr, dT) in ((onT, onT_T), (ogT, ogT_T), (ycT, ycT_T)):
                s_ = K.sb(st, [128, 8, NT], BF16)
                srcs.append((s_, [T() for _ in GROUPS], dr.rearrange("h p t -> p h t"), dT))
            for gi, (t0, ng) in enumerate(QG(l)):
                for (s_, s_Ts, dv, dT) in srcs:
                    K.dma(sp, s_[:, :, t0:t0 + ng], dv[:, :, t0:t0 + ng], dT, [s_Ts[gi]], load_into=s_Ts[gi])
            NW = 2
            wb = [K.sb(st, [128, 3, 8, 256], BF16) for _ in range(NW)]; wb_T = [T() for _ in range(NW)]
            gs = [K.sb(st, [128, 3, NT], BF16) for _ in range(2)]; gs_T = [T() for _ in range(2)]
            pz = [K.ps(st, [128, 512], F32) for _ in range(6)]; pz_T = [T() for _ in range(6)]
            ta = [K.sb(st, [128, 512], F32) for _ in range(2)]; ta_T = [T() for _ in range(2)]
            tb = [K.sb(st, [128, 512], F32) for _ in range(2)]; tb_T = [T() for _ in range(2)]
            stg = [K.sb(st, [128, NT], BF16) for _ in range(2)]; stg_T = [T() for _ in range(2)]
            wbv = w_branch[l].rearrange("i (kc p) n -> p i kc n", p=128)

            def loadw(i):
                K.dma(pool, wb[i % NW][:], wbv[:, :, :, i * 256:(i + 1) * 256], [DIN], [wb_T[i % NW]], load_into=wb_T[i % NW])

            def loadg(n):
                K.dma(sp, gs[n % 2][:], gT.rearrange("(i n) p t -> n p i t", i=3)[n], [gT_T[n], gT_T[16 + n], gT_T[32 + n]],
                      [gs_T[n % 2]], load_into=gs_T[n % 2])

            loadw(0)
            loadg(0)
            pc = 0
            tc = 0
            for n in range(16):
                if n % 2 == 0 and n // 2 + 1 < 8:
                    loadw(n // 2 + 1)
                if n + 1 < 16:
                    loadg(n + 1)
                w = wb[(n // 2) % NW]; w_T = wb_T[(n // 2) % NW]
                c = n % 2
                g_ = gs[n % 2]; g_T = gs_T[n % 2]
                sg = stg[n % 2]; sg_T = stg_T[n % 2]
                for gi, (t0, ng) in enumerate(QG(l)):
                    pp = []
                    for i in range(3):
                        p = pz[pc % 6]; p_T = pz_T[pc % 6]
                        pc += 1
                        for kc in range(8):
                            K.mm(p_T, p[:, 0:ng], w[:, i, kc, c * 128:(c + 1) * 128], srcs[i][0][:, kc, t0:t0 + ng], [w_T, srcs[i][1][gi]],
                                 start=(kc == 0), stop=(kc == 7))
                        pp.append((p, p_T))
                    a_ = ta[tc % 2]; a_T = ta_T[tc % 2]
                    b_ = tb[tc % 2]; b_T = tb_T[tc % 2]
                    tc += 1
                    K.op(dve, lambda: V.tensor_tensor(a_[:, 0:ng], pp[0][0][:, 0:ng], g_[:, 0, t0:t0 + ng], ALU.mult), reads=[pp[0][1], g_T], writes=[a_T])
                    K.op(dve, lambda: V.tensor_tensor(b_[:, 0:ng], pp[1][0][:, 0:ng], g_[:, 1, t0:t0 + ng], ALU.mult), reads=[pp[1][1], g_T], writes=[b_T])
                    K.op(dve, lambda: V.tensor_tensor(a_[:, 0:ng], a_[:, 0:ng], b_[:, 0:ng], ALU.add), reads=[a_T, b_T], writes=[a_T])
                    K.op(dve, lambda: V.tensor_tensor(b_[:, 0:ng], pp[2][0][:, 0:ng], g_[:, 2, t0:t0 + ng], ALU.mult), reads=[pp[2][1], g_T], writes=[b_T])
                    K.op(dve, lambda: V.tensor_tensor(sg[:, t0:t0 + ng], a_[:, 0:ng], b_[:, 0:ng], ALU.add), reads=[a_T, b_T], writes=[sg_T])
                K.dma(sp, mT[n], sg[:], [sg_T], [mT_T[n]], store_from=sg_T)
            K.barrier()

    class PostLN:
        def __init__(self, st, l, which, npz=3, ntacc=2):
            self.npz = npz
            self.l = l
            self.which = which
            self.gpart = 2 if which == 0 else 5
            self.tacc = [K.sb(st, [128, 4, D], F32, "tacc") for _ in range(ntacc)]
            self.tacc_T = [[T() for _ in range(4)] for _ in range(ntacc)]
            self.lng = K.sb(st, [128, D], F32); self.lnb = K.sb(st, [128, D], F32); self.ln_T = T()
            K.dma(sp, self.lng[:], ln_g[l, which].partition_broadcast(128), [DIN], [self.ln_T], load_into=self.ln_T)
            K.dma(sp, self.lnb[:], ln_b[l, which].partition_broadcast(128), [DIN], [self.ln_T], load_into=self.ln_T)
            self.pz = [K.ps(st, [128, 512], F32) for _ in range(npz)]; self.pz_T = [T() for _ in range(npz)]
            self.pt = [K.ps(st, [128, 8, 128], BF16) for _ in range(2)]; self.pt_T = [T() for _ in range(2)]
            self.ys = [K.sb(st, [128, 512], BF16) for _ in range(2)]; self.ys_T = [T() for _ in range(2)]
            self.stt = [mk_stats(st) for _ in range(2)]
            self.pc = 0
            self.yc = 0
            self.lc = 0
            self.pending = None
            self.fin = []
            self.after_fin = []

        def when_drained(self, fn):
            if not self.fin:
                fn()
            else:
                self.after_fin.append(fn)

        def load_x(self, l, u, t0, ng, b, from_xs=False):
            for i in range(ng // 128):
                ap, dT = src_rows(1 if from_xs else l, u, t0 + i * 128, 128)
                K.dma(sp, self.tacc[b][:, i, :], ap, [dT], [self.tacc_T[b][i]], load_into=self.tacc_T[b][i])

        def chunk(self, n, ng, row, mms, first, b):
            l = self.l
            p = self.pz[self.pc % self.npz]; p_T = self.pz_T[self.pc % self.npz]
            self.pc += 1
            for i, (lh, rh, rd_) in enumerate(mms):
                K.mm(p_T, p[:, 0:ng], lh, rh, rd_, start=(i == 0), stop=(i == len(mms) - 1))
            y_ = self.ys[self.yc % 2]; y_T = self.ys_T[self.yc % 2]
            pt = self.pt[self.yc % 2]; pt_T = self.pt_T[self.yc % 2]
            self.yc += 1
            K.actf(y_[:, 0:ng], p[:, 0:ng], AF.Identity, [p_T, modT_T[l]], [y_T], scale=modT[l][:, self.gpart * 16 + n, row:row + 1])
            self.flush()
            self.pending = (n, ng, first, y_, y_T, pt, pt_T, b)
            self.step_finish()

        def flush(self):
            if self.pending is None:
                return
            n, ng, first, y_, y_T, pt, pt_T, b = self.pending
            self.pending = None
            nt = ng // 128
            for i in range(nt):
                K.tr(pt_T, pt[:, i, :], y_[:, i * 128:(i + 1) * 128], ident_b[:], [y_T, ident_b_T])
            tv = self.tacc[b][:, 0:nt, n * 128:(n + 1) * 128]
            tT = self.tacc_T[b][0:nt]
            if first:
                K.op(dve, lambda: V.scalar_tensor_tensor(tv, tv, ALPHA, pt[:, 0:nt, :], ALU.mult, ALU.add), reads=[pt_T] + tT, writes=tT)
            else:
                K.op(dve, lambda: V.tensor_tensor(tv, tv, pt[:, 0:nt, :], ALU.add), reads=[pt_T] + tT, writes=tT)

        def _finish_tile(self, i, b, store_fn):
            xa = self.tacc[b][:, i, :]; x_T = self.tacc_T[b][i]
            stt = self.stt[self.lc % 2]
            self.lc += 1
            rs, nm = ln_stats(stt, xa, x_T)
            K.actf(xa, xa, AF.Identity, [x_T, stt[4]], [x_T], scale=rs[:], bias=nm[:])
            K.op(dve, lambda: V.tensor_tensor(xa, xa, self.lng[:], ALU.mult), reads=[x_T, self.ln_T], writes=[x_T])
            K.op(dve, lambda: V.tensor_tensor(xa, xa, self.lnb[:], ALU.add), reads=[x_T, self.ln_T], writes=[x_T])
            store_fn(i, xa, x_T)

        def finish(self, ng, store_fn, b):
            self.flush()
            for i in range(ng // 128):
                self.fin.append((i, b, store_fn))

        def step_finish(self, k=1):
            for _ in range(k):
                if self.fin:
                    self._finish_tile(*self.fin.pop(0))
            if not self.fin:
                while self.after_fin:
                    self.after_fin.pop(0)()

        def drain(self):
            self.flush()
            while self.fin:
                self._finish_tile(*self.fin.pop(0))
            while self.after_fin:
                self.after_fin.pop(0)()

    def dst_rows(l, u, t0, i, which):
        r0 = t0 + i * 128
        g = [k for k, (a, b) in enumerate(GROUPS) if a <= t0 < a + b][0]
        if l == nlayers - 1 and which == 1:
            if r0 >= S:
                return None
            return y_out[u, r0:r0 + 128, :], None
        return xs[u, r0:r0 + 128, :], xs_T[u][g]

    OUT_T = T()

    def phase_outproj(l, u):
        with contextlib.ExitStack() as st:
            wo = K.sb(st, [128, KC, D], BF16, "wo"); wo_T = [T() for _ in range(8)]
            wv = w_o[l].rearrange("(kc p) n -> p kc n", p=128)
            for c2 in range(8):
                K.dma(pool, wo[:, :, c2 * 256:(c2 + 1) * 256], wv[:, :, c2 * 256:(c2 + 1) * 256], [DIN], [wo_T[c2]], load_into=wo_T[c2])
            mTs = [K.sb(st, [128, 16, 512], BF16) for _ in range(2)]; mTs_T = [T() for _ in range(2)]
            mv = mT.rearrange("n p t -> p n t")

            def loadm(gi):
                t0, ng = GROUPS[gi]
                K.dma(sp, mTs[gi % 2][:, :, 0:ng], mv[:, :, t0:t0 + ng], mT_T, [mTs_T[gi % 2]], load_into=mTs_T[gi % 2])

            loadm(0)
            pl = PostLN(st, l, 0)
            pl.load_x(l, u, GROUPS[0][0], GROUPS[0][1], 0)
            for gi, (t0, ng) in enumerate(QG(l)):
                b = gi % 2
                if gi + 1 < len(QG(l)):
                    loadm(gi + 1)
                    pl.when_drained(lambda gi=gi: pl.load_x(l, u, GROUPS[gi + 1][0], GROUPS[gi + 1][1], (gi + 1) % 2))
                m_ = mTs[gi % 2]; m_T = mTs_T[gi % 2]
                for n in range(16):
                    mms = [(wo[:, kc, n * 128:(n + 1) * 128], m_[:, kc, 0:ng], [wo_T[n // 2], m_T]) for kc in range(KC)]
                    pl.chunk(n, ng, row_of(u, t0), mms, True, b)

                def store(i, xa, x_T, t0=t0):
                    d = dst_rows(l, u, t0, i, 0)
                    K.dma(sp, d[0], xa, [x_T], [d[1]], store_from=x_T)
                pl.finish(ng, store, b)
            pl.drain()
            K.barrier()

    def phase_mlp(l, u):
        NSPLIT = 2
        HC = 64 // NSPLIT
        with contextlib.ExitStack() as st:
            pl = PostLN(st, l, 1, npz=2)
            lnm = LNMod(st, l, 3)
            h2 = K.sb(st, [128, KC, 512], BF16); h2_T = T()
            uT = K.sb(st, [128, HC, 512], BF16); uT_T = T()
            NU = 3
            wu = [K.sb(st, [128, KC, 256], BF16) for _ in range(NU)]; wu_T = [T() for _ in range(NU)]
            ND = 3
            wd = [K.sb(st, [128, HC, 128], BF16) for _ in range(ND)]; wd_T = [T() for _ in range(ND)]
            pu = [K.ps(st, [128, 512], F32) for _ in range(2)]; pu_T = [T() for _ in range(2)]
            rl = [K.sb(st, [128, 512], F32) for _ in range(1)]; rl_T = [T() for _ in range(1)]
            wuv = w_up[l].rearrange("(kc p) n -> p kc n", p=128)
            wdv = w_down[l].rearrange("(kc p) n -> p kc n", p=128)
            ngroups = len(GROUPS) if l < nlayers - 1 else len(GROUPS) - 1
            ul = [(gi, sp_, j) for gi in range(ngroups) for sp_ in range(NSPLIT) for j in range(HC // 2)]
            dl = [(gi, sp_, j) for gi in range(ngroups) for sp_ in range(NSPLIT) for j in range(16)]

            def loadu(i):
                gi, sp_, j = ul[i]
                c0 = sp_ * HC * 128 + j * 256
                K.dma(pool, wu[i % NU][:], wuv[:, :, c0:c0 + 256], [DIN], [wu_T[i % NU]], load_into=wu_T[i % NU])

            def loadd(i):
                gi, sp_, j = dl[i]
                K.dma(pool, wd[i % ND][:], wdv[:, sp_ * HC:(sp_ + 1) * HC, j * 128:(j + 1) * 128], [DIN], [wd_T[i % ND]], load_into=wd_T[i % ND])

            ui = 0
            di = 0
            for i in range(NU - 1):
                loadu(i)
            loadd(0)
            loadd(1)
            uc = 0
            groups = GROUPS[:ngroups]

            def prep(gi):
                t0, ng = groups[gi]
                b = gi % 2
                pl.load_x(l, u, t0, ng, b, from_xs=True)
                lnm.prep([(pl.tacc[b][:, i, :], pl.tacc_T[b][i]) for i in range(ng // 128)])

            def emitT(gi):
                t0, ng = groups[gi]
                lnm.emitT(ng // 128, row_of(u, t0), h2, h2_T, 0)

            prep(0)
            emitT(0)
            for gi in range(ngroups):
                t0, ng = groups[gi]
                b = gi % 2
                for sp_ in range(NSPLIT):
                    last = sp_ == NSPLIT - 1
                    if last and gi + 1 < ngroups:
                        prep(gi + 1)
                    for j in range(HC // 2):
                        if ui + NU - 1 < len(ul):
                            loadu(ui + NU - 1)
                        w = wu[ui % NU]; w_T = wu_T[ui % NU]
                        ui += 1
                        for c in range(2):
                            hc = j * 2 + c
                            p = pu[uc % 2]; p_T = pu_T[uc % 2]
                            r_ = rl[0]; r_T = rl_T[0]
                            uc += 1
                            for kc in range(KC):
                                K.mm(p_T, p[:, 0:ng], w[:, kc, c * 128:(c + 1) * 128], h2[:, kc, 0:ng], [w_T, h2_T],
                                     start=(kc == 0), stop=(kc == KC - 1))
                            K.actf(r_[:, 0:ng], p[:, 0:ng], AF.Relu, [p_T], [r_T])
                            K.op(dve, lambda: V.tensor_tensor(uT[:, hc, 0:ng], r_[:, 0:ng], r_[:, 0:ng], ALU.mult), reads=[r_T], writes=[uT_T])
                            pl.step_finish()
                    if last and gi + 1 < ngroups:
                        emitT(gi + 1)
                    for n in range(16):
                        if di + 2 < len(dl):
                            loadd(di + 2)
                        w = wd[di % ND]; w_T = wd_T[di % ND]
                        di += 1
                        mms = [(w[:, kc, :], uT[:, kc, 0:ng], [w_T, uT_T]) for kc in range(HC)]
                        pl.chunk(n, ng, row_of(u, t0), mms, sp_ == 0, b)

                def store(i, xa, x_T, t0=t0):
                    d = dst_rows(l, u, t0, i, 1)
                    if d is None:
                        return
                    if d[1] is None:
                        K.dma(sp, d[0], xa, [x_T], [OUT_T], store_from=x_T)
                    else:
                        K.dma(sp, d[0], xa, [x_T], [d[1]], store_from=x_T)
                pl.finish(ng, store, b)
            pl.drain()
            K.barrier()

    done = False
    for l in range(nlayers):
        for u in units:
            phase_in_proj(l, u)
            if stop_after in ("A", "B1", "B"):
                done = True
                break
            phase_gqa(l, u)
            phase_na(l, u)
            if stop_after == "D":
                done = True
                break
            phase_merge(l, u)
            if stop_after == "E":
                done = True
                break
            phase_outproj(l, u)
            if stop_after == "F":
                done = True
                break
            phase_mlp(l, u)
            if stop_after == "H":
                done = True
                break
        if done:
            break
    K.barrier()
    K.stack.close()
    return nc


def _rope_tables():
    t = np.arange(S)
    row = (t // 64).astype(np.float32)
    col = (t % 64).astype(np.float32)
    freqs = (np.float32(10000.0) ** (-np.arange(0, 64, 2, dtype=np.float32) / np.float32(64))).astype(np.float32)
    ar = row[:, None] * freqs
    ac = col[:, None] * freqs
    cr, sr, cc, sc = np.cos(ar), np.sin(ar), np.cos(ac), np.sin(ac)
    cos = np.ones((NT, 128), np.float32)
    sin = np.zeros((NT, 128), np.float32)
    cos[:S] = np.concatenate([cr, cr, cc, cc], axis=1)
    sin[:S] = np.concatenate([-sr, sr, -sc, sc], axis=1)
    return cos.astype(np.float32), sin.astype(np.float32)


def _bias_tables(rpb):
    rows = 32
    r = np.arange(rows)
    row_start = np.clip(r - 4, 0, rows - 8)
    col = np.arange(64)
    col_start = np.clip(col - 8, 0, 48)
    in_win = (col[None, :] >= col_start[:, None]) & (col[None, :] < col_start[:, None] + 16)
    dc_idx = np.clip(col[None, :] - col[:, None], -15, 15) + 15

    def tile(j, c):
        kr = np.repeat(np.array([2 * c, 2 * c + 1]), 64)
        kc_ = np.tile(col, 2)
        qr = np.repeat(np.array([2 * j, 2 * j + 1]), 64)
        qc = np.tile(col, 2)
        valid = (kr[:, None] >= row_start[qr][None, :]) & (kr[:, None] < row_start[qr][None, :] + 8) & in_win[qc[None, :], kc_[:, None]]
        dr = np.clip(kr[:, None] - qr[None, :] + 7, 0, 14)
        dc = dc_idx[qc[None, :], kc_[:, None]]
        return valid, dr, dc

    specs = [(5, 5 + d) for d in (-2, -1, 0, 1, 2)]
    for j, ks in ((0, [0, 1, 2, 3]), (1, [0, 1, 2, 3]), (14, [12, 13, 14, 15]), (15, [12, 13, 14, 15])):
        specs += [(j, c) for c in ks]
    out = np.empty((rpb.shape[0], 8, 128, 21 * 128), np.float32)
    for i, (j, c) in enumerate(specs):
        valid, dr, dc = tile(j, c)
        g = rpb[:, :, dr, dc]
        out[:, :, :, i * 128:(i + 1) * 128] = np.where(valid[None, None], g, np.float32(NEG))
    return out


_NC_CACHE = {}


def kernel(x, c, ctx, c_ctx, w_mod, b_mod, w_in, rpb, q_gain, k_gain, conv_w,
           w_branch, w_o, w_up, w_down, ln_g, ln_b):
    f = lambda a: np.ascontiguousarray(np.asarray(a, dtype=np.float32))
    x, c, ctx, c_ctx = f(x), f(c), f(ctx), f(c_ctx)
    shared = dict(w_mod=f(w_mod), b_mod=f(b_mod), w_in=f(w_in), btab=_bias_tables(f(rpb)), q_gain=f(q_gain), k_gain=f(k_gain),
                  conv_w=f(conv_w), w_branch=f(w_branch), w_o=f(w_o), w_up=f(w_up), w_down=f(w_down), ln_g=f(ln_g), ln_b=f(ln_b),
                  ident=np.eye(128, dtype=np.float32))
    shared["rope_cos"], shared["rope_sin"] = _rope_tables()
    if "nc" not in _NC_CACHE:
        _NC_CACHE["nc"] = build_program()
    nc = _NC_CACHE["nc"]
    in_maps = []
    for i in range(8):
        m = dict(shared)
        m["x"] = np.ascontiguousarray(x[2 * i:2 * i + 2])
        m["ctx"] = np.ascontiguousarray(ctx[2 * i:2 * i + 2])
        m["cvec"] = np.ascontiguousarray(np.stack([c[2 * i], c[2 * i + 1], c_ctx, c_ctx]))
        in_maps.append(m)
    res = run_bass_kernel_spmd(nc, in_maps, core_ids=list(range(8)))
    return np.concatenate([r["y"] for r in res.results], axis=0).astype(np.float32)
```

```python
import contextlib
import numpy as np
import concourse.bass as bass
import concourse.mybir as mybir
from concourse.bass_utils import run_bass_kernel_spmd

F32 = mybir.dt.float32
BF16 = mybir.dt.bfloat16
AF = mybir.ActivationFunctionType
ALU = mybir.AluOpType
AX = mybir.AxisListType

D = 2048
KC = 16
S = 2048
L = 256
NT = S + L
NTILE = NT // 128
DEPTH = 2
IN_W = 13824
QA0, QB0, KA0, VA0, KB0, VB0, CB0, CC0, CX0, G0 = 0, 1024, 2048, 3072, 4096, 4352, 4608, 5632, 6656, 7680
HID = 8192
EPS = 1e-6
SCALE = 128 ** -0.5
ALPHA = (2 * DEPTH) ** 0.25
GROUPS = [(0, 512), (512, 512), (1024, 512), (1536, 512), (2048, 256)]
NEG = -30000.0
NFILL = 1


class Sem:
    def __init__(self, handle):
        self.handle = handle
        self.val = 0


class Eng:
    def __init__(self, obj, kind, sem=None):
        self.obj = obj
        self.kind = kind
        self.sem = sem
        self.count = 0
        self.waited = {}


class T:
    __slots__ = ("ap", "last_w", "reads", "lsem", "ssem")

    def __init__(self, ap=None):
        self.ap = ap
        self.last_w = None
        self.reads = {}
        self.lsem = None
        self.ssem = None


class KB:
    def __init__(self, nc, nsem=100):
        self.nc = nc
        self.stack = contextlib.ExitStack()
        self.free_sems = []
        self.all_sems = []
        for i in range(nsem):
            s = Sem(self.stack.enter_context(nc.semaphore(f"s{i}")))
            self.free_sems.append(s)
            self.all_sems.append(s)
        mk = lambda obj, kind: Eng(obj, kind, self.free_sems.pop())
        self.pe = mk(nc.tensor, "pe")
        self.act = mk(nc.scalar, "act")
        self.dve = mk(nc.vector, "dve")
        self.pool = mk(nc.gpsimd, "pool")
        self.sp = Eng(nc.sync, "sp", None)
        self.engs = [self.pe, self.act, self.dve, self.pool, self.sp]
        self.phase_sems = []
        self.uid = 0

    def name(self, p):
        self.uid += 1
        return f"{p}{self.uid}"

    def sb(self, st, shape, dt, name="t"):
        return st.enter_context(self.nc.sbuf_tensor(self.name(name), list(shape), dt))

    def ps(self, st, shape, dt, name="p"):
        return st.enter_context(self.nc.psum_tensor(self.name(name), list(shape), dt))

    def getsem(self):
        s = self.free_sems.pop()
        self.phase_sems.append(s)
        return s

    def _wait(self, eng, deps):
        for sem, val in deps:
            if eng.waited.get(sem, 0) >= val:
                continue
            eng.obj.wait_ge(sem.handle, val)
            eng.waited[sem] = val

    def _deps(self, eng, reads, writes):
        deps = {}

        def add(tok, skip_same):
            if tok is None:
                return
            sem, val = tok
            if eng.sem is not None and sem is eng.sem and (skip_same or eng.kind == "pe"):
                return
            if deps.get(sem, 0) < val:
                deps[sem] = val

        for t in reads:
            add(t.last_w, False)
        for t in writes:
            add(t.last_w, True)
            for s, v in t.reads.items():
                add((s, v), True)
        return deps

    def op(self, eng, fn, reads=(), writes=()):
        deps = self._deps(eng, reads, writes)
        self._wait(eng, deps.items())
        ins = fn()
        eng.count += 1
        ins.then_inc(eng.sem.handle, 1)
        tok = (eng.sem, eng.count)
        for t in reads:
            t.reads[eng.sem] = eng.count
        for t in writes:
            t.last_w = tok
            t.reads = {}
        return ins

    def dma(self, q, out_ap, in_ap, reads, writes, load_into=None, store_from=None, **kw):
        if load_into is not None:
            if load_into.lsem is None:
                load_into.lsem = self.getsem()
            sem = load_into.lsem
        else:
            if store_from.ssem is None:
                store_from.ssem = self.getsem()
            sem = store_from.ssem
        deps = self._deps(q, reads, writes)
        if sem.val > 0 and deps.get(sem, 0) < sem.val:
            deps[sem] = sem.val
        self._wait(q, deps.items())
        ins = q.obj.dma_start(out=out_ap, in_=in_ap, **kw)
        sem.val += 16
        ins.then_inc(sem.handle, 16)
        tok = (sem, sem.val)
        for t in reads:
            t.reads[sem] = sem.val
        for t in writes:
            t.last_w = tok
            t.reads = {}

    def barrier(self, release=True):
        toks = [(e.sem, e.count) for e in self.engs if e.sem is not None and e.count > 0]
        toks += [(s, s.val) for s in self.all_sems if s.val > 0 and all(s is not e.sem for e in self.engs)]
        for e in self.engs:
            self._wait(e, [t for t in toks if t[0] is not e.sem])
        if release:
            self.free_sems.extend(self.phase_sems)
            self.phase_sems = []

    def mm(self, out_t, out_ap, lhsT_ap, rhs_ap, reads, start=True, stop=True):
        return self.op(self.pe, lambda: self.nc.tensor.matmul(out_ap, lhsT_ap, rhs_ap, start=start, stop=stop),
                       reads=reads, writes=[out_t])

    def tr(self, out_t, out_ap, in_ap, ident_ap, reads):
        return self.op(self.pe, lambda: self.nc.tensor.transpose(out_ap, in_ap, ident_ap), reads=reads, writes=[out_t])

    def actf(self, out_ap, in_ap, func, reads, writes, **kw):
        return self.op(self.act, lambda: self.nc.scalar.activation(out_ap, in_ap, func, **kw), reads=reads, writes=writes)


def build_program(debug=False, stop_after=None, nlayers=DEPTH, units=(0, 1)):
    nc = bass.Bass("TRN2", target_bir_lowering=False)
    K = KB(nc)
    pe, act, dve, pool, sp = K.pe, K.act, K.dve, K.pool, K.sp
    V = nc.vector
    dbg_kind = "ExternalOutput" if debug else "Internal"

    def din(name, shape, dt=F32):
        return nc.dram_tensor(name, list(shape), dt, kind="ExternalInput").ap()

    def dscr(name, shape, dt):
        return nc.dram_tensor(name, list(shape), dt, kind=dbg_kind).ap()

    x_in = din("x", [2, S, D])
    ctx_in = din("ctx", [2, L, D])
    cvec = din("cvec", [4, D])
    w_mod = din("w_mod", [DEPTH, D, 6 * D])
    b_mod = din("b_mod", [DEPTH, 6 * D])
    w_in = din("w_in", [DEPTH, D, IN_W])
    btab = din("btab", [DEPTH, 8, 128, 21 * 128])
    q_gain = din("q_gain", [DEPTH, 128])
    k_gain = din("k_gain", [DEPTH, 128])
    conv_w = din("conv_w", [DEPTH, 3, 1024])
    w_branch = din("w_branch", [DEPTH, 3, 1024, D])
    w_o = din("w_o", [DEPTH, D, D])
    w_up = din("w_up", [DEPTH, D, HID])
    w_down = din("w_down", [DEPTH, HID, D])
    ln_g = din("ln_g", [DEPTH, 2, D])
    ln_b = din("ln_b", [DEPTH, 2, D])
    rope_cos = din("rope_cos", [NT, 128])
    rope_sin = din("rope_sin", [NT, 128])
    ident_in = din("ident", [128, 128])
    y_out = nc.dram_tensor("y", [2, S, D], F32, kind="ExternalOutput").ap()

    xs = dscr("xs", [2, NT, D], F32)
    qaT = dscr("qaT", [8, 128, NT], BF16)
    kaT = dscr("kaT", [8, 128, NT], BF16)
    va = dscr("va", [NT, 1024], BF16)
    qbT = dscr("qbT", [8, 128, NT], BF16)
    kbT = dscr("kbT", [2, 128, NT], BF16)
    vb = dscr("vb", [NT, 256], BF16)
    ycT = dscr("ycT", [8, 128, NT], BF16)
    gT = dscr("gT", [48, 128, NT], BF16)
    onT = dscr("onT", [8, 128, NT], BF16)
    ogT = dscr("ogT", [8, 128, NT], BF16)
    mT = dscr("mT", [16, 128, NT], BF16)

    xs_T = [[T() for _ in GROUPS] for _ in range(2)]
    qaT_T = [T() for _ in range(8)]
    kaT_T = [T() for _ in range(8)]
    va_T = [T() for _ in range(NTILE)]
    qbT_T = [T() for _ in range(2)]
    kbT_T = T()
    vb_T = [T() for _ in range(NTILE)]
    ycT_T = [T() for _ in range(8)]
    gT_T = [T() for _ in range(48)]
    onT_T = [T() for _ in range(8)]
    ogT_T = [T() for _ in range(8)]
    mT_T = [T() for _ in range(16)]
    DIN = T()

    gst = K.stack
    ident_f = K.sb(gst, [128, 128], F32, "identf"); ident_f_T = T()
    ident_b = K.sb(gst, [128, 128], BF16, "identb"); ident_b_T = T()
    ones_b = K.sb(gst, [128, 128], BF16, "onesb"); ones_T = T()
    modT = [K.sb(gst, [128, 96, 4], F32, f"modT{l}") for l in range(DEPTH)]
    modT_T = [T() for _ in range(DEPTH)]

    K.dma(sp, ident_f[:], ident_in[:, :], [DIN], [ident_f_T], load_into=ident_f_T)
    K.dma(pool, ident_b[:], ident_in[:, :], [DIN], [ident_b_T], load_into=ident_b_T)
    K.op(dve, lambda: V.memset(ones_b[:], 1.0), writes=[ones_T])
    ones_f = K.sb(gst, [128, 128], F32, "onesf")
    K.op(dve, lambda: V.memset(ones_f[:], 1.0), writes=[ones_T])
    eps_t = K.sb(gst, [128, 1], F32, "eps"); eps_T = T()
    K.op(dve, lambda: V.memset(eps_t[:], EPS), writes=[eps_T])

    cact = K.sb(gst, [128, 64], BF16, "cact"); cact_T = T()

    def phase_cact():
        with contextlib.ExitStack() as st:
            cv = K.sb(st, [64, 128], F32); cv_T = T()
            psc_full = K.ps(st, [128, 512], F32); psc = psc_full[:, 0:64]; psc_T = T()
            K.dma(sp, cv[:], cvec.rearrange("r (kc p) -> (r kc) p", p=128), [DIN], [cv_T], load_into=cv_T)
            K.tr(psc_T, psc, cv[:], ident_f[0:64, 0:64], [cv_T, ident_f_T])
            K.actf(cact[:], psc, AF.Silu, [psc_T], [cact_T])
            K.barrier()

    def mod_gen(l, st, NB):
        cact_v = cact[:].rearrange("p (r kc) -> p kc r", kc=16)
        wm = [K.sb(st, [128, 16, 256], BF16) for _ in range(NB)]
        wm_T = [T() for _ in range(NB)]
        psm = K.ps(st, [128, 128, 4], F32); psm_T = T()
        psb_full = K.ps(st, [128, 512], F32); psb = psb_full[:, 0:96]; psb_T = T()
        brow = K.sb(st, [96, 128], F32); brow_T = T()
        bsb = K.sb(st, [128, 96], F32); bsb_T = T()
        wmv = w_mod[l].rearrange("(kc p) n -> p kc n", p=128)

        def load(i):
            K.dma(pool, wm[i % NB][:], wmv[:, :, i * 256:(i + 1) * 256], [DIN], [wm_T[i % NB]], load_into=wm_T[i % NB])

        for i in range(NB - 1):
            load(i)
        for b in range(48):
            if b + NB - 1 < 48:
                load(b + NB - 1)
            for c in range(2):
                n = b * 2 + c
                for kc in range(16):
                    K.mm(psm_T, psm[:, n, :], wm[b % NB][:, kc, c * 128:(c + 1) * 128], cact_v[:, kc, :],
                         [wm_T[b % NB], cact_T], start=(kc == 0), stop=(kc == 15))
            if b == 47:
                K.dma(sp, brow[:], b_mod[l].rearrange("(n p) -> n p", p=128), [DIN], [brow_T], load_into=brow_T)
                K.tr(psb_T, psb, brow[:], ident_f[0:96, 0:96], [brow_T, ident_f_T])
                K.op(dve, lambda: V.tensor_copy(bsb[:], psb), reads=[psb_T], writes=[bsb_T])
                for r in range(4):
                    K.op(dve, lambda r=r: V.tensor_tensor(modT[l][:, :, r], psm[:, 0:96, r], bsb[:], ALU.add),
                         reads=[psm_T, bsb_T], writes=[modT_T[l]])
                for p0 in (16, 64):
                    K.op(dve, lambda p0=p0: V.tensor_scalar_add(modT[l][:, p0:p0 + 16, :], modT[l][:, p0:p0 + 16, :], 1.0),
                         reads=[modT_T[l]], writes=[modT_T[l]])
            yield b

    def phase_mod(l):
        with contextlib.ExitStack() as st:
            for _ in mod_gen(l, st, 6):
                pass
            K.barrier()

    phase_cact()
    phase_mod(0)
    mod_pending = list(range(1, nlayers))


    def row_of(u, t0):
        return u if t0 < S else 2

    def QG(l):
        return GROUPS if l < nlayers - 1 else GROUPS[:4]

    def ln_stats(st_tiles, x_ap, x_T):
        bst, mv, rs, nm, sT = st_tiles
        xv = x_ap.rearrange("p (c f) -> p c f", f=512)
        for c in range(4):
            K.op(dve, lambda c=c: V.bn_stats(bst[:, c, :], xv[:, c, :]), reads=[x_T], writes=[sT])
        K.op(dve, lambda: V.bn_aggr(mv[:], bst[:].rearrange("p c f -> p (c f)")), reads=[sT], writes=[sT])
        K.actf(rs[:], mv[:, 1:2], AF.Sqrt, [sT, eps_T], [sT], bias=eps_t[:], scale=1.0)
        K.op(dve, lambda: V.reciprocal(rs[:], rs[:]), reads=[sT], writes=[sT])
        K.op(dve, lambda: V.scalar_tensor_tensor(nm[:], mv[:, 0:1], -1.0, rs[:], ALU.mult, ALU.mult), reads=[sT], writes=[sT])
        return rs, nm

    def mk_stats(st):
        return (K.sb(st, [128, 4, 6], F32), K.sb(st, [128, 2], F32), K.sb(st, [128, 1], F32), K.sb(st, [128, 1], F32), T())

    class LNMod:
        def __init__(self, st, l, part, nset=1):
            self.l = l
            self.part = part
            self.sets = [([K.sb(st, [128, D], BF16) for _ in range(4)], [T() for _ in range(4)]) for _ in range(nset)]
            self.xn, self.xn_T = self.sets[0]
            self.stt = [mk_stats(st) for _ in range(2)]
            self.pT = [K.ps(st, [128, 2, 512], BF16) for _ in range(2)]
            self.pT_T = [T() for _ in range(2)]
            self.cnt = 0
            self.ev = 0

        def run(self, tiles, row, hT, hT_T, col0):
            self.prep(tiles)
            self.emitT(len(tiles), row, hT, hT_T, col0)

        def prep(self, tiles):
            for i, (x_ap, x_T) in enumerate(tiles):
                self.prep_tile(i, x_ap, x_T, self.xn, self.xn_T)

        def prep_tile(self, i, x_ap, x_T, xn, xn_T):
            rs, nm = ln_stats(self.stt[self.cnt % 2], x_ap, x_T)
            sT = self.stt[self.cnt % 2][4]
            self.cnt += 1
            K.actf(xn[i][:], x_ap, AF.Identity, [x_T, sT], [xn_T[i]], scale=rs[:], bias=nm[:])

        def emitT(self, n, row, hT, hT_T, col0, xset=None, hook=None):
            l = self.l
            if xset is not None:
                self.xn, self.xn_T = xset
            for kp in range(8):
                pt = self.pT[kp % 2]; pt_T = self.pT_T[kp % 2]
                for k2 in range(2):
                    kc = kp * 2 + k2
                    for i in range(n):
                        K.tr(pt_T, pt[:, k2, i * 128:(i + 1) * 128], self.xn[i][:, kc * 128:(kc + 1) * 128], ident_b[:],
                             [self.xn_T[i], ident_b_T])
                for k2 in range(2):
                    kc = kp * 2 + k2
                    sc = modT[l][:, (self.part + 1) * 16 + kc, row:row + 1]
                    sh = modT[l][:, self.part * 16 + kc, row:row + 1]
                    o = hT[:, kc, col0:col0 + n * 128]
                    if kp % 2 == 0:
                        K.op(dve, lambda o=o, pt=pt, k2=k2, sc=sc, sh=sh: V.tensor_scalar(o, pt[:, k2, 0:n * 128], sc, sh, ALU.mult, ALU.add),
                             reads=[pt_T, modT_T[l]], writes=[hT_T])
                    else:
                        K.actf(o, pt[:, k2, 0:n * 128], AF.Identity, [pt_T, modT_T[l]], [hT_T], scale=sc, bias=sh)
                    self.ev += 1
                if hook is not None:
                    hook(kp)

    def src_rows(l, u, t0, n):
        if l == 0:
            if t0 < S:
                return x_in[u, t0:t0 + n, :], DIN
            return ctx_in[u, t0 - S:t0 - S + n, :], DIN
        g = [i for i, (a, b) in enumerate(GROUPS) if a <= t0 < a + b][0]
        return xs[u, t0:t0 + n, :], xs_T[u][g]

    def phase_in_proj(l, u):
        with contextlib.ExitStack() as st:
            hT = K.sb(st, [128, KC, NT], BF16, "hT"); hT_T = [T() for _ in GROUPS]
            with contextlib.ExitStack() as s2:
                lnm = LNMod(s2, l, 0, nset=2)
                xt = [K.sb(s2, [128, D], F32) for _ in range(8)]
                xt_T = [T() for _ in range(8)]

                def loadx(gi):
                    t0, ng = GROUPS[gi]
                    for i in range(ng // 128):
                        ap, dT = src_rows(l, u, t0 + i * 128, 128)
                        k = (gi % 2) * 4 + i
                        K.dma(sp, xt[k][:], ap, [dT], [xt_T[k]], load_into=xt_T[k])

                def tiles_of(gi):
                    return [(xt[(gi % 2) * 4 + i][:], xt_T[(gi % 2) * 4 + i]) for i in range(GROUPS[gi][1] // 128)]

                loadx(0)
                loadx(1)
                for i, (ap_, T_) in enumerate(tiles_of(0)):
                    lnm.prep_tile(i, ap_, T_, *lnm.sets[0])
                for gi, (t0, ng) in enumerate(GROUPS):
                    nt = tiles_of(gi + 1) if gi + 1 < len(GROUPS) else []
                    nxt = lnm.sets[(gi + 1) % 2]
                    if gi + 2 < len(GROUPS):
                        loadx(gi + 2)

                    def hook(kp, nt=nt, nxt=nxt):
                        if kp % 2 == 1 and kp // 2 < len(nt):
                            i = kp // 2
                            lnm.prep_tile(i, nt[i][0], nt[i][1], nxt[0], nxt[1])

                    lnm.emitT(ng // 128, row_of(u, t0), hT, hT_T[gi], t0, xset=lnm.sets[gi % 2], hook=hook)
                K.barrier()
            if stop_after == "A":
                if debug:
                    dbg = nc.dram_tensor("dbg_hT", [128, KC, NT], BF16, kind="ExternalOutput").ap()
                    dT = T()
                    K.dma(sp, dbg[:, :, :], hT[:], hT_T, [dT], store_from=hT_T[0])
                    K.barrier()
                return
            wv = w_in[l].rearrange("(kc p) n -> p kc n", p=128)

            with contextlib.ExitStack() as s2:
                NW = 2
                wt = [K.sb(s2, [128, KC, 512], BF16) for _ in range(NW)]
                wt_T = [T() for _ in range(NW)]
                cosT = K.sb(s2, [128, NTILE, 128], F32); sinT = K.sb(s2, [128, NTILE, 128], F32); tab_T = T()
                gq = K.sb(s2, [128, 128], F32); gk = K.sb(s2, [128, 128], F32); gain_T = T()
                K.dma(sp, cosT[:], rope_cos.rearrange("(t p) d -> p t d", p=128), [DIN], [tab_T], load_into=tab_T)
                K.dma(sp, sinT[:], rope_sin.rearrange("(t p) d -> p t d", p=128), [DIN], [tab_T], load_into=tab_T)
                K.dma(sp, gq[:], q_gain[l].partition_broadcast(128), [DIN], [gain_T], load_into=gain_T)
                K.dma(sp, gk[:], k_gain[l].partition_broadcast(128), [DIN], [gain_T], load_into=gain_T)
                pz = [K.ps(s2, [128, 512], F32) for _ in range(3)]; pz_T = [T() for _ in range(3)]
                ptr = [K.ps(s2, [128, 8, 128], BF16) for _ in range(2)]; ptr_T = [T() for _ in range(2)]
                NZ = 3
                zs = [K.sb(s2, [128, 512], F32) for _ in range(NZ)]; zs_T = [T() for _ in range(NZ)]
                sq = [K.sb(s2, [128, 512], F32) for _ in range(2)]; sq_T = [T() for _ in range(2)]
                ss = [K.sb(s2, [128, 4], F32) for _ in range(NZ)]
                t1 = [K.sb(s2, [128, 512], F32) for _ in range(2)]; t1_T = [T() for _ in range(2)]
                t2 = [K.sb(s2, [128, 512], F32) for _ in range(2)]; t2_T = [T() for _ in range(2)]
                qr = [K.sb(s2, [128, 512], BF16) for _ in range(NZ)]; qr_T = [T() for _ in range(NZ)]
                qst = K.sb(s2, [128, 4, NT], BF16); qst_T = T()
                vst = [K.sb(s2, [128, 512], BF16) for _ in range(3)]; vst_T = [T() for _ in range(3)]
                tmblocks = [("q", QB0, 0), ("q", QB0 + 512, 1), ("kv", KB0, 0), ("v", VA0, 0), ("v", VA0 + 512, 1)]

                def loadw(i):
                    c0 = tmblocks[i][1]
                    K.dma(pool, wt[i % NW][:], wv[:, :, c0:c0 + 512], [DIN], [wt_T[i % NW]], load_into=wt_T[i % NW])

                loadw(0)
                cnt = 0
                pend = []
                trc = [0]

                def do_tr(item):
                    q_, q_T, nh, t = item
                    pt = ptr[trc[0] % 2]; pt_T = ptr_T[trc[0] % 2]
                    trc[0] += 1
                    for h in range(nh):
                        K.tr(pt_T, pt[:, h, :], q_[:, h * 128:(h + 1) * 128], ident_b[:], [q_T, ident_b_T])
                    K.actf(qst[:, 0:nh, t * 128:(t + 1) * 128], pt[:, 0:nh, :], AF.Copy, [pt_T], [qst_T])

                for bi, (kind, c0, sub) in enumerate(tmblocks):
                    if bi + 1 < len(tmblocks):
                        loadw(bi + 1)
                    w = wt[bi % NW]; w_T = wt_T[bi % NW]
                    for t in range(NTILE if (kind != "q" or l < nlayers - 1) else 16):
                        gi = min(t // 4, 4)
                        p = pz[cnt % 3]; p_T = pz_T[cnt % 3]
                        for kc in range(KC):
                            K.mm(p_T, p[:], hT[:, kc, t * 128:(t + 1) * 128], w[:, kc, :], [hT_T[gi], w_T],
                                 start=(kc == 0), stop=(kc == KC - 1))
                        if kind == "v":
                            v_ = vst[cnt % 3]; v_T = vst_T[cnt % 3]
                            K.actf(v_[:], p[:], AF.Copy, [p_T], [v_T])
                            K.dma(sp, va[t * 128:(t + 1) * 128, sub * 512:(sub + 1) * 512], v_[:], [v_T], [va_T[t]], store_from=v_T)
                        else:
                            nh = 4 if kind == "q" else 2
                            W_ = nh * 128
                            gain = gq if kind == "q" else gk
                            z = zs[cnt % NZ]; z_T = zs_T[cnt % NZ]
                            s_ = ss[cnt % NZ]
                            q_ = qr[cnt % NZ]; q_T = qr_T[cnt % NZ]
                            sq_ = sq[cnt % 2]; sq__T = sq_T[cnt % 2]
                            t1_ = t1[cnt % 2]; t1__T = t1_T[cnt % 2]
                            t2_ = t2[cnt % 2]; t2__T = t2_T[cnt % 2]
                            K.actf(z[:, 0:W_], p[:, 0:W_], AF.Copy, [p_T], [z_T])
                            if kind == "kv":
                                v_ = vst[cnt % 3]; v_T = vst_T[cnt % 3]
                                K.actf(v_[:, 0:256], p[:, 256:512], AF.Copy, [p_T], [v_T])
                                K.dma(sp, vb[t * 128:(t + 1) * 128, :], v_[:, 0:256], [v_T], [vb_T[t]], store_from=v_T)
                            z3 = z[:, 0:W_].rearrange("p (h d) -> p h d", d=128)
                            K.op(pool, lambda: nc.gpsimd.tensor_tensor(sq_[:, 0:W_], z[:, 0:W_], z[:, 0:W_], ALU.mult), reads=[z_T], writes=[sq__T])
                            K.op(dve, lambda: V.tensor_reduce(s_[:, 0:nh], sq_[:, 0:W_].rearrange("p (h d) -> p h d", d=128), AX.X, ALU.add),
                                 reads=[sq__T], writes=[z_T])
                            K.actf(s_[:, 0:nh], s_[:, 0:nh], AF.Sqrt, [z_T, eps_T], [z_T], bias=eps_t[:], scale=1.0 / 128)
                            K.op(dve, lambda: V.reciprocal(s_[:, 0:nh], s_[:, 0:nh]), reads=[z_T], writes=[z_T])
                            K.op(dve, lambda: V.tensor_tensor(z3, z3, s_[:, 0:nh].unsqueeze(2).to_broadcast([128, nh, 128]), ALU.mult), reads=[z_T], writes=[z_T])
                            K.op(dve, lambda: V.tensor_tensor(z3, z3, gain[:].unsqueeze(1).to_broadcast([128, nh, 128]), ALU.mult), reads=[z_T, gain_T], writes=[z_T])
                            K.op(dve, lambda: V.tensor_tensor(t1_[:, 0:W_].rearrange("p (h d) -> p h d", d=128), z3,
                                                              cosT[:, t, :].unsqueeze(1).to_broadcast([128, nh, 128]), ALU.mult),
                                 reads=[z_T, tab_T], writes=[t1__T])
                            zv = z[:, 0:W_].rearrange("p (h a b c) -> p h a b c", a=2, b=2, c=32)
                            sv = sinT[:, t, :].rearrange("p (a b c) -> p a b c", a=2, b=2)
                            tv = t2_[:, 0:W_].rearrange("p (h a b c) -> p h a b c", a=2, b=2, c=32)
                            K.op(dve, lambda: V.tensor_tensor(tv[:, :, :, 0, :], zv[:, :, :, 1, :], sv[:, :, 0, :].unsqueeze(1).to_broadcast([128, nh, 2, 32]), ALU.mult),
                                 reads=[z_T, tab_T], writes=[t2__T])
                            K.op(dve, lambda: V.tensor_tensor(tv[:, :, :, 1, :], zv[:, :, :, 0, :], sv[:, :, 1, :].unsqueeze(1).to_broadcast([128, nh, 2, 32]), ALU.mult),
                                 reads=[z_T, tab_T], writes=[t2__T])
                            K.op(pool, lambda: nc.gpsimd.tensor_tensor(q_[:, 0:W_], t1_[:, 0:W_], t2_[:, 0:W_], ALU.add), reads=[t1__T, t2__T], writes=[q_T])
                            pend.append((q_, q_T, nh, t))
                            if len(pend) > 2:
                                do_tr(pend.pop(0))
                        cnt += 1
                    while pend:
                        do_tr(pend.pop(0))
                    if kind == "q":
                        K.dma(sp, qbT[sub * 4:(sub + 1) * 4].rearrange("h p t -> p h t"), qst[:], [qst_T], [qbT_T[sub]], store_from=qst_T)
                    elif kind == "kv":
                        K.dma(sp, kbT.rearrange("h p t -> p h t"), qst[:, 0:2, :], [qst_T], [kbT_T], store_from=qst_T)
                K.barrier()
            if stop_after == "B1":
                return

            with contextlib.ExitStack() as s2:
                NW = 4
                wt = [K.sb(s2, [128, KC, 256], BF16) for _ in range(NW)]
                wt_T = [T() for _ in range(NW)]
                pz = [K.ps(s2, [128, 512], F32) for _ in range(6)]; pz_T = [T() for _ in range(6)]
                stg = [K.sb(s2, [128, NT], BF16) for _ in range(3)]; stg_T = [T() for _ in range(3)]
                ccs = K.sb(s2, [128, NT], F32); ccs_T = T()
                vv = K.sb(s2, [128, NT], F32); vv_T = T()
                acc = K.sb(s2, [128, NT], F32); acc_T = T()
                cwr = K.sb(s2, [24, 128], F32); cwr_T = T()
                cw = K.sb(s2, [128, 24], F32); cw_T = T()
                K.dma(sp, cwr[:], conv_w[l].rearrange("k (c p) -> (k c) p", p=128), [DIN], [cwr_T], load_into=cwr_T)
                K.tr(pz_T[0], pz[0][:, 0:24], cwr[:], ident_f[0:24, 0:24], [cwr_T, ident_f_T])
                K.op(dve, lambda: V.tensor_copy(cw[:], pz[0][:, 0:24]), reads=[pz_T[0]], writes=[cw_T])
                wl = []
                for h2 in range(4):
                    wl.append(("qa", QA0 + h2 * 256, h2))
                for h2 in range(4):
                    wl.append(("ka", KA0 + h2 * 256, h2))
                for c2 in range(4):
                    wl.append(("cc", CC0 + c2 * 256, c2)); wl.append(("cx", CX0 + c2 * 256, c2)); wl.append(("cb", CB0 + c2 * 256, c2))
                for g2 in range(24):
                    wl.append(("g", G0 + g2 * 256, g2))

                def loadw(i):
                    c0 = wl[i][1]
                    K.dma(pool, wt[i % NW][:], wv[:, :, c0:c0 + 256], [DIN], [wt_T[i % NW]], load_into=wt_T[i % NW])

                for i in range(NW - 1):
                    loadw(i)
                mg = None
                if mod_pending and mod_pending[0] == l + 1:
                    mg = mod_gen(mod_pending.pop(0), s2, 4)
                pc = [0]
                sc_ = [0]

                def proj_chunk(w, w_T, c, gi):
                    t0, ng = GROUPS[gi]
                    p = pz[pc[0] % 6]; p_T = pz_T[pc[0] % 6]
                    pc[0] += 1
                    for kc in range(KC):
                        K.mm(p_T, p[:, 0:ng], w[:, kc, c * 128:(c + 1) * 128], hT[:, kc, t0:t0 + ng], [w_T, hT_T[gi]],
                             start=(kc == 0), stop=(kc == KC - 1))
                    return p, p_T

                ev = [0]
                for wi, (kind, c0, idx) in enumerate(wl):
                    if wi + NW - 1 < len(wl):
                        loadw(wi + NW - 1)
                    if mg is not None and wi >= 2:
                        next(mg, None)
                        if kind == "g":
                            next(mg, None)
                    w = wt[wi % NW]; w_T = wt_T[wi % NW]
                    if kind in ("qa", "ka", "g"):
                        for c in range(2):
                            sg = stg[sc_[0] % 3]; sg_T = stg_T[sc_[0] % 3]
                            sc_[0] += 1
                            for gi, (t0, ng) in enumerate(GROUPS if kind == "ka" else QG(l)):
                                p, p_T = proj_chunk(w, w_T, c, gi)
                                if kind == "g":
                                    K.actf(sg[:, t0:t0 + ng], p[:, 0:ng], AF.Sigmoid, [p_T], [sg_T])
                                elif ev[0] % 2 == 0:
                                    K.actf(sg[:, t0:t0 + ng], p[:, 0:ng], AF.Copy, [p_T], [sg_T])
                                else:
                                    K.op(dve, lambda sg=sg, p=p, t0=t0, ng=ng: V.tensor_copy(sg[:, t0:t0 + ng], p[:, 0:ng]), reads=[p_T], writes=[sg_T])
                                ev[0] += 1
                            n = idx * 2 + c
                            dst, dT = {"qa": (qaT, qaT_T), "ka": (kaT, kaT_T), "g": (gT, gT_T)}[kind]
                            K.dma(sp, dst[n], sg[:], [sg_T], [dT[n]], store_from=sg_T)
                    elif kind == "cc":
                        wcx = wt[(wi + 1) % NW]; wcx_T = wt_T[(wi + 1) % NW]
                        wcb = wt[(wi + 2) % NW]; wcb_T = wt_T[(wi + 2) % NW]
                        for c in range(2):
                            ci = idx * 2 + c
                            for gi, (t0, ng) in enumerate(QG(l)):
                                p, p_T = proj_chunk(w, w_T, c, gi)
                                K.actf(ccs[:, t0:t0 + ng], p[:, 0:ng], AF.Copy, [p_T], [ccs_T])
                            for gi, (t0, ng) in enumerate(QG(l)):
                                p, p_T = proj_chunk(wcx, wcx_T, c, gi)
                                K.op(dve, lambda p=p, t0=t0, ng=ng: V.tensor_tensor(vv[:, t0:t0 + ng], p[:, 0:ng], ccs[:, t0:t0 + ng], ALU.mult),
                                     reads=[p_T, ccs_T], writes=[vv_T])
                            w0 = cw[:, 0 * 8 + ci:0 * 8 + ci + 1]; w1 = cw[:, 8 + ci:8 + ci + 1]; w2 = cw[:, 16 + ci:16 + ci + 1]
                            nv = NT if l < nlayers - 1 else S
                            K.actf(acc[:, 0:nv], vv[:, 0:nv], AF.Identity, [vv_T, cw_T], [acc_T], scale=w1)
                            for (a, b) in (((0, S), (S, NT)) if l < nlayers - 1 else ((0, S),)):
                                K.op(dve, lambda a=a, b=b, w0=w0: V.scalar_tensor_tensor(acc[:, a + 1:b], vv[:, a:b - 1], w0, acc[:, a + 1:b], ALU.mult, ALU.add),
                                     reads=[vv_T, acc_T, cw_T], writes=[acc_T])
                                K.op(dve, lambda a=a, b=b, w2=w2: V.scalar_tensor_tensor(acc[:, a:b - 1], vv[:, a + 1:b], w2, acc[:, a:b - 1], ALU.mult, ALU.add),
                                     reads=[vv_T, acc_T, cw_T], writes=[acc_T])
                            sg = stg[sc_[0] % 3]; sg_T = stg_T[sc_[0] % 3]
                            sc_[0] += 1
                            for gi, (t0, ng) in enumerate(QG(l)):
                                p, p_T = proj_chunk(wcb, wcb_T, c, gi)
                                K.op(dve, lambda sg=sg, p=p, t0=t0, ng=ng: V.tensor_tensor(sg[:, t0:t0 + ng], p[:, 0:ng], acc[:, t0:t0 + ng], ALU.mult),
                                     reads=[p_T, acc_T], writes=[sg_T])
                            K.dma(sp, ycT[ci], sg[:], [sg_T], [ycT_T[ci]], store_from=sg_T)
                if mg is not None:
                    for _ in mg:
                        pass
                K.barrier()

    def phase_gqa(l, u):
        with contextlib.ExitStack() as st:
            kb_s = K.sb(st, [128, 2, NT], BF16); kb_sT = T()
            vb_s = K.sb(st, [128, NTILE, 256], BF16); vb_sT = T()
            K.dma(sp, kb_s[:], kbT.rearrange("h p t -> p h t"), [kbT_T], [kb_sT], load_into=kb_sT)
            K.dma(sp, vb_s[:], vb.rearrange("(t p) c -> p t c", p=128), vb_T, [vb_sT], load_into=vb_sT)
            qT = [K.sb(st, [128, NT], BF16) for _ in range(2)]; qT_T = [T() for _ in range(2)]
            ost = [K.sb(st, [128, NT], BF16) for _ in range(2)]; ost_T = [T() for _ in range(2)]
            pS = [K.ps(st, [128, 512], F32) for _ in range(3)]; pS_T = [T() for _ in range(3)]
            pO = [K.ps(st, [128, 512], F32) for _ in range(2)]; pO_T = [T() for _ in range(2)]
            pD = [K.ps(st, [128, 512], F32) for _ in range(2)]; pD_T = [T() for _ in range(2)]
            P = [K.sb(st, [128, 512], BF16) for _ in range(3)]; P_T = [T() for _ in range(3)]
            rd = [K.sb(st, [128, 512], F32) for _ in range(2)]; rd_T = [T() for _ in range(2)]
            pF = K.ps(st, [128, 512], F32); pF_T = T()

            def loadq(h):
                K.dma(sp, qT[h % 2][:], qbT[h], [qbT_T[h // 4]], [qT_T[h % 2]], load_into=qT_T[h % 2])

            loadq(0)
            sc = 0
            oc = 0
            for h in range(8):
                if h + 1 < 8:
                    loadq(h + 1)
                g = h // 4
                q = qT[h % 2]; q_T = qT_T[h % 2]
                o_ = ost[h % 2]; o_T = ost_T[h % 2]
                for gi, (q0, nq) in enumerate(QG(l)):
                    keys = list(range(NTILE)) if q0 < S else [16, 17]
                    po = pO[oc % 2]; po_T = pO_T[oc % 2]
                    pd = pD[oc % 2]; pd_T = pD_T[oc % 2]
                    r_ = rd[oc % 2]; r_T = rd_T[oc % 2]
                    oc += 1

                    def smm(c, slot):
                        K.mm(pS_T[slot], pS[slot][:, 0:nq], kb_s[:, g, c * 128:(c + 1) * 128], q[:, q0:q0 + nq], [kb_sT, q_T])

                    smm(keys[0], sc % 3)
                    for ki, c in enumerate(keys):
                        slot = sc % 3
                        if ki + 1 < len(keys):
                            smm(keys[ki + 1], (sc + 1) % 3)
                        K.actf(P[slot][:, 0:nq], pS[slot][:, 0:nq], AF.Exp, [pS_T[slot]], [P_T[slot]], scale=SCALE)
                        K.mm(po_T, po[:, 0:nq], vb_s[:, c, g * 128:(g + 1) * 128], P[slot][:, 0:nq], [vb_sT, P_T[slot]],
                             start=(ki == 0), stop=(ki == len(keys) - 1))
                        K.mm(pd_T, pd[:, 0:nq], ones_b[:], P[slot][:, 0:nq], [ones_T, P_T[slot]],
                             start=(ki == 0), stop=(ki == len(keys) - 1))
                        for _ in range(NFILL):
                            K.mm(pF_T, pF[:, 0:128], ones_b[:], q[:, q0:q0 + 128], [ones_T, q_T])
                        sc += 1
                    K.op(dve, lambda: V.reciprocal(r_[:, 0:nq], pd[:, 0:nq]), reads=[pd_T], writes=[r_T])
                    K.op(dve, lambda: V.tensor_tensor(o_[:, q0:q0 + nq], po[:, 0:nq], r_[:, 0:nq], ALU.mult), reads=[po_T, r_T], writes=[o_T])
                K.dma(sp, ogT[h], o_[:], [o_T], [ogT_T[h]], store_from=o_T)
            K.barrier()

    def na_keys(j):
        if j == 0:
            return [0, 1, 2, 3], 5
        if j == 1:
            return [0, 1, 2, 3], 9
        if j == 14:
            return [12, 13, 14, 15], 13
        if j == 15:
            return [12, 13, 14, 15], 17
        return [j - 2, j - 1, j, j + 1, j + 2], 0

    def phase_na(l, u):
        with contextlib.ExitStack() as st:
            va_s = K.sb(st, [128, NTILE, 1024], BF16); va_sT = T()
            K.dma(sp, va_s[:], va.rearrange("(t p) c -> p t c", p=128), va_T, [va_sT], load_into=va_sT)
            kT = [K.sb(st, [128, NT], BF16) for _ in range(2)]; kT_T = [T() for _ in range(2)]
            qT = [K.sb(st, [128, NT], BF16) for _ in range(2)]; qT_T = [T() for _ in range(2)]
            bt = [K.sb(st, [128, 21 * 128], F32) for _ in range(2)]; bt_T = [T() for _ in range(2)]
            ost = [K.sb(st, [128, NT], BF16) for _ in range(2)]; ost_T = [T() for _ in range(2)]
            pS = [K.ps(st, [128, 1024], F32) for _ in range(2)]; pS_T = [T() for _ in range(2)]
            pO = [K.ps(st, [128, 512], F32) for _ in range(2)]; pO_T = [T() for _ in range(2)]
            pD = [K.ps(st, [128, 512], F32) for _ in range(2)]; pD_T = [T() for _ in range(2)]
            tmp = [K.sb(st, [128, 640], F32) for _ in range(2)]; tmp_T = [T() for _ in range(2)]
            P = [K.sb(st, [128, 896], BF16) for _ in range(2)]; P_T = [T() for _ in range(2)]
            rd = [K.sb(st, [128, 512], F32) for _ in range(2)]; rd_T = [T() for _ in range(2)]

            def loadh(h):
                K.dma(sp, kT[h % 2][:], kaT[h], [kaT_T[h]], [kT_T[h % 2]], load_into=kT_T[h % 2])
                K.dma(sp, qT[h % 2][:], qaT[h], [qaT_T[h]], [qT_T[h % 2]], load_into=qT_T[h % 2])
                K.dma(sp, bt[h % 2][:], btab[l, h], [DIN], [bt_T[h % 2]], load_into=bt_T[h % 2])

            loadh(0)
            work = []
            for h in range(8):
                items = [(j * 128, 128) + na_keys(j) for j in range(16)]
                if l < nlayers - 1:
                    items.append((S, 256, [], 0))
                for it, (q0, nq, loc, tb0) in enumerate(items):
                    work.append((h, it, q0, nq, loc, tb0))
            last_it = 16 if l < nlayers - 1 else 15

            def do_S(k):
                h, it, q0, nq, loc, tb0 = work[k]
                ps_ = pS[k % 2]; ps_T = pS_T[k % 2]
                chunks = loc + [16, 17]
                for i, c in enumerate(chunks):
                    K.mm(ps_T, ps_[:, i * nq:(i + 1) * nq], kT[h % 2][:, c * 128:(c + 1) * 128], qT[h % 2][:, q0:q0 + nq], [kT_T[h % 2], qT_T[h % 2]])

            oc = 0
            loadh(1)
            do_S(0)
            for k, (h, it, q0, nq, loc, tb0) in enumerate(work):
                if k + 1 < len(work):
                    do_S(k + 1)
                b_ = bt[h % 2]; b_T = bt_T[h % 2]
                o_ = ost[h % 2]; o_T = ost_T[h % 2]
                slot4 = it % 4 if it < 16 else 0
                if (it < 16 and slot4 == 0) or it == 16:
                    po = pO[oc % 2]; po_T = pO_T[oc % 2]
                    pd = pD[oc % 2]; pd_T = pD_T[oc % 2]
                    r_ = rd[oc % 2]; r_T = rd_T[oc % 2]
                    oc += 1
                ocol = slot4 * 128
                ps_ = pS[k % 2]; ps_T = pS_T[k % 2]
                tm = tmp[k % 2]; tm_T = tmp_T[k % 2]
                p_ = P[k % 2]; p_T = P_T[k % 2]
                chunks = loc + [16, 17]
                nl = len(loc)
                if nl:
                    K.op(dve, lambda: V.scalar_tensor_tensor(tm[:, 0:nl * 128], ps_[:, 0:nl * 128], SCALE, b_[:, tb0 * 128:(tb0 + nl) * 128], ALU.mult, ALU.add),
                         reads=[ps_T, b_T], writes=[tm_T])
                    K.actf(p_[:, 0:nl * 128], tm[:, 0:nl * 128], AF.Exp, [tm_T], [p_T])
                K.actf(p_[:, nl * nq:(nl + 2) * nq], ps_[:, nl * nq:(nl + 2) * nq], AF.Exp, [ps_T], [p_T], scale=SCALE)
                for i, c in enumerate(chunks):
                    K.mm(po_T, po[:, ocol:ocol + nq], va_s[:, c, h * 128:(h + 1) * 128], p_[:, i * nq:(i + 1) * nq], [va_sT, p_T],
                         start=(i == 0), stop=(i == len(chunks) - 1))
                for i, c in enumerate(chunks):
                    K.mm(pd_T, pd[:, ocol:ocol + nq], ones_b[:], p_[:, i * nq:(i + 1) * nq], [ones_T, p_T],
                         start=(i == 0), stop=(i == len(chunks) - 1))
                if it < 16 and slot4 == 3:
                    qq = q0 - 384
                    K.op(dve, lambda: V.reciprocal(r_[:], pd[:]), reads=[pd_T], writes=[r_T])
                    K.op(dve, lambda: V.tensor_tensor(o_[:, qq:qq + 512], po[:], r_[:], ALU.mult), reads=[po_T, r_T], writes=[o_T])
                elif it == 16:
                    K.op(dve, lambda: V.reciprocal(r_[:, 0:256], pd[:, 0:256]), reads=[pd_T], writes=[r_T])
                    K.op(dve, lambda: V.tensor_tensor(o_[:, S:NT], po[:, 0:256], r_[:, 0:256], ALU.mult), reads=[po_T, r_T], writes=[o_T])
                if it == last_it:
                    K.dma(sp, onT[h], o_[:], [o_T], [onT_T[h]], store_from=o_T)
                    if h + 2 < 8:
                        loadh(h + 2)
            K.barrier()

    def phase_merge(l, u):
        with contextlib.ExitStack() as st:
            srcs = []
            for (dr, dT) in ((onT, onT_T), (ogT, ogT_T), (ycT, ycT_T)):
                s_ = K.sb(st, [128, 8, NT], BF16)
                srcs.append((s_, [T() for _ in GROUPS], dr.rearrange("h p t -> p h t"), dT))
            for gi, (t0, ng) in enumerate(QG(l)):
                for (s_, s_Ts, dv, dT) in srcs:
                    K.dma(sp, s_[:, :, t0:t0 + ng], dv[:, :, t0:t0 + ng], dT, [s_Ts[gi]], load_into=s_Ts[gi])
            NW = 2
            wb = [K.sb(st, [128, 3, 8, 256], BF16) for _ in range(NW)]; wb_T = [T() for _ in range(NW)]
            gs = [K.sb(st, [128, 3, NT], BF16) for _ in range(2)]; gs_T = [T() for _ in range(2)]
            pz = [K.ps(st, [128, 512], F32) for _ in range(6)]; pz_T = [T() for _ in range(6)]
            ta = [K.sb(st, [128, 512], F32) for _ in range(2)]; ta_T = [T() for _ in range(2)]
            tb = [K.sb(st, [128, 512], F32) for _ in range(2)]; tb_T = [T() for _ in range(2)]
            stg = [K.sb(st, [128, NT], BF16) for _ in range(2)]; stg_T = [T() for _ in range(2)]
            wbv = w_branch[l].rearrange("i (kc p) n -> p i kc n", p=128)

            def loadw(i):
                K.dma(pool, wb[i % NW][:], wbv[:, :, :, i * 256:(i + 1) * 256], [DIN], [wb_T[i % NW]], load_into=wb_T[i % NW])

            def loadg(n):
                K.dma(sp, gs[n % 2][:], gT.rearrange("(i n) p t -> n p i t", i=3)[n], [gT_T[n], gT_T[16 + n], gT_T[32 + n]],
                      [gs_T[n % 2]], load_into=gs_T[n % 2])

            loadw(0)
            loadg(0)
            pc = 0
            tc = 0
            for n in range(16):
                if n % 2 == 0 and n // 2 + 1 < 8:
                    loadw(n // 2 + 1)
                if n + 1 < 16:
                    loadg(n + 1)
                w = wb[(n // 2) % NW]; w_T = wb_T[(n // 2) % NW]
                c = n % 2
                g_ = gs[n % 2]; g_T = gs_T[n % 2]
                sg = stg[n % 2]; sg_T = stg_T[n % 2]
                for gi, (t0, ng) in enumerate(QG(l)):
                    pp = []
                    for i in range(3):
                        p = pz[pc % 6]; p_T = pz_T[pc % 6]
                        pc += 1
                        for kc in range(8):
                            K.mm(p_T, p[:, 0:ng], w[:, i, kc, c * 128:(c + 1) * 128], srcs[i][0][:, kc, t0:t0 + ng], [w_T, srcs[i][1][gi]],
                                 start=(kc == 0), stop=(kc == 7))
                        pp.append((p, p_T))
                    a_ = ta[tc % 2]; a_T = ta_T[tc % 2]
                    b_ = tb[tc % 2]; b_T = tb_T[tc % 2]
                    tc += 1
                    K.op(dve, lambda: V.tensor_tensor(a_[:, 0:ng], pp[0][0][:, 0:ng], g_[:, 0, t0:t0 + ng], ALU.mult), reads=[pp[0][1], g_T], writes=[a_T])
                    K.op(dve, lambda: V.tensor_tensor(b_[:, 0:ng], pp[1][0][:, 0:ng], g_[:, 1, t0:t0 + ng], ALU.mult), reads=[pp[1][1], g_T], writes=[b_T])
                    K.op(dve, lambda: V.tensor_tensor(a_[:, 0:ng], a_[:, 0:ng], b_[:, 0:ng], ALU.add), reads=[a_T, b_T], writes=[a_T])
                    K.op(dve, lambda: V.tensor_tensor(b_[:, 0:ng], pp[2][0][:, 0:ng], g_[:, 2, t0:t0 + ng], ALU.mult), reads=[pp[2][1], g_T], writes=[b_T])
                    K.op(dve, lambda: V.tensor_tensor(sg[:, t0:t0 + ng], a_[:, 0:ng], b_[:, 0:ng], ALU.add), reads=[a_T, b_T], writes=[sg_T])
                K.dma(sp, mT[n], sg[:], [sg_T], [mT_T[n]], store_from=sg_T)
            K.barrier()

    class PostLN:
        def __init__(self, st, l, which, npz=3, ntacc=2):
            self.npz = npz
            self.l = l
            self.which = which
            self.gpart = 2 if which == 0 else 5
            self.tacc = [K.sb(st, [128, 4, D], F32, "tacc") for _ in range(ntacc)]
            self.tacc_T = [[T() for _ in range(4)] for _ in range(ntacc)]
            self.lng = K.sb(st, [128, D], F32); self.lnb = K.sb(st, [128, D], F32); self.ln_T = T()
            K.dma(sp, self.lng[:], ln_g[l, which].partition_broadcast(128), [DIN], [self.ln_T], load_into=self.ln_T)
            K.dma(sp, self.lnb[:], ln_b[l, which].partition_broadcast(128), [DIN], [self.ln_T], load_into=self.ln_T)
            self.pz = [K.ps(st, [128, 512], F32) for _ in range(npz)]; self.pz_T = [T() for _ in range(npz)]
            self.pt = [K.ps(st, [128, 8, 128], BF16) for _ in range(2)]; self.pt_T = [T() for _ in range(2)]
            self.ys = [K.sb(st, [128, 512], BF16) for _ in range(2)]; self.ys_T = [T() for _ in range(2)]
            self.stt = [mk_stats(st) for _ in range(2)]
            self.pc = 0
            self.yc = 0
            self.lc = 0
            self.pending = None
            self.fin = []
            self.after_fin = []

        def when_drained(self, fn):
            if not self.fin:
                fn()
            else:
                self.after_fin.append(fn)

        def load_x(self, l, u, t0, ng, b, from_xs=False):
            for i in range(ng // 128):
                ap, dT = src_rows(1 if from_xs else l, u, t0 + i * 128, 128)
                K.dma(sp, self.tacc[b][:, i, :], ap, [dT], [self.tacc_T[b][i]], load_into=self.tacc_T[b][i])

        def chunk(self, n, ng, row, mms, first, b):
            l = self.l
            p = self.pz[self.pc % self.npz]; p_T = self.pz_T[self.pc % self.npz]
            self.pc += 1
            for i, (lh, rh, rd_) in enumerate(mms):
                K.mm(p_T, p[:, 0:ng], lh, rh, rd_, start=(i == 0), stop=(i == len(mms) - 1))
            y_ = self.ys[self.yc % 2]; y_T = self.ys_T[self.yc % 2]
            pt = self.pt[self.yc % 2]; pt_T = self.pt_T[self.yc % 2]
            self.yc += 1
            K.actf(y_[:, 0:ng], p[:, 0:ng], AF.Identity, [p_T, modT_T[l]], [y_T], scale=modT[l][:, self.gpart * 16 + n, row:row + 1])
            self.flush()
            self.pending = (n, ng, first, y_, y_T, pt, pt_T, b)
            self.step_finish()

        def flush(self):
            if self.pending is None:
                return
            n, ng, first, y_, y_T, pt, pt_T, b = self.pending
            self.pending = None
            nt = ng // 128
            for i in range(nt):
                K.tr(pt_T, pt[:, i, :], y_[:, i * 128:(i + 1) * 128], ident_b[:], [y_T, ident_b_T])
            tv = self.tacc[b][:, 0:nt, n * 128:(n + 1) * 128]
            tT = self.tacc_T[b][0:nt]
            if first:
                K.op(dve, lambda: V.scalar_tensor_tensor(tv, tv, ALPHA, pt[:, 0:nt, :], ALU.mult, ALU.add), reads=[pt_T] + tT, writes=tT)
            else:
                K.op(dve, lambda: V.tensor_tensor(tv, tv, pt[:, 0:nt, :], ALU.add), reads=[pt_T] + tT, writes=tT)

        def _finish_tile(self, i, b, store_fn):
            xa = self.tacc[b][:, i, :]; x_T = self.tacc_T[b][i]
            stt = self.stt[self.lc % 2]
            self.lc += 1
            rs, nm = ln_stats(stt, xa, x_T)
            K.actf(xa, xa, AF.Identity, [x_T, stt[4]], [x_T], scale=rs[:], bias=nm[:])
            K.op(dve, lambda: V.tensor_tensor(xa, xa, self.lng[:], ALU.mult), reads=[x_T, self.ln_T], writes=[x_T])
            K.op(dve, lambda: V.tensor_tensor(xa, xa, self.lnb[:], ALU.add), reads=[x_T, self.ln_T], writes=[x_T])
            store_fn(i, xa, x_T)

        def finish(self, ng, store_fn, b):
            self.flush()
            for i in range(ng // 128):
                self.fin.append((i, b, store_fn))

        def step_finish(self, k=1):
            for _ in range(k):
                if self.fin:
                    self._finish_tile(*self.fin.pop(0))
            if not self.fin:
                while self.after_fin:
                    self.after_fin.pop(0)()

        def drain(self):
            self.flush()
            while self.fin:
                self._finish_tile(*self.fin.pop(0))
            while self.after_fin:
                self.after_fin.pop(0)()

    def dst_rows(l, u, t0, i, which):
        r0 = t0 + i * 128
        g = [k for k, (a, b) in enumerate(GROUPS) if a <= t0 < a + b][0]
        if l == nlayers - 1 and which == 1:
            if r0 >= S:
                return None
            return y_out[u, r0:r0 + 128, :], None
        return xs[u, r0:r0 + 128, :], xs_T[u][g]

    OUT_T = T()

    def phase_outproj(l, u):
        with contextlib.ExitStack() as st:
            wo = K.sb(st, [128, KC, D], BF16, "wo"); wo_T = [T() for _ in range(8)]
            wv = w_o[l].rearrange("(kc p) n -> p kc n", p=128)
            for c2 in range(8):
                K.dma(pool, wo[:, :, c2 * 256:(c2 + 1) * 256], wv[:, :, c2 * 256:(c2 + 1) * 256], [DIN], [wo_T[c2]], load_into=wo_T[c2])
            mTs = [K.sb(st, [128, 16, 512], BF16) for _ in range(2)]; mTs_T = [T() for _ in range(2)]
            mv = mT.rearrange("n p t -> p n t")

            def loadm(gi):
                t0, ng = GROUPS[gi]
                K.dma(sp, mTs[gi % 2][:, :, 0:ng], mv[:, :, t0:t0 + ng], mT_T, [mTs_T[gi % 2]], load_into=mTs_T[gi % 2])

            loadm(0)
            pl = PostLN(st, l, 0)
            pl.load_x(l, u, GROUPS[0][0], GROUPS[0][1], 0)
            for gi, (t0, ng) in enumerate(QG(l)):
                b = gi % 2
                if gi + 1 < len(QG(l)):
                    loadm(gi + 1)
                    pl.when_drained(lambda gi=gi: pl.load_x(l, u, GROUPS[gi + 1][0], GROUPS[gi + 1][1], (gi + 1) % 2))
                m_ = mTs[gi % 2]; m_T = mTs_T[gi % 2]
                for n in range(16):
                    mms = [(wo[:, kc, n * 128:(n + 1) * 128], m_[:, kc, 0:ng], [wo_T[n // 2], m_T]) for kc in range(KC)]
                    pl.chunk(n, ng, row_of(u, t0), mms, True, b)

                def store(i, xa, x_T, t0=t0):
                    d = dst_rows(l, u, t0, i, 0)
                    K.dma(sp, d[0], xa, [x_T], [d[1]], store_from=x_T)
                pl.finish(ng, store, b)
            pl.drain()
            K.barrier()

    def phase_mlp(l, u):
        NSPLIT = 2
        HC = 64 // NSPLIT
        with contextlib.ExitStack() as st:
            pl = PostLN(st, l, 1, npz=2)
            lnm = LNMod(st, l, 3)
            h2 = K.sb(st, [128, KC, 512], BF16); h2_T = T()
            uT = K.sb(st, [128, HC, 512], BF16); uT_T = T()
            NU = 3
            wu = [K.sb(st, [128, KC, 256], BF16) for _ in range(NU)]; wu_T = [T() for _ in range(NU)]
            ND = 3
            wd = [K.sb(st, [128, HC, 128], BF16) for _ in range(ND)]; wd_T = [T() for _ in range(ND)]
            pu = [K.ps(st, [128, 512], F32) for _ in range(2)]; pu_T = [T() for _ in range(2)]
            rl = [K.sb(st, [128, 512], F32) for _ in range(1)]; rl_T = [T() for _ in range(1)]
            wuv = w_up[l].rearrange("(kc p) n -> p kc n", p=128)
            wdv = w_down[l].rearrange("(kc p) n -> p kc n", p=128)
            ngroups = len(GROUPS) if l < nlayers - 1 else len(GROUPS) - 1
            ul = [(gi, sp_, j) for gi in range(ngroups) for sp_ in range(NSPLIT) for j in range(HC // 2)]
            dl = [(gi, sp_, j) for gi in range(ngroups) for sp_ in range(NSPLIT) for j in range(16)]

            def loadu(i):
                gi, sp_, j = ul[i]
                c0 = sp_ * HC * 128 + j * 256
                K.dma(pool, wu[i % NU][:], wuv[:, :, c0:c0 + 256], [DIN], [wu_T[i % NU]], load_into=wu_T[i % NU])

            def loadd(i):
                gi, sp_, j = dl[i]
                K.dma(pool, wd[i % ND][:], wdv[:, sp_ * HC:(sp_ + 1) * HC, j * 128:(j + 1) * 128], [DIN], [wd_T[i % ND]], load_into=wd_T[i % ND])

            ui = 0
            di = 0
            for i in range(NU - 1):
                loadu(i)
            loadd(0)
            loadd(1)
            uc = 0
            groups = GROUPS[:ngroups]

            def prep(gi):
                t0, ng = groups[gi]
                b = gi % 2
                pl.load_x(l, u, t0, ng, b, from_xs=True)
                lnm.prep([(pl.tacc[b][:, i, :], pl.tacc_T[b][i]) for i in range(ng // 128)])

            def emitT(gi):
                t0, ng = groups[gi]
                lnm.emitT(ng // 128, row_of(u, t0), h2, h2_T, 0)

            prep(0)
            emitT(0)
            for gi in range(ngroups):
                t0, ng = groups[gi]
                b = gi % 2
                for sp_ in range(NSPLIT):
                    last = sp_ == NSPLIT - 1
                    if last and gi + 1 < ngroups:
                        prep(gi + 1)
                    for j in range(HC // 2):
                        if ui + NU - 1 < len(ul):
                            loadu(ui + NU - 1)
                        w = wu[ui % NU]; w_T = wu_T[ui % NU]
                        ui += 1
                        for c in range(2):
                            hc = j * 2 + c
                            p = pu[uc % 2]; p_T = pu_T[uc % 2]
                            r_ = rl[0]; r_T = rl_T[0]
                            uc += 1
                            for kc in range(KC):
                                K.mm(p_T, p[:, 0:ng], w[:, kc, c * 128:(c + 1) * 128], h2[:, kc, 0:ng], [w_T, h2_T],
                                     start=(kc == 0), stop=(kc == KC - 1))
                            K.actf(r_[:, 0:ng], p[:, 0:ng], AF.Relu, [p_T], [r_T])
                            K.op(dve, lambda: V.tensor_tensor(uT[:, hc, 0:ng], r_[:, 0:ng], r_[:, 0:ng], ALU.mult), reads=[r_T], writes=[uT_T])
                            pl.step_finish()
                    if last and gi + 1 < ngroups:
                        emitT(gi + 1)
                    for n in range(16):
                        if di + 2 < len(dl):
                            loadd(di + 2)
                        w = wd[di % ND]; w_T = wd_T[di % ND]
                        di += 1
                        mms = [(w[:, kc, :], uT[:, kc, 0:ng], [w_T, uT_T]) for kc in range(HC)]
                        pl.chunk(n, ng, row_of(u, t0), mms, sp_ == 0, b)

                def store(i, xa, x_T, t0=t0):
                    d = dst_rows(l, u, t0, i, 1)
                    if d is None:
                        return
                    if d[1] is None:
                        K.dma(sp, d[0], xa, [x_T], [OUT_T], store_from=x_T)
                    else:
                        K.dma(sp, d[0], xa, [x_T], [d[1]], store_from=x_T)
                pl.finish(ng, store, b)
            pl.drain()
            K.barrier()

    done = False
    for l in range(nlayers):
        for u in units:
            phase_in_proj(l, u)
            if stop_after in ("A", "B1", "B"):
                done = True
                break
            phase_gqa(l, u)
            phase_na(l, u)
            if stop_after == "D":
                done = True
                break
            phase_merge(l, u)
            if stop_after == "E":
                done = True
                break
            phase_outproj(l, u)
            if stop_after == "F":
                done = True
                break
            phase_mlp(l, u)
            if stop_after == "H":
                done = True
                break
        if done:
            break
    K.barrier()
    K.stack.close()
    return nc


def _rope_tables():
    t = np.arange(S)
    row = (t // 64).astype(np.float32)
    col = (t % 64).astype(np.float32)
    freqs = (np.float32(10000.0) ** (-np.arange(0, 64, 2, dtype=np.float32) / np.float32(64))).astype(np.float32)
    ar = row[:, None] * freqs
    ac = col[:, None] * freqs
    cr, sr, cc, sc = np.cos(ar), np.sin(ar), np.cos(ac), np.sin(ac)
    cos = np.ones((NT, 128), np.float32)
    sin = np.zeros((NT, 128), np.float32)
    cos[:S] = np.concatenate([cr, cr, cc, cc], axis=1)
    sin[:S] = np.concatenate([-sr, sr, -sc, sc], axis=1)
    return cos.astype(np.float32), sin.astype(np.float32)


def _bias_tables(rpb):
    rows = 32
    r = np.arange(rows)
    row_start = np.clip(r - 4, 0, rows - 8)
    col = np.arange(64)
    col_start = np.clip(col - 8, 0, 48)
    in_win = (col[None, :] >= col_start[:, None]) & (col[None, :] < col_start[:, None] + 16)
    dc_idx = np.clip(col[None, :] - col[:, None], -15, 15) + 15

    def tile(j, c):
        kr = np.repeat(np.array([2 * c, 2 * c + 1]), 64)
        kc_ = np.tile(col, 2)
        qr = np.repeat(np.array([2 * j, 2 * j + 1]), 64)
        qc = np.tile(col, 2)
        valid = (kr[:, None] >= row_start[qr][None, :]) & (kr[:, None] < row_start[qr][None, :] + 8) & in_win[qc[None, :], kc_[:, None]]
        dr = np.clip(kr[:, None] - qr[None, :] + 7, 0, 14)
        dc = dc_idx[qc[None, :], kc_[:, None]]
        return valid, dr, dc

    specs = [(5, 5 + d) for d in (-2, -1, 0, 1, 2)]
    for j, ks in ((0, [0, 1, 2, 3]), (1, [0, 1, 2, 3]), (14, [12, 13, 14, 15]), (15, [12, 13, 14, 15])):
        specs += [(j, c) for c in ks]
    out = np.empty((rpb.shape[0], 8, 128, 21 * 128), np.float32)
    for i, (j, c) in enumerate(specs):
        valid, dr, dc = tile(j, c)
        g = rpb[:, :, dr, dc]
        out[:, :, :, i * 128:(i + 1) * 128] = np.where(valid[None, None], g, np.float32(NEG))
    return out


_NC_CACHE = {}


def kernel(x, c, ctx, c_ctx, w_mod, b_mod, w_in, rpb, q_gain, k_gain, conv_w,
           w_branch, w_o, w_up, w_down, ln_g, ln_b):
    f = lambda a: np.ascontiguousarray(np.asarray(a, dtype=np.float32))
    x, c, ctx, c_ctx = f(x), f(c), f(ctx), f(c_ctx)
    shared = dict(w_mod=f(w_mod), b_mod=f(b_mod), w_in=f(w_in), btab=_bias_tables(f(rpb)), q_gain=f(q_gain), k_gain=f(k_gain),
                  conv_w=f(conv_w), w_branch=f(w_branch), w_o=f(w_o), w_up=f(w_up), w_down=f(w_down), ln_g=f(ln_g), ln_b=f(ln_b),
                  ident=np.eye(128, dtype=np.float32))
    shared["rope_cos"], shared["rope_sin"] = _rope_tables()
    if "nc" not in _NC_CACHE:
        _NC_CACHE["nc"] = build_program()
    nc = _NC_CACHE["nc"]
    in_maps = []
    for i in range(8):
        m = dict(shared)
        m["x"] = np.ascontiguousarray(x[2 * i:2 * i + 2])
        m["ctx"] = np.ascontiguousarray(ctx[2 * i:2 * i + 2])
        m["cvec"] = np.ascontiguousarray(np.stack([c[2 * i], c[2 * i + 1], c_ctx, c_ctx]))
        in_maps.append(m)
    res = run_bass_kernel_spmd(nc, in_maps, core_ids=list(range(8)))
    return np.concatenate([r["y"] for r in res.results], axis=0).astype(np.float32)
```

```python
import contextlib
import numpy as np
import concourse.bass as bass
import concourse.mybir as mybir
from concourse.bass_utils import run_bass_kernel_spmd

F32 = mybir.dt.float32
BF16 = mybir.dt.bfloat16
AF = mybir.ActivationFunctionType
ALU = mybir.AluOpType
AX = mybir.AxisListType

D = 2048
KC = 16
S = 2048
L = 256
NT = S + L
NTILE = NT // 128
DEPTH = 2
IN_W = 13824
QA0, QB0, KA0, VA0, KB0, VB0, CB0, CC0, CX0, G0 = 0, 1024, 2048, 3072, 4096, 4352, 4608, 5632, 6656, 7680
HID = 8192
EPS = 1e-6
SCALE = 128 ** -0.5
ALPHA = (2 * DEPTH) ** 0.25
GROUPS = [(0, 512), (512, 512), (1024, 512), (1536, 512), (2048, 256)]
NEG = -30000.0
NFILL = 1


class Sem:
    def __init__(self, handle):
        self.handle = handle
        self.val = 0


class Eng:
    def __init__(self, obj, kind, sem=None):
        self.obj = obj
        self.kind = kind
        self.sem = sem
        self.count = 0
        self.waited = {}


class T:
    __slots__ = ("ap", "last_w", "reads", "lsem", "ssem")

    def __init__(self, ap=None):
        self.ap = ap
        self.last_w = None
        self.reads = {}
        self.lsem = None
        self.ssem = None


class KB:
    def __init__(self, nc, nsem=100):
        self.nc = nc
        self.stack = contextlib.ExitStack()
        self.free_sems = []
        self.all_sems = []
        for i in range(nsem):
            s = Sem(self.stack.enter_context(nc.semaphore(f"s{i}")))
            self.free_sems.append(s)
            self.all_sems.append(s)
        mk = lambda obj, kind: Eng(obj, kind, self.free_sems.pop())
        self.pe = mk(nc.tensor, "pe")
        self.act = mk(nc.scalar, "act")
        self.dve = mk(nc.vector, "dve")
        self.pool = mk(nc.gpsimd, "pool")
        self.sp = Eng(nc.sync, "sp", None)
        self.engs = [self.pe, self.act, self.dve, self.pool, self.sp]
        self.phase_sems = []
        self.uid = 0

    def name(self, p):
        self.uid += 1
        return f"{p}{self.uid}"

    def sb(self, st, shape, dt, name="t"):
        return st.enter_context(self.nc.sbuf_tensor(self.name(name), list(shape), dt))

    def ps(self, st, shape, dt, name="p"):
        return st.enter_context(self.nc.psum_tensor(self.name(name), list(shape), dt))

    def getsem(self):
        s = self.free_sems.pop()
        self.phase_sems.append(s)
        return s

    def _wait(self, eng, deps):
        for sem, val in deps:
            if eng.waited.get(sem, 0) >= val:
                continue
            eng.obj.wait_ge(sem.handle, val)
            eng.waited[sem] = val

    def _deps(self, eng, reads, writes):
        deps = {}

        def add(tok, skip_same):
            if tok is None:
                return
            sem, val = tok
            if eng.sem is not None and sem is eng.sem and (skip_same or eng.kind == "pe"):
                return
            if deps.get(sem, 0) < val:
                deps[sem] = val

        for t in reads:
            add(t.last_w, False)
        for t in writes:
            add(t.last_w, True)
            for s, v in t.reads.items():
                add((s, v), True)
        return deps

    def op(self, eng, fn, reads=(), writes=()):
        deps = self._deps(eng, reads, writes)
        self._wait(eng, deps.items())
        ins = fn()
        eng.count += 1
        ins.then_inc(eng.sem.handle, 1)
        tok = (eng.sem, eng.count)
        for t in reads:
            t.reads[eng.sem] = eng.count
        for t in writes:
            t.last_w = tok
            t.reads = {}
        return ins

    def dma(self, q, out_ap, in_ap, reads, writes, load_into=None, store_from=None, **kw):
        if load_into is not None:
            if load_into.lsem is None:
                load_into.lsem = self.getsem()
            sem = load_into.lsem
        else:
            if store_from.ssem is None:
                store_from.ssem = self.getsem()
            sem = store_from.ssem
        deps = self._deps(q, reads, writes)
        if sem.val > 0 and deps.get(sem, 0) < sem.val:
            deps[sem] = sem.val
        self._wait(q, deps.items())
        ins = q.obj.dma_start(out=out_ap, in_=in_ap, **kw)
        sem.val += 16
        ins.then_inc(sem.handle, 16)
        tok = (sem, sem.val)
        for t in reads:
            t.reads[sem] = sem.val
        for t in writes:
            t.last_w = tok
            t.reads = {}

    def barrier(self, release=True):
        toks = [(e.sem, e.count) for e in self.engs if e.sem is not None and e.count > 0]
        toks += [(s, s.val) for s in self.all_sems if s.val > 0 and all(s is not e.sem for e in self.engs)]
        for e in self.engs:
            self._wait(e, [t for t in toks if t[0] is not e.sem])
        if release:
            self.free_sems.extend(self.phase_sems)
            self.phase_sems = []

    def mm(self, out_t, out_ap, lhsT_ap, rhs_ap, reads, start=True, stop=True):
        return self.op(self.pe, lambda: self.nc.tensor.matmul(out_ap, lhsT_ap, rhs_ap, start=start, stop=stop),
                       reads=reads, writes=[out_t])

    def tr(self, out_t, out_ap, in_ap, ident_ap, reads):
        return self.op(self.pe, lambda: self.nc.tensor.transpose(out_ap, in_ap, ident_ap), reads=reads, writes=[out_t])

    def actf(self, out_ap, in_ap, func, reads, writes, **kw):
        return self.op(self.act, lambda: self.nc.scalar.activation(out_ap, in_ap, func, **kw), reads=reads, writes=writes)


def build_program(debug=False, stop_after=None, nlayers=DEPTH, units=(0, 1)):
    nc = bass.Bass("TRN2", target_bir_lowering=False)
    K = KB(nc)
    pe, act, dve, pool, sp = K.pe, K.act, K.dve, K.pool, K.sp
    V = nc.vector
    dbg_kind = "ExternalOutput" if debug else "Internal"

    def din(name, shape, dt=F32):
        return nc.dram_tensor(name, list(shape), dt, kind="ExternalInput").ap()

    def dscr(name, shape, dt):
        return nc.dram_tensor(name, list(shape), dt, kind=dbg_kind).ap()

    x_in = din("x", [2, S, D])
    ctx_in = din("ctx", [2, L, D])
    cvec = din("cvec", [4, D])
    w_mod = din("w_mod", [DEPTH, D, 6 * D])
    b_mod = din("b_mod", [DEPTH, 6 * D])
    w_in = din("w_in", [DEPTH, D, IN_W])
    btab = din("btab", [DEPTH, 8, 128, 21 * 128])
    q_gain = din("q_gain", [DEPTH, 128])
    k_gain = din("k_gain", [DEPTH, 128])
    conv_w = din("conv_w", [DEPTH, 3, 1024])
    w_branch = din("w_branch", [DEPTH, 3, 1024, D])
    w_o = din("w_o", [DEPTH, D, D])
    w_up = din("w_up", [DEPTH, D, HID])
    w_down = din("w_down", [DEPTH, HID, D])
    ln_g = din("ln_g", [DEPTH, 2, D])
    ln_b = din("ln_b", [DEPTH, 2, D])
    rope_cos = din("rope_cos", [NT, 128])
    rope_sin = din("rope_sin", [NT, 128])
    ident_in = din("ident", [128, 128])
    y_out = nc.dram_tensor("y", [2, S, D], F32, kind="ExternalOutput").ap()

    xs = dscr("xs", [2, NT, D], F32)
    qaT = dscr("qaT", [8, 128, NT], BF16)
    kaT = dscr("kaT", [8, 128, NT], BF16)
    va = dscr("va", [NT, 1024], BF16)
    qbT = dscr("qbT", [8, 128, NT], BF16)
    kbT = dscr("kbT", [2, 128, NT], BF16)
    vb = dscr("vb", [NT, 256], BF16)
    ycT = dscr("ycT", [8, 128, NT], BF16)
    gT = dscr("gT", [48, 128, NT], BF16)
    onT = dscr("onT", [8, 128, NT], BF16)
    ogT = dscr("ogT", [8, 128, NT], BF16)
    mT = dscr("mT", [16, 128, NT], BF16)

    xs_T = [[T() for _ in GROUPS] for _ in range(2)]
    qaT_T = [T() for _ in range(8)]
    kaT_T = [T() for _ in range(8)]
    va_T = [T() for _ in range(NTILE)]
    qbT_T = [T() for _ in range(2)]
    kbT_T = T()
    vb_T = [T() for _ in range(NTILE)]
    ycT_T = [T() for _ in range(8)]
    gT_T = [T() for _ in range(48)]
    onT_T = [T() for _ in range(8)]
    ogT_T = [T() for _ in range(8)]
    mT_T = [T() for _ in range(16)]
    DIN = T()

    gst = K.stack
    ident_f = K.sb(gst, [128, 128], F32, "identf"); ident_f_T = T()
    ident_b = K.sb(gst, [128, 128], BF16, "identb"); ident_b_T = T()
    ones_b = K.sb(gst, [128, 128], BF16, "onesb"); ones_T = T()
    modT = [K.sb(gst, [128, 96, 4], F32, f"modT{l}") for l in range(DEPTH)]
    modT_T = [T() for _ in range(DEPTH)]

    K.dma(sp, ident_f[:], ident_in[:, :], [DIN], [ident_f_T], load_into=ident_f_T)
    K.dma(pool, ident_b[:], ident_in[:, :], [DIN], [ident_b_T], load_into=ident_b_T)
    K.op(dve, lambda: V.memset(ones_b[:], 1.0), writes=[ones_T])
    ones_f = K.sb(gst, [128, 128], F32, "onesf")
    K.op(dve, lambda: V.memset(ones_f[:], 1.0), writes=[ones_T])
    eps_t = K.sb(gst, [128, 1], F32, "eps"); eps_T = T()
    K.op(dve, lambda: V.memset(eps_t[:], EPS), writes=[eps_T])

    cact = K.sb(gst, [128, 64], BF16, "cact"); cact_T = T()

    def phase_cact():
        with contextlib.ExitStack() as st:
            cv = K.sb(st, [64, 128], F32); cv_T = T()
            psc_full = K.ps(st, [128, 512], F32); psc = psc_full[:, 0:64]; psc_T = T()
            K.dma(sp, cv[:], cvec.rearrange("r (kc p) -> (r kc) p", p=128), [DIN], [cv_T], load_into=cv_T)
            K.tr(psc_T, psc, cv[:], ident_f[0:64, 0:64], [cv_T, ident_f_T])
            K.actf(cact[:], psc, AF.Silu, [psc_T], [cact_T])
            K.barrier()

    def mod_gen(l, st, NB):
        cact_v = cact[:].rearrange("p (r kc) -> p kc r", kc=16)
        wm = [K.sb(st, [128, 16, 256], BF16) for _ in range(NB)]
        wm_T = [T() for _ in range(NB)]
        psm = K.ps(st, [128, 128, 4], F32); psm_T = T()
        psb_full = K.ps(st, [128, 512], F32); psb = psb_full[:, 0:96]; psb_T = T()
        brow = K.sb(st, [96, 128], F32); brow_T = T()
        bsb = K.sb(st, [128, 96], F32); bsb_T = T()
        wmv = w_mod[l].rearrange("(kc p) n -> p kc n", p=128)

        def load(i):
            K.dma(pool, wm[i % NB][:], wmv[:, :, i * 256:(i + 1) * 256], [DIN], [wm_T[i % NB]], load_into=wm_T[i % NB])

        for i in range(NB - 1):
            load(i)
        for b in range(48):
            if b + NB - 1 < 48:
                load(b + NB - 1)
            for c in range(2):
                n = b * 2 + c
                for kc in range(16):
                    K.mm(psm_T, psm[:, n, :], wm[b % NB][:, kc, c * 128:(c + 1) * 128], cact_v[:, kc, :],
                         [wm_T[b % NB], cact_T], start=(kc == 0), stop=(kc == 15))
            if b == 47:
                K.dma(sp, brow[:], b_mod[l].rearrange("(n p) -> n p", p=128), [DIN], [brow_T], load_into=brow_T)
                K.tr(psb_T, psb, brow[:], ident_f[0:96, 0:96], [brow_T, ident_f_T])
                K.op(dve, lambda: V.tensor_copy(bsb[:], psb), reads=[psb_T], writes=[bsb_T])
                for r in range(4):
                    K.op(dve, lambda r=r: V.tensor_tensor(modT[l][:, :, r], psm[:, 0:96, r], bsb[:], ALU.add),
                         reads=[psm_T, bsb_T], writes=[modT_T[l]])
                for p0 in (16, 64):
                    K.op(dve, lambda p0=p0: V.tensor_scalar_add(modT[l][:, p0:p0 + 16, :], modT[l][:, p0:p0 + 16, :], 1.0),
                         reads=[modT_T[l]], writes=[modT_T[l]])
            yield b

    def phase_mod(l):
        with contextlib.ExitStack() as st:
            for _ in mod_gen(l, st, 6):
                pass
            K.barrier()

    phase_cact()
    phase_mod(0)
    mod_pending = list(range(1, nlayers))


    def row_of(u, t0):
        return u if t0 < S else 2

    def QG(l):
        return GROUPS if l < nlayers - 1 else GROUPS[:4]

    def ln_stats(st_tiles, x_ap, x_T):
        bst, mv, rs, nm, sT = st_tiles
        xv = x_ap.rearrange("p (c f) -> p c f", f=512)
        for c in range(4):
            K.op(dve, lambda c=c: V.bn_stats(bst[:, c, :], xv[:, c, :]), reads=[x_T], writes=[sT])
        K.op(dve, lambda: V.bn_aggr(mv[:], bst[:].rearrange("p c f -> p (c f)")), reads=[sT], writes=[sT])
        K.actf(rs[:], mv[:, 1:2], AF.Sqrt, [sT, eps_T], [sT], bias=eps_t[:], scale=1.0)
        K.op(dve, lambda: V.reciprocal(rs[:], rs[:]), reads=[sT], writes=[sT])
        K.op(dve, lambda: V.scalar_tensor_tensor(nm[:], mv[:, 0:1], -1.0, rs[:], ALU.mult, ALU.mult), reads=[sT], writes=[sT])
        return rs, nm

    def mk_stats(st):
        return (K.sb(st, [128, 4, 6], F32), K.sb(st, [128, 2], F32), K.sb(st, [128, 1], F32), K.sb(st, [128, 1], F32), T())

    class LNMod:
        def __init__(self, st, l, part, nset=1):
            self.l = l
            self.part = part
            self.sets = [([K.sb(st, [128, D], BF16) for _ in range(4)], [T() for _ in range(4)]) for _ in range(nset)]
            self.xn, self.xn_T = self.sets[0]
            self.stt = [mk_stats(st) for _ in range(2)]
            self.pT = [K.ps(st, [128, 2, 512], BF16) for _ in range(2)]
            self.pT_T = [T() for _ in range(2)]
            self.cnt = 0
            self.ev = 0

        def run(self, tiles, row, hT, hT_T, col0):
            self.prep(tiles)
            self.emitT(len(tiles), row, hT, hT_T, col0)

        def prep(self, tiles):
            for i, (x_ap, x_T) in enumerate(tiles):
                self.prep_tile(i, x_ap, x_T, self.xn, self.xn_T)

        def prep_tile(self, i, x_ap, x_T, xn, xn_T):
            rs, nm = ln_stats(self.stt[self.cnt % 2], x_ap, x_T)
            sT = self.stt[self.cnt % 2][4]
            self.cnt += 1
            K.actf(xn[i][:], x_ap, AF.Identity, [x_T, sT], [xn_T[i]], scale=rs[:], bias=nm[:])

        def emitT(self, n, row, hT, hT_T, col0, xset=None, hook=None):
            l = self.l
            if xset is not None:
                self.xn, self.xn_T = xset
            for kp in range(8):
                pt = self.pT[kp % 2]; pt_T = self.pT_T[kp % 2]
                for k2 in range(2):
                    kc = kp * 2 + k2
                    for i in range(n):
                        K.tr(pt_T, pt[:, k2, i * 128:(i + 1) * 128], self.xn[i][:, kc * 128:(kc + 1) * 128], ident_b[:],
                             [self.xn_T[i], ident_b_T])
                for k2 in range(2):
                    kc = kp * 2 + k2
                    sc = modT[l][:, (self.part + 1) * 16 + kc, row:row + 1]
                    sh = modT[l][:, self.part * 16 + kc, row:row + 1]
                    o = hT[:, kc, col0:col0 + n * 128]
                    if kp % 2 == 0:
                        K.op(dve, lambda o=o, pt=pt, k2=k2, sc=sc, sh=sh: V.tensor_scalar(o, pt[:, k2, 0:n * 128], sc, sh, ALU.mult, ALU.add),
                             reads=[pt_T, modT_T[l]], writes=[hT_T])
                    else:
                        K.actf(o, pt[:, k2, 0:n * 128], AF.Identity, [pt_T, modT_T[l]], [hT_T], scale=sc, bias=sh)
                    self.ev += 1
                if hook is not None:
                    hook(kp)

    def src_rows(l, u, t0, n):
        if l == 0:
            if t0 < S:
                return x_in[u, t0:t0 + n, :], DIN
            return ctx_in[u, t0 - S:t0 - S + n, :], DIN
        g = [i for i, (a, b) in enumerate(GROUPS) if a <= t0 < a + b][0]
        return xs[u, t0:t0 + n, :], xs_T[u][g]

    def phase_in_proj(l, u):
        with contextlib.ExitStack() as st:
            hT = K.sb(st, [128, KC, NT], BF16, "hT"); hT_T = [T() for _ in GROUPS]
            with contextlib.ExitStack() as s2:
                lnm = LNMod(s2, l, 0, nset=2)
                xt = [K.sb(s2, [128, D], F32) for _ in range(8)]
                xt_T = [T() for _ in range(8)]

                def loadx(gi):
                    t0, ng = GROUPS[gi]
                    for i in range(ng // 128):
                        ap, dT = src_rows(l, u, t0 + i * 128, 128)
                        k = (gi % 2) * 4 + i
                        K.dma(sp, xt[k][:], ap, [dT], [xt_T[k]], load_into=xt_T[k])

                def tiles_of(gi):
                    return [(xt[(gi % 2) * 4 + i][:], xt_T[(gi % 2) * 4 + i]) for i in range(GROUPS[gi][1] // 128)]

                loadx(0)
                loadx(1)
                for i, (ap_, T_) in enumerate(tiles_of(0)):
                    lnm.prep_tile(i, ap_, T_, *lnm.sets[0])
                for gi, (t0, ng) in enumerate(GROUPS):
                    nt = tiles_of(gi + 1) if gi + 1 < len(GROUPS) else []
                    nxt = lnm.sets[(gi + 1) % 2]
                    if gi + 2 < len(GROUPS):
                        loadx(gi + 2)

                    def hook(kp, nt=nt, nxt=nxt):
                        if kp % 2 == 1 and kp // 2 < len(nt):
                            i = kp // 2
                            lnm.prep_tile(i, nt[i][0], nt[i][1], nxt[0], nxt[1])

                    lnm.emitT(ng // 128, row_of(u, t0), hT, hT_T[gi], t0, xset=lnm.sets[gi % 2], hook=hook)
                K.barrier()
            if stop_after == "A":
                if debug:
                    dbg = nc.dram_tensor("dbg_hT", [128, KC, NT], BF16, kind="ExternalOutput").ap()
                    dT = T()
                    K.dma(sp, dbg[:, :, :], hT[:], hT_T, [dT], store_from=hT_T[0])
                    K.barrier()
                return
            wv = w_in[l].rearrange("(kc p) n -> p kc n", p=128)

            with contextlib.ExitStack() as s2:
                NW = 2
                wt = [K.sb(s2, [128, KC, 512], BF16) for _ in range(NW)]
                wt_T = [T() for _ in range(NW)]
                cosT = K.sb(s2, [128, NTILE, 128], F32); sinT = K.sb(s2, [128, NTILE, 128], F32); tab_T = T()
                gq = K.sb(s2, [128, 128], F32); gk = K.sb(s2, [128, 128], F32); gain_T = T()
                K.dma(sp, cosT[:], rope_cos.rearrange("(t p) d -> p t d", p=128), [DIN], [tab_T], load_into=tab_T)
                K.dma(sp, sinT[:], rope_sin.rearrange("(t p) d -> p t d", p=128), [DIN], [tab_T], load_into=tab_T)
                K.dma(sp, gq[:], q_gain[l].partition_broadcast(128), [DIN], [gain_T], load_into=gain_T)
                K.dma(sp, gk[:], k_gain[l].partition_broadcast(128), [DIN], [gain_T], load_into=gain_T)
                pz = [K.ps(s2, [128, 512], F32) for _ in range(3)]; pz_T = [T() for _ in range(3)]
                ptr = [K.ps(s2, [128, 8, 128], BF16) for _ in range(2)]; ptr_T = [T() for _ in range(2)]
                NZ = 3
                zs = [K.sb(s2, [128, 512], F32) for _ in range(NZ)]; zs_T = [T() for _ in range(NZ)]
                sq = [K.sb(s2, [128, 512], F32) for _ in range(2)]; sq_T = [T() for _ in range(2)]
                ss = [K.sb(s2, [128, 4], F32) for _ in range(NZ)]
                t1 = [K.sb(s2, [128, 512], F32) for _ in range(2)]; t1_T = [T() for _ in range(2)]
                t2 = [K.sb(s2, [128, 512], F32) for _ in range(2)]; t2_T = [T() for _ in range(2)]
                qr = [K.sb(s2, [128, 512], BF16) for _ in range(NZ)]; qr_T = [T() for _ in range(NZ)]
                qst = K.sb(s2, [128, 4, NT], BF16); qst_T = T()
                vst = [K.sb(s2, [128, 512], BF16) for _ in range(3)]; vst_T = [T() for _ in range(3)]
                tmblocks = [("q", QB0, 0), ("q", QB0 + 512, 1), ("kv", KB0, 0), ("v", VA0, 0), ("v", VA0 + 512, 1)]

                def loadw(i):
                    c0 = tmblocks[i][1]
                    K.dma(pool, wt[i % NW][:], wv[:, :, c0:c0 + 512], [DIN], [wt_T[i % NW]], load_into=wt_T[i % NW])

                loadw(0)
                cnt = 0
                pend = []
                trc = [0]

                def do_tr(item):
                    q_, q_T, nh, t = item
                    pt = ptr[trc[0] % 2]; pt_T = ptr_T[trc[0] % 2]
                    trc[0] += 1
                    for h in range(nh):
                        K.tr(pt_T, pt[:, h, :], q_[:, h * 128:(h + 1) * 128], ident_b[:], [q_T, ident_b_T])
                    K.actf(qst[:, 0:nh, t * 128:(t + 1) * 128], pt[:, 0:nh, :], AF.Copy, [pt_T], [qst_T])

                for bi, (kind, c0, sub) in enumerate(tmblocks):
                    if bi + 1 < len(tmblocks):
                        loadw(bi + 1)
                    w = wt[bi % NW]; w_T = wt_T[bi % NW]
                    for t in range(NTILE if (kind != "q" or l < nlayers - 1) else 16):
                        gi = min(t // 4, 4)
                        p = pz[cnt % 3]; p_T = pz_T[cnt % 3]
                        for kc in range(KC):
                            K.mm(p_T, p[:], hT[:, kc, t * 128:(t + 1) * 128], w[:, kc, :], [hT_T[gi], w_T],
                                 start=(kc == 0), stop=(kc == KC - 1))
                        if kind == "v":
                            v_ = vst[cnt % 3]; v_T = vst_T[cnt % 3]
                            K.actf(v_[:], p[:], AF.Copy, [p_T], [v_T])
                            K.dma(sp, va[t * 128:(t + 1) * 128, sub * 512:(sub + 1) * 512], v_[:], [v_T], [va_T[t]], store_from=v_T)
                        else:
                            nh = 4 if kind == "q" else 2
                            W_ = nh * 128
                            gain = gq if kind == "q" else gk
                            z = zs[cnt % NZ]; z_T = zs_T[cnt % NZ]
                            s_ = ss[cnt % NZ]
                            q_ = qr[cnt % NZ]; q_T = qr_T[cnt % NZ]
                            sq_ = sq[cnt % 2]; sq__T = sq_T[cnt % 2]
                            t1_ = t1[cnt % 2]; t1__T = t1_T[cnt % 2]
                            t2_ = t2[cnt % 2]; t2__T = t2_T[cnt % 2]
                            K.actf(z[:, 0:W_], p[:, 0:W_], AF.Copy, [p_T], [z_T])
                            if kind == "kv":
                                v_ = vst[cnt % 3]; v_T = vst_T[cnt % 3]
                                K.actf(v_[:, 0:256], p[:, 256:512], AF.Copy, [p_T], [v_T])
                                K.dma(sp, vb[t * 128:(t + 1) * 128, :], v_[:, 0:256], [v_T], [vb_T[t]], store_from=v_T)
                            z3 = z[:, 0:W_].rearrange("p (h d) -> p h d", d=128)
                            K.op(pool, lambda: nc.gpsimd.tensor_tensor(sq_[:, 0:W_], z[:, 0:W_], z[:, 0:W_], ALU.mult), reads=[z_T], writes=[sq__T])
                            K.op(dve, lambda: V.tensor_reduce(s_[:, 0:nh], sq_[:, 0:W_].rearrange("p (h d) -> p h d", d=128), AX.X, ALU.add),
                                 reads=[sq__T], writes=[z_T])
                            K.actf(s_[:, 0:nh], s_[:, 0:nh], AF.Sqrt, [z_T, eps_T], [z_T], bias=eps_t[:], scale=1.0 / 128)
                            K.op(dve, lambda: V.reciprocal(s_[:, 0:nh], s_[:, 0:nh]), reads=[z_T], writes=[z_T])
                            K.op(dve, lambda: V.tensor_tensor(z3, z3, s_[:, 0:nh].unsqueeze(2).to_broadcast([128, nh, 128]), ALU.mult), reads=[z_T], writes=[z_T])
                            K.op(dve, lambda: V.tensor_tensor(z3, z3, gain[:].unsqueeze(1).to_broadcast([128, nh, 128]), ALU.mult), reads=[z_T, gain_T], writes=[z_T])
                            K.op(dve, lambda: V.tensor_tensor(t1_[:, 0:W_].rearrange("p (h d) -> p h d", d=128), z3,
                                                              cosT[:, t, :].unsqueeze(1).to_broadcast([128, nh, 128]), ALU.mult),
                                 reads=[z_T, tab_T], writes=[t1__T])
                            zv = z[:, 0:W_].rearrange("p (h a b c) -> p h a b c", a=2, b=2, c=32)
                            sv = sinT[:, t, :].rearrange("p (a b c) -> p a b c", a=2, b=2)
                            tv = t2_[:, 0:W_].rearrange("p (h a b c) -> p h a b c", a=2, b=2, c=32)
                            K.op(dve, lambda: V.tensor_tensor(tv[:, :, :, 0, :], zv[:, :, :, 1, :], sv[:, :, 0, :].unsqueeze(1).to_broadcast([128, nh, 2, 32]), ALU.mult),
                                 reads=[z_T, tab_T], writes=[t2__T])
                            K.op(dve, lambda: V.tensor_tensor(tv[:, :, :, 1, :], zv[:, :, :, 0, :], sv[:, :, 1, :].unsqueeze(1).to_broadcast([128, nh, 2, 32]), ALU.mult),
                                 reads=[z_T, tab_T], writes=[t2__T])
                            K.op(pool, lambda: nc.gpsimd.tensor_tensor(q_[:, 0:W_], t1_[:, 0:W_], t2_[:, 0:W_], ALU.add), reads=[t1__T, t2__T], writes=[q_T])
                            pend.append((q_, q_T, nh, t))
                            if len(pend) > 2:
                                do_tr(pend.pop(0))
                        cnt += 1
                    while pend:
                        do_tr(pend.pop(0))
                    if kind == "q":
                        K.dma(sp, qbT[sub * 4:(sub + 1) * 4].rearrange("h p t -> p h t"), qst[:], [qst_T], [qbT_T[sub]], store_from=qst_T)
                    elif kind == "kv":
                        K.dma(sp, kbT.rearrange("h p t -> p h t"), qst[:, 0:2, :], [qst_T], [kbT_T], store_from=qst_T)
                K.barrier()
            if stop_after == "B1":
                return

            with contextlib.ExitStack() as s2:
                NW = 4
                wt = [K.sb(s2, [128, KC, 256], BF16) for _ in range(NW)]
                wt_T = [T() for _ in range(NW)]
                pz = [K.ps(s2, [128, 512], F32) for _ in range(6)]; pz_T = [T() for _ in range(6)]
                stg = [K.sb(s2, [128, NT], BF16) for _ in range(3)]; stg_T = [T() for _ in range(3)]
                ccs = K.sb(s2, [128, NT], F32); ccs_T = T()
                vv = K.sb(s2, [128, NT], F32); vv_T = T()
                acc = K.sb(s2, [128, NT], F32); acc_T = T()
                cwr = K.sb(s2, [24, 128], F32); cwr_T = T()
                cw = K.sb(s2, [128, 24], F32); cw_T = T()
                K.dma(sp, cwr[:], conv_w[l].rearrange("k (c p) -> (k c) p", p=128), [DIN], [cwr_T], load_into=cwr_T)
                K.tr(pz_T[0], pz[0][:, 0:24], cwr[:], ident_f[0:24, 0:24], [cwr_T, ident_f_T])
                K.op(dve, lambda: V.tensor_copy(cw[:], pz[0][:, 0:24]), reads=[pz_T[0]], writes=[cw_T])
                wl = []
                for h2 in range(4):
                    wl.append(("qa", QA0 + h2 * 256, h2))
                for h2 in range(4):
                    wl.append(("ka", KA0 + h2 * 256, h2))
                for c2 in range(4):
                    wl.append(("cc", CC0 + c2 * 256, c2)); wl.append(("cx", CX0 + c2 * 256, c2)); wl.append(("cb", CB0 + c2 * 256, c2))
                for g2 in range(24):
                    wl.append(("g", G0 + g2 * 256, g2))

                def loadw(i):
                    c0 = wl[i][1]
                    K.dma(pool, wt[i % NW][:], wv[:, :, c0:c0 + 256], [DIN], [wt_T[i % NW]], load_into=wt_T[i % NW])

                for i in range(NW - 1):
                    loadw(i)
                mg = None
                if mod_pending and mod_pending[0] == l + 1:
                    mg = mod_gen(mod_pending.pop(0), s2, 4)
                pc = [0]
                sc_ = [0]

                def proj_chunk(w, w_T, c, gi):
                    t0, ng = GROUPS[gi]
                    p = pz[pc[0] % 6]; p_T = pz_T[pc[0] % 6]
                    pc[0] += 1
                    for kc in range(KC):
                        K.mm(p_T, p[:, 0:ng], w[:, kc, c * 128:(c + 1) * 128], hT[:, kc, t0:t0 + ng], [w_T, hT_T[gi]],
                             start=(kc == 0), stop=(kc == KC - 1))
                    return p, p_T

                ev = [0]
                for wi, (kind, c0, idx) in enumerate(wl):
                    if wi + NW - 1 < len(wl):
                        loadw(wi + NW - 1)
                    if mg is not None and wi >= 2:
                        next(mg, None)
                        if kind == "g":
                            next(mg, None)
                    w = wt[wi % NW]; w_T = wt_T[wi % NW]
                    if kind in ("qa", "ka", "g"):
                        for c in range(2):
                            sg = stg[sc_[0] % 3]; sg_T = stg_T[sc_[0] % 3]
                            sc_[0] += 1
                            for gi, (t0, ng) in enumerate(GROUPS if kind == "ka" else QG(l)):
                                p, p_T = proj_chunk(w, w_T, c, gi)
                                if kind == "g":
                                    K.actf(sg[:, t0:t0 + ng], p[:, 0:ng], AF.Sigmoid, [p_T], [sg_T])
                                elif ev[0] % 2 == 0:
                                    K.actf(sg[:, t0:t0 + ng], p[:, 0:ng], AF.Copy, [p_T], [sg_T])
                                else:
                                    K.op(dve, lambda sg=sg, p=p, t0=t0, ng=ng: V.tensor_copy(sg[:, t0:t0 + ng], p[:, 0:ng]), reads=[p_T], writes=[sg_T])
                                ev[0] += 1
                            n = idx * 2 + c
                            dst, dT = {"qa": (qaT, qaT_T), "ka": (kaT, kaT_T), "g": (gT, gT_T)}[kind]
                            K.dma(sp, dst[n], sg[:], [sg_T], [dT[n]], store_from=sg_T)
                    elif kind == "cc":
                        wcx = wt[(wi + 1) % NW]; wcx_T = wt_T[(wi + 1) % NW]
                        wcb = wt[(wi + 2) % NW]; wcb_T = wt_T[(wi + 2) % NW]
                        for c in range(2):
                            ci = idx * 2 + c
                            for gi, (t0, ng) in enumerate(QG(l)):
                                p, p_T = proj_chunk(w, w_T, c, gi)
                                K.actf(ccs[:, t0:t0 + ng], p[:, 0:ng], AF.Copy, [p_T], [ccs_T])
                            for gi, (t0, ng) in enumerate(QG(l)):
                                p, p_T = proj_chunk(wcx, wcx_T, c, gi)
                                K.op(dve, lambda p=p, t0=t0, ng=ng: V.tensor_tensor(vv[:, t0:t0 + ng], p[:, 0:ng], ccs[:, t0:t0 + ng], ALU.mult),
                                     reads=[p_T, ccs_T], writes=[vv_T])
                            w0 = cw[:, 0 * 8 + ci:0 * 8 + ci + 1]; w1 = cw[:, 8 + ci:8 + ci + 1]; w2 = cw[:, 16 + ci:16 + ci + 1]
                            nv = NT if l < nlayers - 1 else S
                            K.actf(acc[:, 0:nv], vv[:, 0:nv], AF.Identity, [vv_T, cw_T], [acc_T], scale=w1)
                            for (a, b) in (((0, S), (S, NT)) if l < nlayers - 1 else ((0, S),)):
                                K.op(dve, lambda a=a, b=b, w0=w0: V.scalar_tensor_tensor(acc[:, a + 1:b], vv[:, a:b - 1], w0, acc[:, a + 1:b], ALU.mult, ALU.add),
                                     reads=[vv_T, acc_T, cw_T], writes=[acc_T])
                                K.op(dve, lambda a=a, b=b, w2=w2: V.scalar_tensor_tensor(acc[:, a:b - 1], vv[:, a + 1:b], w2, acc[:, a:b - 1], ALU.mult, ALU.add),
                                     reads=[vv_T, acc_T, cw_T], writes=[acc_T])
                            sg = stg[sc_[0] % 3]; sg_T = stg_T[sc_[0] % 3]
                            sc_[0] += 1
                            for gi, (t0, ng) in enumerate(QG(l)):
                                p, p_T = proj_chunk(wcb, wcb_T, c, gi)
                                K.op(dve, lambda sg=sg, p=p, t0=t0, ng=ng: V.tensor_tensor(sg[:, t0:t0 + ng], p[:, 0:ng], acc[:, t0:t0 + ng], ALU.mult),
                                     reads=[p_T, acc_T], writes=[sg_T])
                            K.dma(sp, ycT[ci], sg[:], [sg_T], [ycT_T[ci]], store_from=sg_T)
                if mg is not None:
                    for _ in mg:
                        pass
                K.barrier()

    def phase_gqa(l, u):
        with contextlib.ExitStack() as st:
            kb_s = K.sb(st, [128, 2, NT], BF16); kb_sT = T()
            vb_s = K.sb(st, [128, NTILE, 256], BF16); vb_sT = T()
            K.dma(sp, kb_s[:], kbT.rearrange("h p t -> p h t"), [kbT_T], [kb_sT], load_into=kb_sT)
            K.dma(sp, vb_s[:], vb.rearrange("(t p) c -> p t c", p=128), vb_T, [vb_sT], load_into=vb_sT)
            qT = [K.sb(st, [128, NT], BF16) for _ in range(2)]; qT_T = [T() for _ in range(2)]
            ost = [K.sb(st, [128, NT], BF16) for _ in range(2)]; ost_T = [T() for _ in range(2)]
            pS = [K.ps(st, [128, 512], F32) for _ in range(3)]; pS_T = [T() for _ in range(3)]
            pO = [K.ps(st, [128, 512], F32) for _ in range(2)]; pO_T = [T() for _ in range(2)]
            pD = [K.ps(st, [128, 512], F32) for _ in range(2)]; pD_T = [T() for _ in range(2)]
            P = [K.sb(st, [128, 512], BF16) for _ in range(3)]; P_T = [T() for _ in range(3)]
            rd = [K.sb(st, [128, 512], F32) for _ in range(2)]; rd_T = [T() for _ in range(2)]
            pF = K.ps(st, [128, 512], F32); pF_T = T()

            def loadq(h):
                K.dma(sp, qT[h % 2][:], qbT[h], [qbT_T[h // 4]], [qT_T[h % 2]], load_into=qT_T[h % 2])

            loadq(0)
            sc = 0
            oc = 0
            for h in range(8):
                if h + 1 < 8:
                    loadq(h + 1)
                g = h // 4
                q = qT[h % 2]; q_T = qT_T[h % 2]
                o_ = ost[h % 2]; o_T = ost_T[h % 2]
                for gi, (q0, nq) in enumerate(QG(l)):
                    keys = list(range(NTILE)) if q0 < S else [16, 17]
                    po = pO[oc % 2]; po_T = pO_T[oc % 2]
                    pd = pD[oc % 2]; pd_T = pD_T[oc % 2]
                    r_ = rd[oc % 2]; r_T = rd_T[oc % 2]
                    oc += 1

                    def smm(c, slot):
                        K.mm(pS_T[slot], pS[slot][:, 0:nq], kb_s[:, g, c * 128:(c + 1) * 128], q[:, q0:q0 + nq], [kb_sT, q_T])

                    smm(keys[0], sc % 3)
                    for ki, c in enumerate(keys):
                        slot = sc % 3
                        if ki + 1 < len(keys):
                            smm(keys[ki + 1], (sc + 1) % 3)
                        K.actf(P[slot][:, 0:nq], pS[slot][:, 0:nq], AF.Exp, [pS_T[slot]], [P_T[slot]], scale=SCALE)
                        K.mm(po_T, po[:, 0:nq], vb_s[:, c, g * 128:(g + 1) * 128], P[slot][:, 0:nq], [vb_sT, P_T[slot]],
                             start=(ki == 0), stop=(ki == len(keys) - 1))
                        K.mm(pd_T, pd[:, 0:nq], ones_b[:], P[slot][:, 0:nq], [ones_T, P_T[slot]],
                             start=(ki == 0), stop=(ki == len(keys) - 1))
                        for _ in range(NFILL):
                            K.mm(pF_T, pF[:, 0:128], ones_b[:], q[:, q0:q0 + 128], [ones_T, q_T])
                        sc += 1
                    K.op(dve, lambda: V.reciprocal(r_[:, 0:nq], pd[:, 0:nq]), reads=[pd_T], writes=[r_T])
                    K.op(dve, lambda: V.tensor_tensor(o_[:, q0:q0 + nq], po[:, 0:nq], r_[:, 0:nq], ALU.mult), reads=[po_T, r_T], writes=[o_T])
                K.dma(sp, ogT[h], o_[:], [o_T], [ogT_T[h]], store_from=o_T)
            K.barrier()

    def na_keys(j):
        if j == 0:
            return [0, 1, 2, 3], 5
        if j == 1:
            return [0, 1, 2, 3], 9
        if j == 14:
            return [12, 13, 14, 15], 13
        if j == 15:
            return [12, 13, 14, 15], 17
        return [j - 2, j - 1, j, j + 1, j + 2], 0

    def phase_na(l, u):
        with contextlib.ExitStack() as st:
            va_s = K.sb(st, [128, NTILE, 1024], BF16); va_sT = T()
            K.dma(sp, va_s[:], va.rearrange("(t p) c -> p t c", p=128), va_T, [va_sT], load_into=va_sT)
            kT = [K.sb(st, [128, NT], BF16) for _ in range(2)]; kT_T = [T() for _ in range(2)]
            qT = [K.sb(st, [128, NT], BF16) for _ in range(2)]; qT_T = [T() for _ in range(2)]
            bt = [K.sb(st, [128, 21 * 128], F32) for _ in range(2)]; bt_T = [T() for _ in range(2)]
            ost = [K.sb(st, [128, NT], BF16) for _ in range(2)]; ost_T = [T() for _ in range(2)]
            pS = [K.ps(st, [128, 1024], F32) for _ in range(2)]; pS_T = [T() for _ in range(2)]
            pO = [K.ps(st, [128, 512], F32) for _ in range(2)]; pO_T = [T() for _ in range(2)]
            pD = [K.ps(st, [128, 512], F32) for _ in range(2)]; pD_T = [T() for _ in range(2)]
            tmp = [K.sb(st, [128, 640], F32) for _ in range(2)]; tmp_T = [T() for _ in range(2)]
            P = [K.sb(st, [128, 896], BF16) for _ in range(2)]; P_T = [T() for _ in range(2)]
            rd = [K.sb(st, [128, 512], F32) for _ in range(2)]; rd_T = [T() for _ in range(2)]

            def loadh(h):
                K.dma(sp, kT[h % 2][:], kaT[h], [kaT_T[h]], [kT_T[h % 2]], load_into=kT_T[h % 2])
                K.dma(sp, qT[h % 2][:], qaT[h], [qaT_T[h]], [qT_T[h % 2]], load_into=qT_T[h % 2])
                K.dma(sp, bt[h % 2][:], btab[l, h], [DIN], [bt_T[h % 2]], load_into=bt_T[h % 2])

            loadh(0)
            work = []
            for h in range(8):
                items = [(j * 128, 128) + na_keys(j) for j in range(16)]
                if l < nlayers - 1:
                    items.append((S, 256, [], 0))
                for it, (q0, nq, loc, tb0) in enumerate(items):
                    work.append((h, it, q0, nq, loc, tb0))
            last_it = 16 if l < nlayers - 1 else 15

            def do_S(k):
                h, it, q0, nq, loc, tb0 = work[k]
                ps_ = pS[k % 2]; ps_T = pS_T[k % 2]
                chunks = loc + [16, 17]
                for i, c in enumerate(chunks):
                    K.mm(ps_T, ps_[:, i * nq:(i + 1) * nq], kT[h % 2][:, c * 128:(c + 1) * 128], qT[h % 2][:, q0:q0 + nq], [kT_T[h % 2], qT_T[h % 2]])

            oc = 0
            loadh(1)
            do_S(0)
            for k, (h, it, q0, nq, loc, tb0) in enumerate(work):
                if k + 1 < len(work):
                    do_S(k + 1)
                b_ = bt[h % 2]; b_T = bt_T[h % 2]
                o_ = ost[h % 2]; o_T = ost_T[h % 2]
                slot4 = it % 4 if it < 16 else 0
                if (it < 16 and slot4 == 0) or it == 16:
                    po = pO[oc % 2]; po_T = pO_T[oc % 2]
                    pd = pD[oc % 2]; pd_T = pD_T[oc % 2]
                    r_ = rd[oc % 2]; r_T = rd_T[oc % 2]
                    oc += 1
                ocol = slot4 * 128
                ps_ = pS[k % 2]; ps_T = pS_T[k % 2]
                tm = tmp[k % 2]; tm_T = tmp_T[k % 2]
                p_ = P[k % 2]; p_T = P_T[k % 2]
                chunks = loc + [16, 17]
                nl = len(loc)
                if nl:
                    K.op(dve, lambda: V.scalar_tensor_tensor(tm[:, 0:nl * 128], ps_[:, 0:nl * 128], SCALE, b_[:, tb0 * 128:(tb0 + nl) * 128], ALU.mult, ALU.add),
                         reads=[ps_T, b_T], writes=[tm_T])
                    K.actf(p_[:, 0:nl * 128], tm[:, 0:nl * 128], AF.Exp, [tm_T], [p_T])
                K.actf(p_[:, nl * nq:(nl + 2) * nq], ps_[:, nl * nq:(nl + 2) * nq], AF.Exp, [ps_T], [p_T], scale=SCALE)
                for i, c in enumerate(chunks):
                    K.mm(po_T, po[:, ocol:ocol + nq], va_s[:, c, h * 128:(h + 1) * 128], p_[:, i * nq:(i + 1) * nq], [va_sT, p_T],
                         start=(i == 0), stop=(i == len(chunks) - 1))
                for i, c in enumerate(chunks):
                    K.mm(pd_T, pd[:, ocol:ocol + nq], ones_b[:], p_[:, i * nq:(i + 1) * nq], [ones_T, p_T],
                         start=(i == 0), stop=(i == len(chunks) - 1))
                if it < 16 and slot4 == 3:
                    qq = q0 - 384
                    K.op(dve, lambda: V.reciprocal(r_[:], pd[:]), reads=[pd_T], writes=[r_T])
                    K.op(dve, lambda: V.tensor_tensor(o_[:, qq:qq + 512], po[:], r_[:], ALU.mult), reads=[po_T, r_T], writes=[o_T])
                elif it == 16:
                    K.op(dve, lambda: V.reciprocal(r_[:, 0:256], pd[:, 0:256]), reads=[pd_T], writes=[r_T])
                    K.op(dve, lambda: V.tensor_tensor(o_[:, S:NT], po[:, 0:256], r_[:, 0:256], ALU.mult), reads=[po_T, r_T], writes=[o_T])
                if it == last_it:
                    K.dma(sp, onT[h], o_[:], [o_T], [onT_T[h]], store_from=o_T)
                    if h + 2 < 8:
                        loadh(h + 2)
            K.barrier()

    def phase_merge(l, u):
        with contextlib.ExitStack() as st:
            srcs = []
            for (dr, dT) in ((onT, onT_T), (ogT, ogT_T), (ycT, ycT_T)):
                s_ = K.sb(st, [128, 8, NT], BF16)
                srcs.append((s_, [T() for _ in GROUPS], dr.rearrange("h p t -> p h t"), dT))
            def load_src(gi):
                t0, ng = GROUPS[gi]
                for (s_, s_Ts, dv, dT) in srcs:
                    K.dma(sp, s_[:, :, t0:t0 + ng], dv[:, :, t0:t0 + ng], dT, [s_Ts[gi]], load_into=s_Ts[gi])

            load_src(0)
            NW = 2
            wb = [K.sb(st, [128, 3, 8, 256], BF16) for _ in range(NW)]; wb_T = [T() for _ in range(NW)]
            gs = [K.sb(st, [128, 3, NT], BF16) for _ in range(2)]; gs_T = [T() for _ in range(2)]
            pz = [K.ps(st, [128, 512], F32) for _ in range(6)]; pz_T = [T() for _ in range(6)]
            ta = [K.sb(st, [128, 512], F32) for _ in range(2)]; ta_T = [T() for _ in range(2)]
            tb = [K.sb(st, [128, 512], F32) for _ in range(2)]; tb_T = [T() for _ in range(2)]
            stg = [K.sb(st, [128, NT], BF16) for _ in range(2)]; stg_T = [T() for _ in range(2)]
            wbv = w_branch[l].rearrange("i (kc p) n -> p i kc n", p=128)

            def loadw(i):
                K.dma(pool, wb[i % NW][:], wbv[:, :, :, i * 256:(i + 1) * 256], [DIN], [wb_T[i % NW]], load_into=wb_T[i % NW])

            def loadg(n):
                K.dma(sp, gs[n % 2][:], gT.rearrange("(i n) p t -> n p i t", i=3)[n], [gT_T[n], gT_T[16 + n], gT_T[32 + n]],
                      [gs_T[n % 2]], load_into=gs_T[n % 2])

            loadw(0)
            loadg(0)
            for gi in range(1, len(QG(l))):
                load_src(gi)
            pc = 0
            tc = 0
            for n in range(16):
                if n % 2 == 0 and n // 2 + 1 < 8:
                    loadw(n // 2 + 1)
                if n + 1 < 16:
                    loadg(n + 1)
                w = wb[(n // 2) % NW]; w_T = wb_T[(n // 2) % NW]
                c = n % 2
                g_ = gs[n % 2]; g_T = gs_T[n % 2]
                sg = stg[n % 2]; sg_T = stg_T[n % 2]
                for gi, (t0, ng) in enumerate(QG(l)):
                    pp = []
                    for i in range(3):
                        p = pz[pc % 6]; p_T = pz_T[pc % 6]
                        pc += 1
                        for kc in range(8):
                            K.mm(p_T, p[:, 0:ng], w[:, i, kc, c * 128:(c + 1) * 128], srcs[i][0][:, kc, t0:t0 + ng], [w_T, srcs[i][1][gi]],
                                 start=(kc == 0), stop=(kc == 7))
                        pp.append((p, p_T))
                    a_ = ta[tc % 2]; a_T = ta_T[tc % 2]
                    b_ = tb[tc % 2]; b_T = tb_T[tc % 2]
                    tc += 1
                    K.op(dve, lambda: V.tensor_tensor(a_[:, 0:ng], pp[0][0][:, 0:ng], g_[:, 0, t0:t0 + ng], ALU.mult), reads=[pp[0][1], g_T], writes=[a_T])
                    K.op(dve, lambda: V.tensor_tensor(b_[:, 0:ng], pp[1][0][:, 0:ng], g_[:, 1, t0:t0 + ng], ALU.mult), reads=[pp[1][1], g_T], writes=[b_T])
                    K.op(dve, lambda: V.tensor_tensor(a_[:, 0:ng], a_[:, 0:ng], b_[:, 0:ng], ALU.add), reads=[a_T, b_T], writes=[a_T])
                    K.op(dve, lambda: V.tensor_tensor(b_[:, 0:ng], pp[2][0][:, 0:ng], g_[:, 2, t0:t0 + ng], ALU.mult), reads=[pp[2][1], g_T], writes=[b_T])
                    K.op(dve, lambda: V.tensor_tensor(sg[:, t0:t0 + ng], a_[:, 0:ng], b_[:, 0:ng], ALU.add), reads=[a_T, b_T], writes=[sg_T])
                K.dma(sp, mT[n], sg[:], [sg_T], [mT_T[n]], store_from=sg_T)
            K.barrier()

    class PostLN:
        def __init__(self, st, l, which, npz=3, ntacc=2):
            self.npz = npz
            self.l = l
            self.which = which
            self.gpart = 2 if which == 0 else 5
            self.tacc = [K.sb(st, [128, 4, D], F32, "tacc") for _ in range(ntacc)]
            self.tacc_T = [[T() for _ in range(4)] for _ in range(ntacc)]
            self.lng = K.sb(st, [128, D], F32); self.lnb = K.sb(st, [128, D], F32); self.ln_T = T()
            K.dma(sp, self.lng[:], ln_g[l, which].partition_broadcast(128), [DIN], [self.ln_T], load_into=self.ln_T)
            K.dma(sp, self.lnb[:], ln_b[l, which].partition_broadcast(128), [DIN], [self.ln_T], load_into=self.ln_T)
            self.pz = [K.ps(st, [128, 512], F32) for _ in range(npz)]; self.pz_T = [T() for _ in range(npz)]
            self.pt = [K.ps(st, [128, 8, 128], BF16) for _ in range(2)]; self.pt_T = [T() for _ in range(2)]
            self.ys = [K.sb(st, [128, 512], BF16) for _ in range(2)]; self.ys_T = [T() for _ in range(2)]
            self.stt = [mk_stats(st) for _ in range(2)]
            self.pc = 0
            self.yc = 0
            self.lc = 0
            self.pending = None
            self.fin = []
            self.after_fin = []

        def when_drained(self, fn):
            if not self.fin:
                fn()
            else:
                self.after_fin.append(fn)

        def load_x(self, l, u, t0, ng, b, from_xs=False):
            for i in range(ng // 128):
                ap, dT = src_rows(1 if from_xs else l, u, t0 + i * 128, 128)
                K.dma(sp, self.tacc[b][:, i, :], ap, [dT], [self.tacc_T[b][i]], load_into=self.tacc_T[b][i])

        def chunk(self, n, ng, row, mms, first, b):
            l = self.l
            p = self.pz[self.pc % self.npz]; p_T = self.pz_T[self.pc % self.npz]
            self.pc += 1
            for i, (lh, rh, rd_) in enumerate(mms):
                K.mm(p_T, p[:, 0:ng], lh, rh, rd_, start=(i == 0), stop=(i == len(mms) - 1))
            y_ = self.ys[self.yc % 2]; y_T = self.ys_T[self.yc % 2]
            pt = self.pt[self.yc % 2]; pt_T = self.pt_T[self.yc % 2]
            self.yc += 1
            K.actf(y_[:, 0:ng], p[:, 0:ng], AF.Identity, [p_T, modT_T[l]], [y_T], scale=modT[l][:, self.gpart * 16 + n, row:row + 1])
            self.flush()
            self.pending = (n, ng, first, y_, y_T, pt, pt_T, b)
            self.step_finish()

        def flush(self):
            if self.pending is None:
                return
            n, ng, first, y_, y_T, pt, pt_T, b = self.pending
            self.pending = None
            nt = ng // 128
            for i in range(nt):
                K.tr(pt_T, pt[:, i, :], y_[:, i * 128:(i + 1) * 128], ident_b[:], [y_T, ident_b_T])
            tv = self.tacc[b][:, 0:nt, n * 128:(n + 1) * 128]
            tT = self.tacc_T[b][0:nt]
            if first:
                K.op(dve, lambda: V.scalar_tensor_tensor(tv, tv, ALPHA, pt[:, 0:nt, :], ALU.mult, ALU.add), reads=[pt_T] + tT, writes=tT)
            else:
                K.op(dve, lambda: V.tensor_tensor(tv, tv, pt[:, 0:nt, :], ALU.add), reads=[pt_T] + tT, writes=tT)

        def _finish_tile(self, i, b, store_fn):
            xa = self.tacc[b][:, i, :]; x_T = self.tacc_T[b][i]
            stt = self.stt[self.lc % 2]
            self.lc += 1
            rs, nm = ln_stats(stt, xa, x_T)
            K.actf(xa, xa, AF.Identity, [x_T, stt[4]], [x_T], scale=rs[:], bias=nm[:])
            K.op(dve, lambda: V.tensor_tensor(xa, xa, self.lng[:], ALU.mult), reads=[x_T, self.ln_T], writes=[x_T])
            K.op(dve, lambda: V.tensor_tensor(xa, xa, self.lnb[:], ALU.add), reads=[x_T, self.ln_T], writes=[x_T])
            store_fn(i, xa, x_T)

        def finish(self, ng, store_fn, b):
            self.flush()
            for i in range(ng // 128):
                self.fin.append((i, b, store_fn))

        def step_finish(self, k=1):
            for _ in range(k):
                if self.fin:
                    self._finish_tile(*self.fin.pop(0))
            if not self.fin:
                while self.after_fin:
                    self.after_fin.pop(0)()

        def drain(self):
            self.flush()
            while self.fin:
                self._finish_tile(*self.fin.pop(0))
            while self.after_fin:
                self.after_fin.pop(0)()

    def dst_rows(l, u, t0, i, which):
        r0 = t0 + i * 128
        g = [k for k, (a, b) in enumerate(GROUPS) if a <= t0 < a + b][0]
        if l == nlayers - 1 and which == 1:
            if r0 >= S:
                return None
            return y_out[u, r0:r0 + 128, :], None
        return xs[u, r0:r0 + 128, :], xs_T[u][g]

    OUT_T = T()

    def phase_outproj(l, u):
        with contextlib.ExitStack() as st:
            wo = K.sb(st, [128, KC, D], BF16, "wo"); wo_T = [T() for _ in range(8)]
            wv = w_o[l].rearrange("(kc p) n -> p kc n", p=128)
            for c2 in range(8):
                K.dma(pool, wo[:, :, c2 * 256:(c2 + 1) * 256], wv[:, :, c2 * 256:(c2 + 1) * 256], [DIN], [wo_T[c2]], load_into=wo_T[c2])
            mTs = [K.sb(st, [128, 16, 512], BF16) for _ in range(2)]; mTs_T = [T() for _ in range(2)]
            mv = mT.rearrange("n p t -> p n t")

            def loadm(gi):
                t0, ng = GROUPS[gi]
                K.dma(sp, mTs[gi % 2][:, :, 0:ng], mv[:, :, t0:t0 + ng], mT_T, [mTs_T[gi % 2]], load_into=mTs_T[gi % 2])

            loadm(0)
            pl = PostLN(st, l, 0)
            pl.load_x(l, u, GROUPS[0][0], GROUPS[0][1], 0)
            for gi, (t0, ng) in enumerate(QG(l)):
                b = gi % 2
                if gi + 1 < len(QG(l)):
                    loadm(gi + 1)
                    pl.when_drained(lambda gi=gi: pl.load_x(l, u, GROUPS[gi + 1][0], GROUPS[gi + 1][1], (gi + 1) % 2))
                m_ = mTs[gi % 2]; m_T = mTs_T[gi % 2]
                for n in range(16):
                    mms = [(wo[:, kc, n * 128:(n + 1) * 128], m_[:, kc, 0:ng], [wo_T[n // 2], m_T]) for kc in range(KC)]
                    pl.chunk(n, ng, row_of(u, t0), mms, True, b)

                def store(i, xa, x_T, t0=t0):
                    d = dst_rows(l, u, t0, i, 0)
                    K.dma(sp, d[0], xa, [x_T], [d[1]], store_from=x_T)
                pl.finish(ng, store, b)
            pl.drain()
            K.barrier()

    def phase_mlp(l, u):
        NSPLIT = 2
        HC = 64 // NSPLIT
        with contextlib.ExitStack() as st:
            pl = PostLN(st, l, 1, npz=2)
            lnm = LNMod(st, l, 3)
            h2 = K.sb(st, [128, KC, 512], BF16); h2_T = T()
            uT = K.sb(st, [128, HC, 512], BF16); uT_T = T()
            NU = 3
            wu = [K.sb(st, [128, KC, 256], BF16) for _ in range(NU)]; wu_T = [T() for _ in range(NU)]
            ND = 3
            wd = [K.sb(st, [128, HC, 128], BF16) for _ in range(ND)]; wd_T = [T() for _ in range(ND)]
            pu = [K.ps(st, [128, 512], F32) for _ in range(2)]; pu_T = [T() for _ in range(2)]
            rl = [K.sb(st, [128, 512], F32) for _ in range(1)]; rl_T = [T() for _ in range(1)]
            wuv = w_up[l].rearrange("(kc p) n -> p kc n", p=128)
            wdv = w_down[l].rearrange("(kc p) n -> p kc n", p=128)
            ngroups = len(GROUPS) if l < nlayers - 1 else len(GROUPS) - 1
            ul = [(gi, sp_, j) for gi in range(ngroups) for sp_ in range(NSPLIT) for j in range(HC // 2)]
            dl = [(gi, sp_, j) for gi in range(ngroups) for sp_ in range(NSPLIT) for j in range(16)]

            def loadu(i):
                gi, sp_, j = ul[i]
                c0 = sp_ * HC * 128 + j * 256
                K.dma(pool, wu[i % NU][:], wuv[:, :, c0:c0 + 256], [DIN], [wu_T[i % NU]], load_into=wu_T[i % NU])

            def loadd(i):
                gi, sp_, j = dl[i]
                K.dma(pool, wd[i % ND][:], wdv[:, sp_ * HC:(sp_ + 1) * HC, j * 128:(j + 1) * 128], [DIN], [wd_T[i % ND]], load_into=wd_T[i % ND])

            ui = 0
            di = 0
            for i in range(NU - 1):
                loadu(i)
            loadd(0)
            loadd(1)
            uc = 0
            groups = GROUPS[:ngroups]

            def prep(gi):
                t0, ng = groups[gi]
                b = gi % 2
                pl.load_x(l, u, t0, ng, b, from_xs=True)
                lnm.prep([(pl.tacc[b][:, i, :], pl.tacc_T[b][i]) for i in range(ng // 128)])

            def emitT(gi):
                t0, ng = groups[gi]
                lnm.emitT(ng // 128, row_of(u, t0), h2, h2_T, 0)

            prep(0)
            emitT(0)
            for gi in range(ngroups):
                t0, ng = groups[gi]
                b = gi % 2
                for sp_ in range(NSPLIT):
                    last = sp_ == NSPLIT - 1
                    if last and gi + 1 < ngroups:
                        prep(gi + 1)
                    for j in range(HC // 2):
                        if ui + NU - 1 < len(ul):
                            loadu(ui + NU - 1)
                        w = wu[ui % NU]; w_T = wu_T[ui % NU]
                        ui += 1
                        for c in range(2):
                            hc = j * 2 + c
                            p = pu[uc % 2]; p_T = pu_T[uc % 2]
                            r_ = rl[0]; r_T = rl_T[0]
                            uc += 1
                            for kc in range(KC):
                                K.mm(p_T, p[:, 0:ng], w[:, kc, c * 128:(c + 1) * 128], h2[:, kc, 0:ng], [w_T, h2_T],
                                     start=(kc == 0), stop=(kc == KC - 1))
                            K.actf(r_[:, 0:ng], p[:, 0:ng], AF.Relu, [p_T], [r_T])
                            K.op(dve, lambda: V.tensor_tensor(uT[:, hc, 0:ng], r_[:, 0:ng], r_[:, 0:ng], ALU.mult), reads=[r_T], writes=[uT_T])
                            pl.step_finish()
                    if last and gi + 1 < ngroups:
                        emitT(gi + 1)
                    for n in range(16):
                        if di + 2 < len(dl):
                            loadd(di + 2)
                        w = wd[di % ND]; w_T = wd_T[di % ND]
                        di += 1
                        mms = [(w[:, kc, :], uT[:, kc, 0:ng], [w_T, uT_T]) for kc in range(HC)]
                        pl.chunk(n, ng, row_of(u, t0), mms, sp_ == 0, b)

                def store(i, xa, x_T, t0=t0):
                    d = dst_rows(l, u, t0, i, 1)
                    if d is None:
                        return
                    if d[1] is None:
                        K.dma(sp, d[0], xa, [x_T], [OUT_T], store_from=x_T)
                    else:
                        K.dma(sp, d[0], xa, [x_T], [d[1]], store_from=x_T)
                pl.finish(ng, store, b)
            pl.drain()
            K.barrier()

    done = False
    for l in range(nlayers):
        for u in units:
            phase_in_proj(l, u)
            if stop_after in ("A", "B1", "B"):
                done = True
                break
            phase_gqa(l, u)
            phase_na(l, u)
            if stop_after == "D":
                done = True
                break
            phase_merge(l, u)
            if stop_after == "E":
                done = True
                break
            phase_outproj(l, u)
            if stop_after == "F":
                done = True
                break
            phase_mlp(l, u)
            if stop_after == "H":
                done = True
                break
        if done:
            break
    K.barrier()
    K.stack.close()
    return nc


def _rope_tables():
    t = np.arange(S)
    row = (t // 64).astype(np.float32)
    col = (t % 64).astype(np.float32)
    freqs = (np.float32(10000.0) ** (-np.arange(0, 64, 2, dtype=np.float32) / np.float32(64))).astype(np.float32)
    ar = row[:, None] * freqs
    ac = col[:, None] * freqs
    cr, sr, cc, sc = np.cos(ar), np.sin(ar), np.cos(ac), np.sin(ac)
    cos = np.ones((NT, 128), np.float32)
    sin = np.zeros((NT, 128), np.float32)
    cos[:S] = np.concatenate([cr, cr, cc, cc], axis=1)
    sin[:S] = np.concatenate([-sr, sr, -sc, sc], axis=1)
    return cos.astype(np.float32), sin.astype(np.float32)


def _bias_tables(rpb):
    rows = 32
    r = np.arange(rows)
    row_start = np.clip(r - 4, 0, rows - 8)
    col = np.arange(64)
    col_start = np.clip(col - 8, 0, 48)
    in_win = (col[None, :] >= col_start[:, None]) & (col[None, :] < col_start[:, None] + 16)
    dc_idx = np.clip(col[None, :] - col[:, None], -15, 15) + 15

    def tile(j, c):
        kr = np.repeat(np.array([2 * c, 2 * c + 1]), 64)
        kc_ = np.tile(col, 2)
        qr = np.repeat(np.array([2 * j, 2 * j + 1]), 64)
        qc = np.tile(col, 2)
        valid = (kr[:, None] >= row_start[qr][None, :]) & (kr[:, None] < row_start[qr][None, :] + 8) & in_win[qc[None, :], kc_[:, None]]
        dr = np.clip(kr[:, None] - qr[None, :] + 7, 0, 14)
        dc = dc_idx[qc[None, :], kc_[:, None]]
        return valid, dr, dc

    specs = [(5, 5 + d) for d in (-2, -1, 0, 1, 2)]
    for j, ks in ((0, [0, 1, 2, 3]), (1, [0, 1, 2, 3]), (14, [12, 13, 14, 15]), (15, [12, 13, 14, 15])):
        specs += [(j, c) for c in ks]
    out = np.empty((rpb.shape[0], 8, 128, 21 * 128), np.float32)
    for i, (j, c) in enumerate(specs):
        valid, dr, dc = tile(j, c)
        g = rpb[:, :, dr, dc]
        out[:, :, :, i * 128:(i + 1) * 128] = np.where(valid[None, None], g, np.float32(NEG))
    return out


_NC_CACHE = {}


def kernel(x, c, ctx, c_ctx, w_mod, b_mod, w_in, rpb, q_gain, k_gain, conv_w,
           w_branch, w_o, w_up, w_down, ln_g, ln_b):
    f = lambda a: np.ascontiguousarray(np.asarray(a, dtype=np.float32))
    x, c, ctx, c_ctx = f(x), f(c), f(ctx), f(c_ctx)
    shared = dict(w_mod=f(w_mod), b_mod=f(b_mod), w_in=f(w_in), btab=_bias_tables(f(rpb)), q_gain=f(q_gain), k_gain=f(k_gain),
                  conv_w=f(conv_w), w_branch=f(w_branch), w_o=f(w_o), w_up=f(w_up), w_down=f(w_down), ln_g=f(ln_g), ln_b=f(ln_b),
                  ident=np.eye(128, dtype=np.float32))
    shared["rope_cos"], shared["rope_sin"] = _rope_tables()
    if "nc" not in _NC_CACHE:
        _NC_CACHE["nc"] = build_program()
    nc = _NC_CACHE["nc"]
    in_maps = []
    for i in range(8):
        m = dict(shared)
        m["x"] = np.ascontiguousarray(x[2 * i:2 * i + 2])
        m["ctx"] = np.ascontiguousarray(ctx[2 * i:2 * i + 2])
        m["cvec"] = np.ascontiguousarray(np.stack([c[2 * i], c[2 * i + 1], c_ctx, c_ctx]))
        in_maps.append(m)
    res = run_bass_kernel_spmd(nc, in_maps, core_ids=list(range(8)))
    return np.concatenate([r["y"] for r in res.results], axis=0).astype(np.float32)
```
